# Optimizing a Trainium2 kernel written in Bass

```python
import math
import jax, jax.numpy as jnp
from jax import lax
import numpy as np

D_MODEL = 1024
BATCH = 16
SEQ = 2048
DEPTH = 2
DEC_BATCH = 32
DEC_SEQ = 8
PAST_LEN = 16384
PAGE_SIZE = 128

NORM_EPS = 1e-6
N_MOD = 9
D_FF = 2816
A_GROUPS = ((128, 1), (512, 4), (2048, 16))
A_N_GROUPS = 3
A_HEADS = 8
A_HEAD_DIM = 64
A_WIDTH = A_HEADS * A_HEAD_DIM
A_KEYS = 128
A_BLOCK = 128
REL_BUCKETS = 32
REL_MAX_EXACT = 16
REL_MAX_DISTANCE = 2048
B_WIDTH = 1024
B_HEAD_DIM = 64
B_HEADS = B_WIDTH // B_HEAD_DIM
B_GROUPS = 2
B_STATE = 128
B_CONV = 4
B_CONV_CH = B_WIDTH + 2 * B_GROUPS * B_STATE
B_CHUNK = 128
C_WIDTH = 1024
C_BLOCKS = 8
C_BLOCK_DIM = C_WIDTH // C_BLOCKS
C_CONV = 4
C_POW = 8.0
IN_SPLIT = (A_N_GROUPS * A_WIDTH, A_N_GROUPS * A_WIDTH, A_N_GROUPS * A_WIDTH,
            B_WIDTH, B_CONV_CH, B_HEADS, C_WIDTH, C_WIDTH, 3 * D_MODEL)
N_IN = 3 * A_N_GROUPS * A_WIDTH + B_WIDTH + B_CONV_CH + B_HEADS + 2 * C_WIDTH + 3 * D_MODEL

kernel_name = "hybrid_gated_dilated_ssd_lru_decoder_step"


def rmsnorm(x, g):
    x32 = x.astype(jnp.float32)
    y = x32 * lax.rsqrt(jnp.mean(x32 * x32, axis=-1, keepdims=True) + NORM_EPS)
    return (y * g.astype(jnp.float32)).astype(x.dtype)


def swiglu(h, w_in, w_out):
    u, v = jnp.split(h @ w_in, 2, axis=-1)
    return (jax.nn.silu(u) * v) @ w_out


def t5_bucket(dist):
    dist = np.asarray(dist)
    large = REL_MAX_EXACT + (np.log(np.maximum(dist, 1) / REL_MAX_EXACT)
                             / math.log(REL_MAX_DISTANCE / REL_MAX_EXACT)
                             * (REL_BUCKETS - REL_MAX_EXACT)).astype(np.int64)
    large = np.minimum(large, REL_BUCKETS - 1)
    return np.where(dist < REL_MAX_EXACT, dist, large).astype(np.int32)


def group_bias(rel_bias, g):
    dil = A_GROUPS[g][1]
    buckets = t5_bucket(np.arange(A_KEYS + 1) * dil)
    return rel_bias[buckets][:, g * A_HEADS:(g + 1) * A_HEADS].T.astype(jnp.float32)


def dilated_window_prompt(q, k, v, bias, dil):
    b, s, h, dh = q.shape
    m = s // dil
    mp = -(-m // A_BLOCK) * A_BLOCK
    nb = mp // A_BLOCK

    def strided(t):
        t = t.reshape(b, m, dil, h, dh).transpose(0, 2, 1, 3, 4).reshape(b * dil, m, h, dh)
        return jnp.pad(t, ((0, 0), (0, mp - m), (0, 0), (0, 0)))

    def band(t):
        tb = t.reshape(b * dil, nb, A_BLOCK, h, dh)
        prev = jnp.pad(tb, ((0, 0), (1, 0), (0, 0), (0, 0), (0, 0)))[:, :-1]
        return jnp.concatenate([prev, tb], axis=2)

    qb = strided(q).reshape(b * dil, nb, A_BLOCK, h, dh)
    kb, vb = band(strided(k)), band(strided(v))
    scores = jnp.einsum('bnqhd,bnkhd->bnhqk', qb, kb,
                        preferred_element_type=jnp.float32) * (A_HEAD_DIM ** -0.5)
    qi = np.arange(A_BLOCK)[:, None]
    kj = np.arange(2 * A_BLOCK)[None, :]
    dist = A_BLOCK + qi - kj
    valid = (dist >= 0) & (dist <= A_KEYS)
    first = valid & (kj >= A_BLOCK)
    valid_nb = np.concatenate([first[None], np.broadcast_to(valid, (nb - 1,) + valid.shape)], 0)
    bias_full = bias[:, np.clip(dist, 0, A_KEYS)]
    logits = jnp.where(valid_nb[None, :, None], scores + bias_full[None, None], -jnp.inf)
    lse = jax.nn.logsumexp(logits, axis=-1)
    p = jnp.exp(logits - lse[..., None])
    o = jnp.einsum('bnhqk,bnkhd->bnqhd', p.astype(vb.dtype), vb)
    o = o.reshape(b, dil, mp, h, dh)[:, :, :m].transpose(0, 2, 1, 3, 4).reshape(b, s, h, dh)
    lse = lse.transpose(0, 1, 3, 2).reshape(b, dil, mp, h)[:, :, :m]
    lse = lse.transpose(0, 2, 1, 3).reshape(b, s, h)
    return o, lse


def dilated_window_sample(q, k_new, v_new, kv_buf, bias, dil):
    b, t, h, dh = q.shape
    wb = kv_buf.shape[1]
    keys = jnp.concatenate([kv_buf[:, :, 0].astype(k_new.dtype), k_new], axis=1)
    vals = jnp.concatenate([kv_buf[:, :, 1].astype(v_new.dtype), v_new], axis=1)
    idx = wb + np.arange(t)[:, None] - dil * np.arange(A_KEYS + 1)[None, :]
    valid = idx >= 0
    idx = np.maximum(idx, 0)
    kg, vg = keys[:, idx], vals[:, idx]
    scores = jnp.einsum('bthd,btkhd->bhtk', q, kg,
                        preferred_element_type=jnp.float32) * (A_HEAD_DIM ** -0.5)
    logits = jnp.where(valid[None, None], scores + bias[:, None, :], -jnp.inf)
    lse = jax.nn.logsumexp(logits, axis=-1)
    p = jnp.exp(logits - lse[..., None])
    o = jnp.einsum('bhtk,btkhd->bthd', p.astype(vg.dtype), vg)
    return o, lse.transpose(0, 2, 1)


def causal_conv(u, buf, w, bias):
    kw = w.shape[0]
    full = jnp.concatenate([buf.astype(u.dtype), u], axis=1)
    out = lax.conv_general_dilated(full, w[:, None, :].astype(u.dtype), window_strides=(1,),
                                   padding='VALID', dimension_numbers=('NWC', 'WIO', 'NWC'),
                                   feature_group_count=u.shape[-1])
    return out + bias, full[:, full.shape[1] - (kw - 1):]


def ssd_scan(x, dt, a, bm, cm, h0):
    f32 = jnp.float32
    b, L, nh, p = x.shape
    g, n = bm.shape[2], bm.shape[3]
    r = nh // g
    q = B_CHUNK if L % B_CHUNK == 0 else L
    nc = L // q
    xr = x.astype(f32).reshape(b, nc, q, g, r, p)
    dtr = dt.reshape(b, nc, q, g, r)
    br = bm.astype(f32).reshape(b, nc, q, g, n)
    cr = cm.astype(f32).reshape(b, nc, q, g, n)
    acum = jnp.cumsum(dtr * a.reshape(g, r), axis=2)
    xdt = xr * dtr[..., None]
    causal = np.tril(np.ones((q, q), bool))[:, :, None, None]
    seg = acum[:, :, :, None] - acum[:, :, None, :]
    decay_ls = jnp.exp(jnp.where(causal, seg, -jnp.inf))
    cb = jnp.einsum('bclgn,bcsgn->bclsg', cr, br)
    y_diag = jnp.einsum('bclsgr,bcsgrp->bclgrp', cb[..., None] * decay_ls, xdt)
    to_end = jnp.exp(acum[:, :, -1:] - acum)
    chunk_states = jnp.einsum('bclgn,bclgrp->bcgrpn', br, xdt * to_end[..., None])
    chunk_decay = jnp.exp(acum[:, :, -1])

    def step(hc, inp):
        dec, st = inp
        return dec[..., None, None] * hc + st, hc

    h_last, h_in = lax.scan(step, h0.astype(f32).reshape(b, g, r, p, n),
                            (jnp.moveaxis(chunk_decay, 1, 0), jnp.moveaxis(chunk_states, 1, 0)))
    h_in = jnp.moveaxis(h_in, 0, 1)
    y_off = jnp.einsum('bclgn,bcgrpn->bclgrp', cr, h_in) * jnp.exp(acum)[..., None]
    y = (y_diag + y_off).reshape(b, L, nh, p)
    return y.astype(x.dtype), h_last.reshape(b, nh, p, n)


def rg_lru(xc, h0, w_r, b_r, w_i, b_i, lam):
    f32 = jnp.float32
    b, L, _ = xc.shape
    x32 = xc.astype(f32)
    xb = x32.reshape(b, L, C_BLOCKS, C_BLOCK_DIM)
    rg = jax.nn.sigmoid(jnp.einsum('blhi,hij->blhj', xb, w_r.astype(f32)).reshape(b, L, C_WIDTH) + b_r)
    ig = jax.nn.sigmoid(jnp.einsum('blhi,hij->blhj', xb, w_i.astype(f32)).reshape(b, L, C_WIDTH) + b_i)
    log_a = -C_POW * rg * jax.nn.softplus(-lam.astype(f32))
    a = jnp.exp(log_a)
    u = x32 * ig * jnp.sqrt(-jnp.expm1(2.0 * log_a))
    u = u.at[:, 0].add(a[:, 0] * h0.astype(f32))

    def comb(e1, e2):
        a1, b1 = e1
        a2, b2 = e2
        return a1 * a2, a2 * b1 + b2

    _, hs = lax.associative_scan(comb, (a, u), axis=1)
    return hs.astype(xc.dtype), hs[:, -1]


def mixer(h, lw, st, rel_bias, prompt):
    b, L, _ = h.shape
    f32 = jnp.float32
    offs = [int(o) for o in np.cumsum(IN_SPLIT)[:-1]]
    qa, ka, va, zb, xbc, dtb, xc, gc, gates = jnp.split(h @ lw['w_in'], offs, axis=-1)
    if prompt:
        conv_b0 = jnp.zeros((b, B_CONV - 1, B_CONV_CH), h.dtype)
        ssm0 = jnp.zeros((b, B_HEADS, B_HEAD_DIM, B_STATE), f32)
        conv_c0 = jnp.zeros((b, C_CONV - 1, C_WIDTH), h.dtype)
        lru0 = jnp.zeros((b, C_WIDTH), f32)
    else:
        conv_b0, ssm0, conv_c0, lru0 = st['conv_b'], st['ssm'], st['conv_c'], st['lru']
    shp = (b, L, A_N_GROUPS, A_HEADS, A_HEAD_DIM)
    qa, ka, va = qa.reshape(shp), ka.reshape(shp), va.reshape(shp)
    outs, lses, new_kv = [], [], []
    for g, (win, dil) in enumerate(A_GROUPS):
        bias = group_bias(rel_bias, g)
        if prompt:
            o, lse = dilated_window_prompt(qa[:, :, g], ka[:, :, g], va[:, :, g], bias, dil)
            keep = min(win, L)
            new_kv.append(jnp.stack([ka[:, L - keep:, g], va[:, L - keep:, g]], axis=2))
        else:
            o, lse = dilated_window_sample(qa[:, :, g], ka[:, :, g], va[:, :, g], st['kv'][g], bias, dil)
            new_kv.append(jnp.stack([ka[:, :, g], va[:, :, g]], axis=2))
        outs.append(o)
        lses.append(lse)
    wgt = jax.nn.softmax(jnp.stack(lses, 0), axis=0)
    oa = jnp.einsum('gblh,gblhd->blhd', wgt, jnp.stack(outs, 0).astype(f32))
    ya = oa.reshape(b, L, A_WIDTH).astype(h.dtype) @ lw['w_a_proj']
    xbc, conv_b_new = causal_conv(xbc, conv_b0, lw['conv_b_w'], lw['conv_b_b'])
    xbc = jax.nn.silu(xbc)
    xs, bm, cm = jnp.split(xbc, [B_WIDTH, B_WIDTH + B_GROUPS * B_STATE], axis=-1)
    xs = xs.reshape(b, L, B_HEADS, B_HEAD_DIM)
    dt = jax.nn.softplus(dtb.astype(f32) + lw['dt_bias'].astype(f32))
    a_neg = -jnp.exp(lw['a_log'].astype(f32))
    y, ssm_new = ssd_scan(xs, dt, a_neg, bm.reshape(b, L, B_GROUPS, B_STATE),
                          cm.reshape(b, L, B_GROUPS, B_STATE), ssm0)
    y = y + lw['d_skip'][:, None].astype(y.dtype) * xs
    y = y.reshape(b, L, B_WIDTH) * jax.nn.silu(zb)
    y = rmsnorm(y.reshape(b, L, B_GROUPS, B_WIDTH // B_GROUPS),
                lw['g_ssm_norm'].reshape(B_GROUPS, B_WIDTH // B_GROUPS)).reshape(b, L, B_WIDTH)
    yb = y @ lw['w_b_proj']
    xc, conv_c_new = causal_conv(xc, conv_c0, lw['conv_c_w'], lw['conv_c_b'])
    hc, lru_new = rg_lru(xc, lru0, lw['w_rgate'], lw['b_rgate'], lw['w_igate'], lw['b_igate'],
                         lw['lru_lambda'])
    yc = (hc * jax.nn.gelu(gc)) @ lw['w_c_proj']
    ga, gb, gcc = jnp.split(jax.nn.sigmoid(gates), 3, axis=-1)
    out = (ga * ya + gb * yb + gcc * yc) @ lw['w_out']
    return out, (new_kv[0], new_kv[1], new_kv[2], conv_b_new, ssm_new, conv_c_new, lru_new)


def block(x, c, lw, st, rel_bias, prompt):
    mod = jax.nn.silu(c) @ lw['w_ada'] + lw['b_ada']
    sh1, sc1, g1, sh2, sc2, g2, sh3, sc3, g3 = jnp.split(mod[:, None, :], N_MOD, axis=-1)
    h = rmsnorm(x, lw['g_ff1']) * (1 + sc1) + sh1
    x = x + 0.5 * g1 * swiglu(h, lw['w_ff1_in'], lw['w_ff1_out'])
    h = rmsnorm(x, lw['g_mix']) * (1 + sc2) + sh2
    m, new_st = mixer(h, lw, st, rel_bias, prompt)
    x = x + g2 * m
    h = rmsnorm(x, lw['g_ff2']) * (1 + sc3) + sh3
    x = x + 0.5 * g3 * swiglu(h, lw['w_ff2_in'], lw['w_ff2_out'])
    return x, new_st


def setup_inputs(seed: int = 0) -> dict:
    key = jax.random.key(seed)
    ks = iter(jax.random.split(key, 64))
    f32 = jnp.float32

    def nrm(shape, scale):
        return jax.random.normal(next(ks), shape, f32) * scale

    wb = [min(w, PAST_LEN) for w, _ in A_GROUPS]
    dt0 = jnp.exp(jax.random.uniform(next(ks), (DEPTH, B_HEADS), f32, math.log(1e-3), math.log(1e-1)))
    a_init = jax.random.uniform(next(ks), (DEPTH, B_HEADS), f32, 1.0, 16.0)
    a0 = jax.random.uniform(next(ks), (DEPTH, C_WIDTH), f32, 0.9, 0.999)
    sig = a0 ** (1.0 / C_POW)
    D = D_MODEL
    return {
        "x_prompt": nrm((BATCH, SEQ, D), 1.0),
        "x_sample": nrm((DEC_BATCH, DEC_SEQ, D), 1.0),
        "c_prompt": nrm((BATCH, D), 1.0),
        "c_sample": nrm((DEC_BATCH, D), 1.0),
        "cache_win1_kv": nrm((DEPTH, DEC_BATCH, wb[0], 2, A_HEADS, A_HEAD_DIM), 1.0),
        "cache_win2_kv": nrm((DEPTH, DEC_BATCH, wb[1], 2, A_HEADS, A_HEAD_DIM), 1.0),
        "cache_win3_kv": nrm((DEPTH, DEC_BATCH, wb[2], 2, A_HEADS, A_HEAD_DIM), 1.0),
        "state_conv_b": nrm((DEPTH, DEC_BATCH, B_CONV - 1, B_CONV_CH), 1.0),
        "state_ssm": nrm((DEPTH, DEC_BATCH, B_HEADS, B_HEAD_DIM, B_STATE), 0.1),
        "state_conv_c": nrm((DEPTH, DEC_BATCH, C_CONV - 1, C_WIDTH), 1.0),
        "state_lru": nrm((DEPTH, DEC_BATCH, C_WIDTH), 0.5),
        "rel_bias": nrm((REL_BUCKETS, A_N_GROUPS * A_HEADS), 0.2),
        "w_ada": nrm((DEPTH, D, N_MOD * D), 0.5 * D ** -0.5),
        "b_ada": nrm((DEPTH, N_MOD * D), 0.02),
        "g_ff1": 1.0 + nrm((DEPTH, D), 0.05),
        "w_ff1_in": nrm((DEPTH, D, 2 * D_FF), D ** -0.5),
        "w_ff1_out": nrm((DEPTH, D_FF, D), D_FF ** -0.5),
        "g_mix": 1.0 + nrm((DEPTH, D), 0.05),
        "w_in": nrm((DEPTH, D, N_IN), D ** -0.5),
        "w_a_proj": nrm((DEPTH, A_WIDTH, D), A_WIDTH ** -0.5),
        "conv_b_w": nrm((DEPTH, B_CONV, B_CONV_CH), B_CONV ** -0.5),
        "conv_b_b": nrm((DEPTH, B_CONV_CH), 0.02),
        "dt_bias": dt0 + jnp.log(-jnp.expm1(-dt0)),
        "a_log": jnp.log(a_init),
        "d_skip": 1.0 + nrm((DEPTH, B_HEADS), 0.05),
        "g_ssm_norm": 1.0 + nrm((DEPTH, B_WIDTH), 0.05),
        "w_b_proj": nrm((DEPTH, B_WIDTH, D), B_WIDTH ** -0.5),
        "conv_c_w": nrm((DEPTH, C_CONV, C_WIDTH), C_CONV ** -0.5),
        "conv_c_b": nrm((DEPTH, C_WIDTH), 0.02),
        "w_rgate": nrm((DEPTH, C_BLOCKS, C_BLOCK_DIM, C_BLOCK_DIM), C_BLOCK_DIM ** -0.5),
        "b_rgate": nrm((DEPTH, C_WIDTH), 0.02),
        "w_igate": nrm((DEPTH, C_BLOCKS, C_BLOCK_DIM, C_BLOCK_DIM), C_BLOCK_DIM ** -0.5),
        "b_igate": nrm((DEPTH, C_WIDTH), 0.02),
        "lru_lambda": jnp.log(sig) - jnp.log1p(-sig),
        "w_c_proj": nrm((DEPTH, C_WIDTH, D), C_WIDTH ** -0.5),
        "w_out": nrm((DEPTH, D, D), D ** -0.5),
        "g_ff2": 1.0 + nrm((DEPTH, D), 0.05),
        "w_ff2_in": nrm((DEPTH, D, 2 * D_FF), D ** -0.5),
        "w_ff2_out": nrm((DEPTH, D_FF, D), D_FF ** -0.5),
        "g_final": 1.0 + nrm((D,), 0.05),
    }


def reference(x_prompt, x_sample, c_prompt, c_sample, cache_win1_kv, cache_win2_kv, cache_win3_kv,
              state_conv_b, state_ssm, state_conv_c, state_lru, rel_bias, w_ada, b_ada, g_ff1,
              w_ff1_in, w_ff1_out, g_mix, w_in, w_a_proj, conv_b_w, conv_b_b, dt_bias, a_log, d_skip,
              g_ssm_norm, w_b_proj, conv_c_w, conv_c_b, w_rgate, b_rgate, w_igate, b_igate, lru_lambda,
              w_c_proj, w_out, g_ff2, w_ff2_in, w_ff2_out, g_final):
    yp, ys = x_prompt, x_sample
    new_p = [[] for _ in range(7)]
    new_s = [[] for _ in range(7)]
    for l in range(DEPTH):
        lw = dict(w_ada=w_ada[l], b_ada=b_ada[l], g_ff1=g_ff1[l], w_ff1_in=w_ff1_in[l],
                  w_ff1_out=w_ff1_out[l], g_mix=g_mix[l], w_in=w_in[l], w_a_proj=w_a_proj[l],
                  conv_b_w=conv_b_w[l], conv_b_b=conv_b_b[l], dt_bias=dt_bias[l], a_log=a_log[l],
                  d_skip=d_skip[l], g_ssm_norm=g_ssm_norm[l], w_b_proj=w_b_proj[l],
                  conv_c_w=conv_c_w[l], conv_c_b=conv_c_b[l], w_rgate=w_rgate[l], b_rgate=b_rgate[l],
                  w_igate=w_igate[l], b_igate=b_igate[l], lru_lambda=lru_lambda[l],
                  w_c_proj=w_c_proj[l], w_out=w_out[l], g_ff2=g_ff2[l], w_ff2_in=w_ff2_in[l],
                  w_ff2_out=w_ff2_out[l])
        st = dict(kv=(cache_win1_kv[l], cache_win2_kv[l], cache_win3_kv[l]), conv_b=state_conv_b[l],
                  ssm=state_ssm[l], conv_c=state_conv_c[l], lru=state_lru[l])
        yp, stp = block(yp, c_prompt, lw, None, rel_bias, True)
        ys, sts = block(ys, c_sample, lw, st, rel_bias, False)
        for i in range(7):
            new_p[i].append(stp[i])
            new_s[i].append(sts[i])
    yp = rmsnorm(yp, g_final)
    ys = rmsnorm(ys, g_final)
    p_kv1, p_kv2, p_kv3, p_conv_b, p_ssm, p_conv_c, p_lru = [jnp.stack(v, 0) for v in new_p]
    s_kv1, s_kv2, s_kv3, s_conv_b, s_ssm, s_conv_c, s_lru = [jnp.stack(v, 0) for v in new_s]
    return (yp, ys, p_kv1, p_kv2, p_kv3, p_conv_b, p_ssm, p_conv_c, p_lru,
            s_kv1, s_kv2, s_kv3, s_conv_b, s_ssm, s_conv_c, s_lru)
```

```python
import math
from contextlib import ExitStack
import numpy as np
import concourse.bass as bass
import concourse.mybir as mybir
from concourse.bass_utils import run_bass_kernel_spmd

F32 = mybir.dt.float32
BF16 = mybir.dt.bfloat16
ALU = mybir.AluOpType
AF = mybir.ActivationFunctionType
AX = mybir.AxisListType

NCORES = 8
D = 1024
KC = 8
DEPTH = 2
SEQ = 2048
NPS = 2
NSS = 4
DL = 8
ST = NSS * DL
NSEQ = NPS + NSS
DFF = 2816
FC = 22
NIN = 12304
import os
MIX_PARTS = int(os.environ.get("MK_PARTS", "3"))
ATTN_P = int(os.environ.get("MK_ATTN_P", "1"))
ATTN_S = int(os.environ.get("MK_ATTN_S", "1"))
NLAY = int(os.environ.get("MK_LAYERS", "2"))
A_STAGE = int(os.environ.get("MK_ASTAGE", "4"))
A_HP = int(os.environ.get("MK_HP", "4"))
A_GQS = [int(x) for x in os.environ.get("MK_GQS", "0,1,2").split(",")]
GRPS = [int(x) for x in os.environ.get("MK_GRPS", "0,1,2").split(",")]
EPS = 1e-6
GROUPS = ((128, 1), (512, 4), (2048, 16))
O_Q, O_K, O_V = 0, 1536, 3072
O_Z = 4608
O_XBC = 5632
O_DT = 7168
O_XC = 7184
O_GC = 8208
O_GATES = 9232


def t5_bucket(dist):
    dist = np.asarray(dist)
    large = 16 + (np.log(np.maximum(dist, 1) / 16) / math.log(2048 / 16) * 16).astype(np.int64)
    large = np.minimum(large, 31)
    return np.where(dist < 16, dist, large).astype(np.int32)


class Buf:
    __slots__ = ("w", "r")

    def __init__(self):
        self.w = None
        self.r = []


class K:
    def __init__(self, nc):
        self.nc = nc
        self.engs = {"pe": nc.tensor, "act": nc.scalar, "dve": nc.vector, "pool": nc.gpsimd, "sp": nc.sync}
        self.sem = {}
        self.cnt = {}
        for e in ("pe", "act", "dve", "pool"):
            self.sem[e] = nc.alloc_semaphore("s_" + e)
            self.cnt[e] = 0
        self.known = {e: {} for e in self.engs}
        self.dpool = {}
        for q, n in (("sp", 24), ("pool", 16), ("act", 6)):
            self.dpool[q] = [[nc.alloc_semaphore(f"d_{q}{i}"), 0] for i in range(n)]
        self.dnext = {q: 0 for q in self.dpool}
        self.pe_sem_ids = {id(self.sem["pe"])}
        self.nsem = 0
        self.last = {}

    def _need(self, e, deps):
        m = {}
        for d in deps:
            if d is None:
                continue
            s, v = d
            if e == "pe" and id(s) in self.pe_sem_ids:
                continue
            if m.get(id(s), (None, 0))[1] < v:
                m[id(s)] = (s, v)
        out = []
        kn = self.known[e]
        for k, (s, v) in m.items():
            if kn.get(k, 0) < v:
                kn[k] = v
                out.append((s, v))
        return out

    @staticmethod
    def _deps(reads, writes):
        deps = []
        for b in reads:
            deps.append(b.w)
        for b in writes:
            deps.append(b.w)
            deps.extend(b.r)
        return deps

    def op(self, e, fn, reads=(), writes=()):
        eng = self.engs[e]
        for (s, v) in self._need(e, self._deps(reads, writes)):
            eng.wait_ge(s, v)
        if self.cnt[e] >= 30000:
            self.nsem += 1
            self.sem[e] = self.nc.alloc_semaphore(f"s_{e}_{self.nsem}")
            self.cnt[e] = 0
            if e == "pe":
                self.pe_sem_ids.add(id(self.sem[e]))
        ins = fn(eng)
        self.cnt[e] += 1
        ins.then_inc(self.sem[e], 1)
        tok = (self.sem[e], self.cnt[e])
        self.last[e] = tok
        for b in reads:
            b.r.append(tok)
            if len(b.r) > 24:
                b.r = self._compact(b.r)
        for b in writes:
            b.w = tok
            b.r = []
        return ins

    @staticmethod
    def _compact(lst):
        m = {}
        for (s, v) in lst:
            if m.get(id(s), (None, 0))[1] < v:
                m[id(s)] = (s, v)
        return list(m.values())

    def dma(self, q, out, in_, reads=(), writes=(), **kw):
        eng = self.engs[q]
        pool = self.dpool[q]
        i = self.dnext[q]
        self.dnext[q] = (i + 1) % len(pool)
        slot = pool[i]
        deps = self._deps(reads, writes)
        if slot[1] > 0:
            deps.append((slot[0], slot[1]))
        for (s, v) in self._need(q, deps):
            eng.wait_ge(s, v)
        slot[1] += 16
        eng.dma_start(out=out, in_=in_, **kw).then_inc(slot[0], 16)
        tok = (slot[0], slot[1])
        for b in reads:
            b.r.append(tok)
            if len(b.r) > 24:
                b.r = self._compact(b.r)
        for b in writes:
            b.w = tok
            b.r = []

    def barrier(self):
        deps = [self.last[e] for e in self.last]
        for q in self.dpool:
            for slot in self.dpool[q]:
                if slot[1] > 0:
                    deps.append((slot[0], slot[1]))
        for e in self.engs:
            for (s, v) in self._need(e, deps):
                self.engs[e].wait_ge(s, v)


def build_program():
    nc = bass.Bass("TRN2", target_bir_lowering=False)
    k = K(nc)

    def din(name, shape):
        return nc.dram_tensor(name, list(shape), F32, kind="ExternalInput").ap()

    def dout(name, shape):
        return nc.dram_tensor(name, list(shape), F32, kind="ExternalOutput").ap()

    xp = din("xp", [NPS, SEQ, D])
    xs = din("xs", [ST, D])
    cT = din("cT", [128, KC, NSEQ])
    kvc = [din("kvc1", [DEPTH, NSS, 128, 1024]), din("kvc2", [DEPTH, NSS, 512, 1024]),
           din("kvc3", [DEPTH, NSS, 2048, 1024])]
    st_cb = din("st_cb", [DEPTH, NSS, 128, 12, 3])
    st_ssm = din("st_ssm", [DEPTH, NSS, 1024, 128])
    st_cc = din("st_cc", [DEPTH, NSS, 128, 8, 3])
    st_lru = din("st_lru", [DEPTH, NSS, 128, 8])
    ebias = din("ebias", [128, 24, 256])
    emask = din("emask", [128, 256])
    selg = din("selg", [3, NSEQ, 128])
    selk = din("selk", [128, 32, 32])
    w_ada = din("w_ada", [DEPTH, D, 9 * D])
    b_adaT = din("b_adaT", [DEPTH, 128, 72])
    b_adaG = din("b_adaG", [DEPTH, NSEQ, 3 * D])
    gnT = [din("g_ff1T", [DEPTH, 128, KC]), din("g_mixT", [DEPTH, 128, KC]), din("g_ff2T", [DEPTH, 128, KC])]
    gfin = din("gfin", [128, D])
    w_ffi = [din("w_ff1_in", [DEPTH, D, 2 * DFF]), din("w_ff2_in", [DEPTH, D, 2 * DFF])]
    w_ffo = [din("w_ff1_out", [DEPTH, DFF, D]), din("w_ff2_out", [DEPTH, DFF, D])]
    w_in = din("w_in", [DEPTH, D, NIN])
    w_ap = din("w_a_proj", [DEPTH, 512, D])
    w_bp = din("w_b_proj", [DEPTH, D, D])
    w_cp = din("w_c_proj", [DEPTH, D, D])
    w_o = din("w_out", [DEPTH, D, D])
    cbwT = din("cbwT", [DEPTH, 128, 12, 4])
    cbbT = din("cbbT", [DEPTH, 128, 12])
    ccwT = din("ccwT", [DEPTH, 128, 8, 4])
    ccbT = din("ccbT", [DEPTH, 128, 8])
    dtb_bc = din("dtb_bc", [DEPTH, 128, 16])
    alog_bc = din("alog_bc", [DEPTH, 128, 16])
    dsk_bc = din("dsk_bc", [DEPTH, 128, 16])
    gssm_bc = din("gssm_bc", [DEPTH, 128, D])
    w_rg = din("w_rgate", [DEPTH, 8, 128, 128])
    w_ig = din("w_igate", [DEPTH, 8, 128, 128])
    brT = din("brT", [DEPTH, 128, 8])
    biT = din("biT", [DEPTH, 128, 8])
    lamT = din("lamT", [DEPTH, 128, 8])

    yp = dout("yp", [NPS, SEQ, D])
    ys = dout("ys", [ST, D])
    pkv = [dout("pkv1", [DEPTH, NPS, 128, 1024]), dout("pkv2", [DEPTH, NPS, 512, 1024]),
           dout("pkv3", [DEPTH, NPS, 2048, 1024])]
    skv = [dout("skv1", [DEPTH, NSS, DL, 1024]), dout("skv2", [DEPTH, NSS, DL, 1024]),
           dout("skv3", [DEPTH, NSS, DL, 1024])]
    o_cb = dout("o_cb", [DEPTH, NSEQ, 128, 12, 3])
    o_ssm = dout("o_ssm", [DEPTH, NSEQ, 1024, 128])
    o_cc = dout("o_cc", [DEPTH, NSEQ, 128, 8, 3])
    o_lru = dout("o_lru", [DEPTH, NSEQ, 128, 8])

    xres = nc.dram_tensor("xres", [NPS * SEQ + ST, D], F32).ap()
    gsc = nc.dram_tensor("gsc", [NSEQ, 3 * D], F32).ap()
    def scr(name, shape):
        return nc.dram_tensor(name, list(shape), BF16).ap()
    S_ffi = [[scr(f"S_ffi{l}_{w}", [44, 128, KC, 128]) for w in range(2)] for l in range(DEPTH)]
    S_ffo = [[scr(f"S_ffo{l}_{w}", [4, 128, FC, 256]) for w in range(2)] for l in range(DEPTH)]
    S_inA = [scr(f"S_inA{l}", [56, 128, KC, 128]) for l in range(DEPTH)]
    S_dt = [scr(f"S_dt{l}", [128, KC, 16]) for l in range(DEPTH)]
    S_inB = [scr(f"S_inB{l}", [40, 128, KC, 128]) for l in range(DEPTH)]
    S_ap = [scr(f"S_ap{l}", [8, 128, 4, 128]) for l in range(DEPTH)]
    S_bp = [scr(f"S_bp{l}", [8, 128, KC, 128]) for l in range(DEPTH)]
    S_cp = [scr(f"S_cp{l}", [8, 128, KC, 128]) for l in range(DEPTH)]
    S_o = [scr(f"S_o{l}", [4, 128, KC, 256]) for l in range(DEPTH)]

    def inA(l, col):
        assert col % 128 == 0 and col < 7168
        return S_inA[l][col // 128]

    def inB(l, col):
        assert (col - 7184) % 128 == 0 and col >= 7184
        return S_inB[l][(col - 7184) // 128]
    b_gsc = Buf()
    xres_b = [[Buf() for _ in range(2)] for _ in range(NPS)] + [[Buf()]]

    def sb(name, shape, dt=F32):
        return nc.alloc_sbuf_tensor(name, list(shape), dt)

    uid = [0]

    def TMP(es, name, shape, dt=F32):
        uid[0] += 1
        return es.enter_context(nc.sbuf_tensor(f"{name}_{uid[0]}", list(shape), dt))

    identf = sb("identf", [128, 128]); ident = sb("ident", [128, 128], BF16)
    tri = sb("tri", [128, 128])
    negtri = sb("negtri", [128, 128])
    ones_f = sb("ones_f", [128, 128])
    ones_b = sb("ones_b", [128, 128], BF16)
    epsb = sb("epsb", [128, 1]); oneb = sb("oneb", [128, 1])
    E = sb("E", [128, 24, 256], BF16)
    csil = sb("csil", [128, KC, NSEQ], BF16)
    modT = sb("modT", [128, 72, NSEQ])
    modA = sb("modA", [128, 3, KC, NSEQ]); modB = sb("modB", [128, 3, KC, NSEQ])
    gbc = sb("gbc", [128, 3, D])
    gn_sb = sb("gn_sb", [128, 3, KC])
    cbw = sb("cbw", [128, 12, 4]); cbb = sb("cbb", [128, 12]); ccw = sb("ccw", [128, 8, 4]); ccb = sb("ccb", [128, 8])
    dtb_sb = sb("dtb_sb", [128, 16]); aneg_sb = sb("aneg_sb", [128, 16]); dsk_sb = sb("dsk_sb", [128, 16])
    br_sb = sb("br_sb", [128, 8]); bi_sb = sb("bi_sb", [128, 8]); cneg_sb = sb("cneg_sb", [128, 8])
    b_const, b_E, b_mod, b_lay, b_gbc, b_AB = Buf(), Buf(), Buf(), Buf(), Buf(), Buf()

    PS = [nc.alloc_psum_tensor(f"ps{i}", [128, 512], F32) for i in range(8)]
    PSB = [Buf() for _ in range(8)]
    ps_i = [0]

    ps_lim = [8]

    def next_ps():
        i = ps_i[0] % ps_lim[0]
        ps_i[0] = (i + 1) % ps_lim[0]
        return PS[i], PSB[i]

    def bf(ps):
        return ps[:].bitcast(BF16)

    k.op("pool", lambda e: e.memset(identf[:], 1.0), writes=[b_const])
    k.op("pool", lambda e: e.affine_select(out=identf[:], in_=identf[:], pattern=[[-1, 128]], compare_op=ALU.is_equal,
                                           fill=0.0, base=0, channel_multiplier=1), reads=[b_const], writes=[b_const])
    k.op("pool", lambda e: e.memset(tri[:], 1.0), writes=[b_const])
    k.op("pool", lambda e: e.affine_select(out=tri[:], in_=tri[:], pattern=[[1, 128]], compare_op=ALU.is_ge,
                                           fill=0.0, base=0, channel_multiplier=-1), reads=[b_const], writes=[b_const])
    k.op("pool", lambda e: e.memset(negtri[:], 0.0), writes=[b_const])
    k.op("pool", lambda e: e.affine_select(out=negtri[:], in_=negtri[:], pattern=[[1, 128]], compare_op=ALU.is_ge,
                                           fill=-30000.0, base=0, channel_multiplier=-1), reads=[b_const], writes=[b_const])
    k.op("dve", lambda e: e.memset(ones_f[:], 1.0), writes=[b_const])
    k.op("dve", lambda e: e.memset(ones_b[:], 1.0), writes=[b_const])
    k.op("dve", lambda e: e.memset(epsb[:], EPS), writes=[b_const])
    k.op("dve", lambda e: e.memset(oneb[:], 1.0), writes=[b_const])
    k.op("dve", lambda e: e.tensor_copy(out=ident[:], in_=identf[:]), reads=[b_const], writes=[b_const])

    with ExitStack() as es:
        stg = TMP(es, "stg", [128, 8, 256], F32)
        msk = TMP(es, "msk", [128, 256], F32)
        cst = TMP(es, "cst", [128, KC, NSEQ], F32)
        b_stg = Buf()
        k.dma("sp", msk[:], emask, writes=[b_stg])
        for i in range(3):
            k.dma("sp", stg[:], ebias[:, i * 8:(i + 1) * 8, :], writes=[b_stg])
            k.op("act", lambda e: e.activation(out=stg[:], in_=stg[:], func=AF.Exp), reads=[b_stg], writes=[b_stg])
            k.op("dve", lambda e: e.tensor_tensor(out=E[:, i * 8:(i + 1) * 8, :], in0=stg[:],
                                                  in1=msk[:].unsqueeze(1).to_broadcast([128, 8, 256]), op=ALU.mult),
                 reads=[b_stg], writes=[b_E])
        k.dma("sp", cst[:], cT, writes=[b_stg])
        k.op("act", lambda e: e.activation(out=csil[:], in_=cst[:], func=AF.Silu), reads=[b_stg], writes=[b_mod])
        k.barrier()

    def wload(dst, src, buf):
        if src.dtype == BF16:
            k.dma("sp", dst, src, writes=[buf])
        else:
            k.dma("pool", dst, src, writes=[buf])

    def precast(l):
        with ExitStack() as es:
            stg = [TMP(es, f"pc_s{i}", [128, NIN], BF16) for i in range(2)]
            bst = [Buf(), Buf()]
            cnt = [0]

            def chunk(src_rows, n, outs):
                i = cnt[0] % 2
                cnt[0] += 1
                k.dma("pool", stg[i][:, 0:n], src_rows, writes=[bst[i]])
                for (dst, c0, nb, bw) in outs:
                    b0 = 0
                    while b0 < nb:
                        nn = min(16, nb - b0)
                        k.dma("sp", dst[:, b0:b0 + nn, :],
                              stg[i][:, c0 + b0 * bw:c0 + (b0 + nn) * bw].rearrange("p (b n) -> p b n", n=bw), reads=[bst[i]])
                        b0 += nn

            for w in range(2):
                for c in range(KC):
                    chunk(w_ffi[w][l][c * 128:(c + 1) * 128, :], 2 * DFF,
                          [(S_ffi[l][w][:, :, c, :].rearrange("b p n -> p b n"), 0, 44, 128)])
                for j in range(FC):
                    chunk(w_ffo[w][l][j * 128:(j + 1) * 128, :], D,
                          [(S_ffo[l][w][:, :, j, :].rearrange("q p n -> p q n"), 0, 4, 256)])
            for c in range(KC):
                chunk(w_in[l][c * 128:(c + 1) * 128, :], NIN,
                      [(S_inA[l][:, :, c, :].rearrange("b p n -> p b n"), 0, 56, 128),
                       (S_dt[l][:, c:c + 1, :], 7168, 1, 16),
                       (S_inB[l][:, :, c, :].rearrange("b p n -> p b n"), 7184, 40, 128)])
            for c in range(4):
                chunk(w_ap[l][c * 128:(c + 1) * 128, :], D, [(S_ap[l][:, :, c, :].rearrange("b p n -> p b n"), 0, 8, 128)])
            for c in range(KC):
                chunk(w_bp[l][c * 128:(c + 1) * 128, :], D, [(S_bp[l][:, :, c, :].rearrange("b p n -> p b n"), 0, 8, 128)])
                chunk(w_cp[l][c * 128:(c + 1) * 128, :], D, [(S_cp[l][:, :, c, :].rearrange("b p n -> p b n"), 0, 8, 128)])
                chunk(w_o[l][c * 128:(c + 1) * 128, :], D, [(S_o[l][:, :, c, :].rearrange("q p n -> p q n"), 0, 4, 256)])
            k.barrier()

    def ada(l):
        with ExitStack() as es:
            wa = TMP(es, "wa_full", [128, KC, 9 * D], BF16)
            bwa = [Buf() for _ in range(KC)]
            badT = TMP(es, "badT", [128, 72], F32)
            badG = TMP(es, "badG", [NSEQ, 3 * D], F32)
            lam_t = TMP(es, "lam_t", [128, 8], F32)
            modrows = TMP(es, "modrows", [NSEQ, 3 * D], F32)
            b_mr = Buf()
            b_t = Buf()
            k.dma("sp", badT[:], b_adaT[l], writes=[b_t])
            k.dma("sp", badG[:], b_adaG[l], writes=[b_t])
            for c in range(KC):
                k.dma("pool", wa[:, c, :], w_ada[l][c * 128:(c + 1) * 128, :], writes=[bwa[c]])
            ps, pb = next_ps()
            for j in range(72):
                for c in range(KC):
                    k.op("pe", lambda e: e.matmul(ps[:, j * NSEQ:(j + 1) * NSEQ], lhsT=wa[:, c, j * 128:(j + 1) * 128],
                                                  rhs=csil[:, c, :], start=(c == 0), stop=(c == KC - 1)),
                         reads=[bwa[c], b_mod], writes=[pb])
            for gi in range(3):
                for hf in range(2):
                    ps2, pb2 = next_ps()
                    col0 = (3 * gi + 2) * D + hf * 512
                    for c in range(KC):
                        k.op("pe", lambda e: e.matmul(ps2[0:NSEQ, :], lhsT=csil[:, c, :], rhs=wa[:, c, col0:col0 + 512],
                                                      start=(c == 0), stop=(c == KC - 1)),
                             reads=[bwa[c], b_mod], writes=[pb2])
                    sl = slice(gi * D + hf * 512, gi * D + (hf + 1) * 512)
                    k.op("dve", lambda e: e.tensor_tensor(out=modrows[:, sl], in0=ps2[0:NSEQ, :], in1=badG[:, sl], op=ALU.add),
                         reads=[pb2, b_t], writes=[b_mr])
            k.op("dve", lambda e: e.tensor_tensor(out=modT[:], in0=ps[:, 0:72 * NSEQ].rearrange("p (j s) -> p j s", s=NSEQ),
                                                  in1=badT[:].unsqueeze(2).to_broadcast([128, 72, NSEQ]), op=ALU.add),
                 reads=[pb, b_t], writes=[b_mod])
            k.dma("sp", gsc, modrows[:], reads=[b_mr], writes=[b_gsc])
            for i in range(3):
                k.dma("sp", gn_sb[:, i, :], gnT[i][l], writes=[b_lay])
            k.dma("sp", cbw[:], cbwT[l], writes=[b_lay]); k.dma("sp", cbb[:], cbbT[l], writes=[b_lay])
            k.dma("sp", ccw[:], ccwT[l], writes=[b_lay]); k.dma("sp", ccb[:], ccbT[l], writes=[b_lay])
            k.dma("sp", dtb_sb[:], dtb_bc[l], writes=[b_lay]); k.dma("sp", aneg_sb[:], alog_bc[l], writes=[b_lay])
            k.dma("sp", dsk_sb[:], dsk_bc[l], writes=[b_lay])
            k.dma("sp", br_sb[:], brT[l], writes=[b_lay]); k.dma("sp", bi_sb[:], biT[l], writes=[b_lay])
            k.dma("sp", lam_t[:], lamT[l], writes=[b_lay])
            k.op("act", lambda e: e.activation(out=aneg_sb[:], in_=aneg_sb[:], func=AF.Exp), reads=[b_lay], writes=[b_lay])
            k.op("dve", lambda e: e.tensor_scalar(out=aneg_sb[:], in0=aneg_sb[:], scalar1=-1.0, scalar2=None, op0=ALU.mult),
                 reads=[b_lay], writes=[b_lay])
            k.op("act", lambda e: e.activation(out=lam_t[:], in_=lam_t[:], func=AF.Exp, scale=-1.0), reads=[b_lay], writes=[b_lay])
            k.op("act", lambda e: e.activation(out=lam_t[:], in_=lam_t[:], func=AF.Ln, bias=oneb[:], scale=1.0),
                 reads=[b_lay, b_const], writes=[b_lay])
            k.op("dve", lambda e: e.tensor_scalar(out=cneg_sb[:], in0=lam_t[:], scalar1=-8.0, scalar2=None, op0=ALU.mult),
                 reads=[b_lay], writes=[b_lay])
            for i in range(3):
                sc = modT[:, (3 * i + 1) * 8:(3 * i + 2) * 8, :]
                sh = modT[:, (3 * i) * 8:(3 * i + 1) * 8, :]
                k.op("dve", lambda e: e.tensor_scalar(out=modA[:, i], in0=sc, scalar1=1.0, scalar2=None, op0=ALU.add),
                     reads=[b_mod], writes=[b_AB])
                k.op("dve", lambda e: e.tensor_tensor(out=modA[:, i], in0=modA[:, i],
                                                      in1=gn_sb[:, i, :].unsqueeze(2).to_broadcast([128, KC, NSEQ]), op=ALU.mult),
                     reads=[b_AB, b_lay], writes=[b_AB])
                k.op("dve", lambda e: e.tensor_copy(out=modB[:, i], in_=sh), reads=[b_mod], writes=[b_AB])
            k.barrier()

    def grp_info(g):
        if g < NPS:
            return dict(P=128, T=SEQ, seqs=[g], L=SEQ, row0=g * SEQ)
        return dict(P=ST, T=ST, seqs=list(range(NPS, NSEQ)), L=DL, row0=NPS * SEQ)

    def setup_group(g, l):
        gi = grp_info(g)
        nseg = len(gi["seqs"])
        seg = gi["P"] // nseg
        for si, s_ in enumerate(gi["seqs"]):
            k.dma("sp", gbc[si * seg:(si + 1) * seg].rearrange("p a d -> p (a d)"), gsc[s_].partition_broadcast(seg),
                  reads=[b_gsc], writes=[b_gbc])

    def setup_AB(g, i, Afull, Bfull, b_ab):
        gi = grp_info(g)
        P = gi["P"]
        nseg = len(gi["seqs"])
        seg = P // nseg
        for si, s in enumerate(gi["seqs"]):
            k.op("dve", lambda e: e.tensor_copy(out=Afull[:, :, si * seg:(si + 1) * seg],
                                                in_=modA[:, i, :, s:s + 1].to_broadcast([128, KC, seg])),
                 reads=[b_AB], writes=[b_ab])
            k.op("dve", lambda e: e.tensor_copy(out=Bfull[:, :, si * seg:(si + 1) * seg],
                                                in_=modB[:, i, :, s:s + 1].to_broadcast([128, KC, seg])),
                 reads=[b_AB], writes=[b_ab])

    def norm_alloc(es):
        return dict(
            Afull=TMP(es, "Afull", [128, KC, 128], F32), Bfull=TMP(es, "Bfull", [128, KC, 128], F32), b_ab=Buf(),
            ss=TMP(es, "n_ss", [128, 16], F32), junk=TMP(es, "n_junk", [128, D], F32),
            xn=[TMP(es, f"n_xn{i}", [128, D], BF16) for i in range(2)],
            tmp=[TMP(es, f"n_tmp{i}", [128, KC, 128], F32) for i in range(2)],
            bss=Buf(), bj=Buf(), bxn=[Buf(), Buf()], btmp=[Buf(), Buf()])

    def norm_to_hT(g, sub_i, xt, bx, P, nblk, hT, bh, tok0, es, ctx=None):
        if ctx is None:
            ctx = norm_alloc(es)
        Afull, Bfull, b_ab = ctx["Afull"], ctx["Bfull"], ctx["b_ab"]
        setup_AB(g, sub_i, Afull, Bfull, b_ab)
        ss, junk, xn, tmp = ctx["ss"], ctx["junk"], ctx["xn"], ctx["tmp"]
        bss, bj, bxn, btmp = ctx["bss"], ctx["bj"], ctx["bxn"], ctx["btmp"]
        for b in range(nblk):
            k.op("act", lambda e: e.activation(out=junk[0:P, :], in_=xt[0:P, b, :], func=AF.Square, accum_out=ss[0:P, b:b + 1]),
                 reads=[bx], writes=[bj, bss])
        k.op("act", lambda e: e.activation(out=ss[0:P, 0:nblk], in_=ss[0:P, 0:nblk], func=AF.Sqrt, bias=epsb[0:P, :], scale=1.0 / D),
             reads=[bss, b_const], writes=[bss])
        k.op("dve", lambda e: e.reciprocal(out=ss[0:P, 0:nblk], in_=ss[0:P, 0:nblk]), reads=[bss], writes=[bss])
        for b in range(nblk):
            x_ = xn[b % 2]; bx_ = bxn[b % 2]; t_ = tmp[b % 2]; bt_ = btmp[b % 2]
            k.op("act", lambda e: e.activation(out=x_[0:P, :], in_=xt[0:P, b, :], func=AF.Identity, scale=ss[0:P, b:b + 1]),
                 reads=[bx, bss], writes=[bx_])
            ps, pb = next_ps()
            pv = bf(ps)
            for c in range(KC):
                k.op("pe", lambda e: e.transpose(out=pv[:, c * 128:c * 128 + P], in_=x_[0:P, c * 128:(c + 1) * 128],
                                                 identity=ident[0:P, 0:P]), reads=[bx_, b_const], writes=[pb])
            pvv = pv[:, 0:KC * 128].rearrange("p (c t) -> p c t", t=128)[:, :, 0:P]
            k.op("dve", lambda e: e.tensor_tensor(out=t_[:, :, 0:P], in0=pvv, in1=Afull[:, :, 0:P], op=ALU.mult),
                 reads=[pb, b_ab], writes=[bt_])
            k.op("dve", lambda e: e.tensor_tensor(out=hT[:, :, tok0 + b * P: tok0 + (b + 1) * P], in0=t_[:, :, 0:P],
                                                  in1=Bfull[:, :, 0:P], op=ALU.add), reads=[bt_, b_ab], writes=[bh])

    def resid_update(xt_slice, bx, ps_ap, pb, P, gi_idx, col0, ncol, scale, tmp, btmp):
        k.op("dve", lambda e: e.scalar_tensor_tensor(out=tmp[0:P, 0:ncol], in0=ps_ap, scalar=scale,
                                                     in1=gbc[0:P, gi_idx, col0:col0 + ncol], op0=ALU.mult, op1=ALU.mult),
             reads=[pb, b_gbc], writes=[btmp])
        k.op("dve", lambda e: e.tensor_tensor(out=xt_slice, in0=xt_slice, in1=tmp[0:P, 0:ncol], op=ALU.add),
             reads=[btmp, bx], writes=[bx])

    def ff(l, which, g, src_rows, first):
        gi = grp_info(g)
        P, T = gi["P"], gi["T"]
        TT = min(T, 1024)
        ntile = T // TT
        NB = TT // P
        SUB = min(TT, 512)
        NS = TT // SUB
        sub_i = 0 if which == 0 else 2
        wi = w_ffi[which][l].rearrange("(c p) n -> p c n", p=128)
        wo = w_ffo[which][l].rearrange("(j p) n -> p j n", p=128)
        with ExitStack() as es:
            xt = TMP(es, "f_x", [128, NB, D], F32)
            hT = TMP(es, "f_hT", [128, KC, TT], BF16)
            actT = TMP(es, "f_act", [128, FC, TT], BF16)
            bx, bh, bact = Buf(), Buf(), Buf()
            bw = [(Buf(), Buf()) for _ in range(3)]; bwo = [Buf(), Buf()]; bsu = [Buf(), Buf()]; brt = [Buf(), Buf()]
            wblk = [TMP(es, f"f_w{i}", [128, 2, KC, 128], BF16) for i in range(3)]
            wob = [TMP(es, f"f_wo{i}", [128, FC, 256], BF16) for i in range(2)]
            su = [TMP(es, f"f_su{i}", [128, 512], F32) for i in range(2)]
            rt = [TMP(es, f"f_rt{i}", [128, 256], F32) for i in range(2)]
            nctx = norm_alloc(es)
            for ti in range(ntile):
                xb = xres_b[g][ti]
                r0 = gi["row0"] + ti * TT
                if first:
                    src = src_rows[ti * TT:(ti + 1) * TT, :]
                    k.dma("pool", xt[0:P], src.rearrange("(b p) d -> p b d", p=P), writes=[bx])
                else:
                    k.dma("pool", xt[0:P], xres[r0:r0 + TT, :].rearrange("(b p) d -> p b d", p=P), reads=[xb], writes=[bx])
                norm_to_hT(g, sub_i, xt, bx, P, NB, hT, bh, 0, None, nctx)
                cnt = 0
                for j in range(FC):
                    w = wblk[j % 3]; bw_ = bw[j % 3]
                    wload(w[:, 0], S_ffi[l][which][j], bw_[0])
                    wload(w[:, 1], S_ffi[l][which][FC + j], bw_[1])
                    for s in range(NS):
                        psu, pbu = next_ps()
                        psv, pbv = next_ps()
                        tsl = slice(s * SUB, (s + 1) * SUB)
                        for c in range(KC):
                            k.op("pe", lambda e: e.matmul(psu[:, 0:SUB], lhsT=w[:, 0, c, :], rhs=hT[:, c, tsl],
                                                          start=(c == 0), stop=(c == KC - 1)), reads=[bw_[0], bh], writes=[pbu])
                        for c in range(KC):
                            k.op("pe", lambda e: e.matmul(psv[:, 0:SUB], lhsT=w[:, 1, c, :], rhs=hT[:, c, tsl],
                                                          start=(c == 0), stop=(c == KC - 1)), reads=[bw_[1], bh], writes=[pbv])
                        s_ = su[cnt % 2]; bs_ = bsu[cnt % 2]; cnt += 1
                        k.op("act", lambda e: e.activation(out=s_[:, 0:SUB], in_=psu[:, 0:SUB], func=AF.Silu),
                             reads=[pbu], writes=[bs_])
                        k.op("dve", lambda e: e.tensor_tensor(out=actT[:, j, tsl], in0=s_[:, 0:SUB], in1=psv[:, 0:SUB], op=ALU.mult),
                             reads=[bs_, pbv], writes=[bact])
                cnt = 0
                for q in range(4):
                    w = wob[q % 2]; bw_ = bwo[q % 2]
                    wload(w[:], S_ffo[l][which][q], bw_)
                    for b in range(NB):
                        ps, pb = next_ps()
                        for j in range(FC):
                            k.op("pe", lambda e: e.matmul(ps[0:P, 0:256], lhsT=actT[:, j, b * P:(b + 1) * P], rhs=w[:, j, :],
                                                          start=(j == 0), stop=(j == FC - 1)), reads=[bact, bw_], writes=[pb])
                        resid_update(xt[0:P, b, q * 256:(q + 1) * 256], bx, ps[0:P, 0:256], pb, P, sub_i, q * 256, 256, 0.5,
                                     rt[cnt % 2], brt[cnt % 2])
                        cnt += 1
                k.dma("pool", xres[r0:r0 + TT, :].rearrange("(b p) d -> p b d", p=P), xt[0:P], reads=[bx], writes=[xb])
            k.barrier()

    def final_norm(g, dst_rows):
        gi = grp_info(g)
        P, T = gi["P"], gi["T"]
        TT = min(T, 1024)
        NB = TT // P
        for ti in range(T // TT):
            with ExitStack() as es:
                xt = TMP(es, "fn_x", [128, NB, D], F32)
                ss = TMP(es, "fn_ss", [128, 16], F32)
                junk = TMP(es, "fn_junk", [128, D], F32)
                gfin_sb = TMP(es, "gfin_sb", [128, D], F32)
                k.dma("sp", gfin_sb[:], gfin, writes=[b_lay])
                bx, bss, bj = Buf(), Buf(), Buf()
                r0 = gi["row0"] + ti * TT
                k.dma("sp", xt[0:P], xres[r0:r0 + TT, :].rearrange("(b p) d -> p b d", p=P), reads=[xres_b[g][ti]], writes=[bx])
                for b in range(NB):
                    k.op("act", lambda e: e.activation(out=junk[0:P, :], in_=xt[0:P, b, :], func=AF.Square, accum_out=ss[0:P, b:b + 1]),
                         reads=[bx], writes=[bj, bss])
                k.op("act", lambda e: e.activation(out=ss[0:P, 0:NB], in_=ss[0:P, 0:NB], func=AF.Sqrt, bias=epsb[0:P, :], scale=1.0 / D),
                     reads=[bss, b_const], writes=[bss])
                k.op("dve", lambda e: e.reciprocal(out=ss[0:P, 0:NB], in_=ss[0:P, 0:NB]), reads=[bss], writes=[bss])
                for b in range(NB):
                    k.op("dve", lambda e: e.scalar_tensor_tensor(out=xt[0:P, b, :], in0=xt[0:P, b, :], scalar=ss[0:P, b:b + 1],
                                                                 in1=gfin_sb[0:P, :], op0=ALU.mult, op1=ALU.mult),
                         reads=[bx, bss, b_lay], writes=[bx])
                k.dma("sp", dst_rows[ti * TT:(ti + 1) * TT, :].rearrange("(b p) d -> p b d", p=P), xt[0:P], reads=[bx])
                k.barrier()


    def pg_alloc(es, nck):
        return dict(
            wps=[TMP(es, f"pg_wp{i}", [128, nck, 128], BF16) for i in range(2)],
            wgs=[TMP(es, f"pg_wg{i}", [128, KC, 128], BF16) for i in range(2)],
            sg=[TMP(es, f"pg_sg{i}", [128, 512], F32) for i in range(2)],
            bwp=[Buf(), Buf()], bwg=[Buf(), Buf()], bsg=[Buf(), Buf()])

    def proj_gate_merge(l, bi, actf, nck, wproj, hT, bact, mT, bm, tok0, ntok, first, pg, bh=None):
        wps, wgs, sg, bwp, bwg, bsg = pg["wps"], pg["wgs"], pg["sg"], pg["bwp"], pg["bwg"], pg["bsg"]
        tsl = slice(tok0, tok0 + ntok)
        for o in range(8):
            wp, wg = wps[o % 2], wgs[o % 2]
            wload(wp[:], wproj[o], bwp[o % 2])
            wload(wg[:], inB(l, O_GATES + bi * D + o * 128), bwg[o % 2])
            psy, pby = next_ps()
            psg, pbg = next_ps()
            for c in range(nck):
                k.op("pe", lambda e: e.matmul(psy[:, 0:ntok], lhsT=wp[:, c, :], rhs=actf(c), start=(c == 0), stop=(c == nck - 1)),
                     reads=[bwp[o % 2], bact], writes=[pby])
            for c in range(KC):
                k.op("pe", lambda e: e.matmul(psg[:, 0:ntok], lhsT=wg[:, c, :], rhs=hT[:, c, tsl], start=(c == 0), stop=(c == KC - 1)),
                     reads=[bwg[o % 2]] + ([bh] if bh is not None else []), writes=[pbg])
            s_, bs_ = sg[o % 2], bsg[o % 2]
            k.op("act", lambda e: e.activation(out=s_[:, 0:ntok], in_=psg[:, 0:ntok], func=AF.Sigmoid), reads=[pbg], writes=[bs_])
            if first:
                k.op("dve", lambda e: e.tensor_tensor(out=mT[:, o, tsl], in0=s_[:, 0:ntok], in1=psy[:, 0:ntok], op=ALU.mult),
                     reads=[bs_, pby], writes=[bm])
            else:
                k.op("dve", lambda e: e.tensor_tensor(out=s_[:, 0:ntok], in0=s_[:, 0:ntok], in1=psy[:, 0:ntok], op=ALU.mult),
                     reads=[bs_, pby], writes=[bs_])
                k.op("pool", lambda e: e.tensor_tensor(out=mT[:, o, tsl], in0=mT[:, o, tsl], in1=s_[:, 0:ntok], op=ALU.add),
                     reads=[bs_, bm], writes=[bm])

    def attn_prompt(l, g, hT, bh, mT, bm):
        T = SEQ
        wi_l = w_in[l].rearrange("(c p) n -> p c n", p=128)
        with ExitStack() as es:
            oaT = TMP(es, "a_oaT", [128, 4, T], BF16); boa = Buf()
            shift = TMP(es, "a_shift", [64, 128], BF16); bsh = Buf()
            accT = TMP(es, "a_acc", [64, 2, T], F32); bacc = Buf()
            accZ = TMP(es, "a_accz", [1, 2, T], F32); baz = Buf()
            QT = [TMP(es, f"a_q{i}", [128, T], BF16) for i in range(2)]; bq = [Buf(), Buf()]
            KTt = [TMP(es, f"a_k{i}", [128, T], BF16) for i in range(2)]; bk = [Buf(), Buf()]
            Vt = [TMP(es, f"a_v{i}", [128, 16, 128], BF16) for i in range(2)]; bv = [Buf(), Buf()]
            wq = [TMP(es, f"a_wq{i}", [128, KC, 128], BF16) for i in range(2)]; bwq = [Buf(), Buf()]
            wk = [TMP(es, f"a_wk{i}", [128, KC, 128], BF16) for i in range(2)]; bwk = [Buf(), Buf()]
            wkv = [TMP(es, f"a_wkv{i}", [128, KC, 256], BF16) for i in range(2)]; bwkv = [Buf(), Buf()]
            PT = [TMP(es, f"a_pt{i}", [128, 256], BF16) for i in range(4)]; bpt = [Buf() for _ in range(4)]
            ex = [TMP(es, f"a_ex{i}", [128, 256], F32) for i in range(2)]; bex = [Buf(), Buf()]
            kst = [TMP(es, f"a_kst{i}", [128, 256], F32) for i in range(2)]; bkst = [Buf(), Buf()]
            oan = TMP(es, "a_oan", [64, 2, 512], BF16); boan = Buf()
            rz = TMP(es, "a_rz", [1, 2, 512], F32); brz = Buf()
            k.op("dve", lambda e: e.memset(shift[:], 0.0), writes=[bsh])
            k.op("dve", lambda e: e.tensor_copy(out=shift[:, 64:128], in_=ident[0:64, 0:64]), reads=[b_const], writes=[bsh])
            cnt = 0
            kcnt = 0
            it = 0
            for hp in range(A_HP):
                for gq, (win, d) in enumerate(GROUPS):
                    if gq not in A_GQS:
                        continue
                    i2 = it % 2
                    it += 1
                    m = T // d
                    nb = m // 128
                    keep = min(win, T)
                    cq = O_Q + gq * 512 + hp * 128
                    ck = O_K + gq * 512 + hp * 128
                    cv = O_V + gq * 512 + hp * 128
                    wload(wq[i2][:], inA(l, cq), bwq[i2])
                    wload(wk[i2][:], inA(l, ck), bwk[i2])
                    wload(wkv[i2][:, :, 0:128], inA(l, ck), bwkv[i2])
                    wload(wkv[i2][:, :, 128:256], inA(l, cv), bwkv[i2])
                    for s in range(4):
                        sub = slice(s * 512, (s + 1) * 512)
                        for (wt, bw_, dst, bd) in ((wq[i2], bwq[i2], QT[i2], bq[i2]), (wk[i2], bwk[i2], KTt[i2], bk[i2])):
                            ps, pb = next_ps()
                            for c in range(KC):
                                k.op("pe", lambda e: e.matmul(ps[:, :], lhsT=wt[:, c, :], rhs=hT[:, c, sub], start=(c == 0), stop=(c == KC - 1)),
                                     reads=[bw_, bh], writes=[pb])
                            k.op("act", lambda e: e.activation(out=dst[:, sub], in_=ps[:, :], func=AF.Copy), reads=[pb], writes=[bd])
                    for r in range(d if A_STAGE >= 2 else 0):
                        for kb in range(nb):
                            blk = r * nb + kb
                            t0 = r + kb * 128 * d
                            tsl = slice(t0, t0 + 127 * d + 1, d)
                            need_k = (kb * 128 * d >= T - keep)
                            c0 = 0 if need_k else 128
                            ps, pb = next_ps()
                            for c in range(KC):
                                k.op("pe", lambda e: e.matmul(ps[:, c0:256], lhsT=hT[:, c, tsl], rhs=wkv[i2][:, c, c0:256],
                                                              start=(c == 0), stop=(c == KC - 1)), reads=[bwkv[i2], bh], writes=[pb])
                            k.op("act", lambda e: e.activation(out=Vt[i2][:, blk, :], in_=ps[:, 128:256], func=AF.Copy),
                                 reads=[pb], writes=[bv[i2]])
                            if need_k and not int(os.environ.get("MK_NOPKV", "0")):
                                ks_, bks_ = kst[kcnt % 2], bkst[kcnt % 2]
                                kcnt += 1
                                k.op("act", lambda e: e.activation(out=ks_[:], in_=ps[:, 0:256], func=AF.Copy), reads=[pb], writes=[bks_])
                                row0 = t0 - (T - keep)
                                i0 = (row0 - r) // d
                                dst = pkv[gq][l, g].rearrange("(i dd) (a x) -> dd i a x", dd=d, a=2)[r, i0:i0 + 128, :, hp * 128:(hp + 1) * 128]
                                dbg = int(os.environ.get("MK_DBG", "0"))
                                if dbg == 1:
                                    pass
                                elif dbg == 2:
                                    dst2 = pkv[gq][l, g, 0:128, :]
                                    k.dma("sp", dst2[:, hp * 128:(hp + 1) * 128], ks_[:, 0:128], reads=[bks_])
                                    k.dma("sp", dst2[:, 512 + hp * 128:512 + (hp + 1) * 128], ks_[:, 128:256], reads=[bks_])
                                else:
                                    k.dma("sp", dst[:, 0, :], ks_[:, 0:128], reads=[bks_])
                                    k.dma("sp", dst[:, 1, :], ks_[:, 128:256], reads=[bks_])
                    for h2 in range(2 if A_STAGE >= 3 else 0):
                        hg = gq * 8 + hp * 2 + h2
                        psl = slice(h2 * 64, (h2 + 1) * 64)
                        for r in range(d):
                            ptprev = None
                            for kb in range(nb):
                                nq = 256 if kb < nb - 1 else 128
                                t0 = r + kb * 128 * d
                                ksl = slice(t0, t0 + 127 * d + 1, d)
                                qsl = slice(t0, t0 + (nq - 1) * d + 1, d)
                                ps, pb = next_ps()
                                k.op("pe", lambda e: e.matmul(ps[:, 0:nq], lhsT=KTt[i2][psl, ksl], rhs=QT[i2][psl, qsl], start=True, stop=True),
                                     reads=[bk[i2], bq[i2]], writes=[pb])
                                ex_, bex_ = ex[cnt % 2], bex[cnt % 2]
                                pt, bpt_ = PT[cnt % 4], bpt[cnt % 4]
                                cnt += 1
                                k.op("act", lambda e: e.activation(out=ex_[:, 0:nq], in_=ps[:, 0:nq], func=AF.Exp, scale=0.125),
                                     reads=[pb], writes=[bex_])
                                k.op("dve", lambda e: e.tensor_tensor(out=pt[:, 0:nq], in0=ex_[:, 0:nq], in1=E[:, hg, 0:nq], op=ALU.mult),
                                     reads=[bex_, b_E], writes=[bpt_])
                                ps2, pb2 = next_ps()
                                ps3, pb3 = next_ps()
                                if kb > 0:
                                    pp, bpp = ptprev
                                    k.op("pe", lambda e: e.matmul(ps2[0:64, 0:128], lhsT=Vt[i2][:, r * nb + kb - 1, h2 * 64:(h2 + 1) * 64],
                                                                  rhs=pp[:, 128:256], start=True, stop=False), reads=[bv[i2], bpp], writes=[pb2])
                                    k.op("pe", lambda e: e.matmul(ps3[0:1, 0:128], lhsT=ones_b[:, 0:1], rhs=pp[:, 128:256], start=True, stop=False),
                                         reads=[b_const, bpp], writes=[pb3])
                                k.op("pe", lambda e: e.matmul(ps2[0:64, 0:128], lhsT=Vt[i2][:, r * nb + kb, h2 * 64:(h2 + 1) * 64],
                                                              rhs=pt[:, 0:128], start=(kb == 0), stop=True), reads=[bv[i2], bpt_], writes=[pb2])
                                k.op("pe", lambda e: e.matmul(ps3[0:1, 0:128], lhsT=ones_b[:, 0:1], rhs=pt[:, 0:128], start=(kb == 0), stop=True),
                                     reads=[b_const, bpt_], writes=[pb3])
                                adst = accT[0:64, h2, ksl]
                                zdst = accZ[0:1, h2, ksl]
                                if gq == 0:
                                    k.op("act", lambda e: e.activation(out=adst, in_=ps2[0:64, 0:128], func=AF.Copy), reads=[pb2], writes=[bacc])
                                    k.op("act", lambda e: e.activation(out=zdst, in_=ps3[0:1, 0:128], func=AF.Copy), reads=[pb3], writes=[baz])
                                else:
                                    k.op("dve", lambda e: e.tensor_tensor(out=adst, in0=adst, in1=ps2[0:64, 0:128], op=ALU.add),
                                         reads=[pb2, bacc], writes=[bacc])
                                    k.op("dve", lambda e: e.tensor_tensor(out=zdst, in0=zdst, in1=ps3[0:1, 0:128], op=ALU.add),
                                         reads=[pb3, baz], writes=[baz])
                                ptprev = (pt, bpt_)
                for s in range(4 if A_STAGE >= 4 else 0):
                    sub = slice(s * 512, (s + 1) * 512)
                    k.op("dve", lambda e: e.reciprocal(out=rz[0:1, :, :], in_=accZ[0:1, :, sub]), reads=[baz], writes=[brz])
                    for h2 in range(2):
                        ps, pb = next_ps()
                        k.op("pe", lambda e: e.matmul(ps[0:64, :], lhsT=ones_f[0:1, 0:64], rhs=rz[0:1, h2, :], start=True, stop=True),
                             reads=[brz, b_const], writes=[pb])
                        k.op("dve", lambda e: e.tensor_tensor(out=oan[:, h2, :], in0=accT[0:64, h2, sub], in1=ps[0:64, :], op=ALU.mult),
                             reads=[pb, bacc], writes=[boan])
                    ps, pb = next_ps()
                    k.op("pe", lambda e: e.matmul(ps[:, :], lhsT=ident[0:64, :], rhs=oan[:, 0, :], start=True, stop=False),
                         reads=[boan, b_const], writes=[pb])
                    k.op("pe", lambda e: e.matmul(ps[:, :], lhsT=shift[:, :], rhs=oan[:, 1, :], start=False, stop=True),
                         reads=[boan, bsh], writes=[pb])
                    k.op("act", lambda e: e.activation(out=oaT[:, hp, sub], in_=ps[:, :], func=AF.Copy), reads=[pb], writes=[boa])
            k.barrier()
            with ExitStack() as es2:
                pg = pg_alloc(es2, 4)
                for s in range(4):
                    proj_gate_merge(l, 0, lambda c: oaT[:, c, s * 512:(s + 1) * 512], 4, S_ap[l], hT, boa, mT, bm, s * 512, 512, True, pg, bh=bh)
                k.barrier()


    def ssd(l, g, s, tok0, L, hT, bh, mT, bm):
        prompt = g < NPS
        SBT = min(L, 256)
        CH = min(L, 128)
        NCHK = SBT // CH
        wi_l = w_in[l].rearrange("(c p) n -> p c n", p=128)
        with ExitStack() as es:
            xpad = TMP(es, "s_xpad", [128, 12, SBT + 3], F32); bxp = [Buf() for _ in range(12)]
            xa = TMP(es, "s_xa", [128, 12, SBT], BF16); bxa = Buf()
            szT = TMP(es, "s_szT", [128, 8, SBT], BF16); bsz = Buf()
            ynT = TMP(es, "s_ynT", [128, 8, SBT], BF16); byn = Buf()
            S = TMP(es, "s_S", [128, 1024], F32); bS = Buf()
            Sb = TMP(es, "s_Sb", [128, 1024], BF16); bSb = Buf()
            wx = [TMP(es, f"s_wx{i}", [128, KC, 128], BF16) for i in range(2)]; bwx = [Buf(), Buf()]
            wdt = TMP(es, "s_wdt", [128, KC, 16], BF16); bwdt = Buf()
            gss = TMP(es, "s_gss", [128, D], F32); bgss = Buf()
            X = TMP(es, "s_X", [128, 16, CH], F32); bX = Buf()
            ea = TMP(es, "s_ea", [128, 16, CH], BF16); bea = Buf()
            dec = TMP(es, "s_dec", [128, 16, CH], BF16); bdec = Buf()
            MT = TMP(es, "s_MT", [128, 16, CH], BF16); bMT = Buf()
            Cs = TMP(es, "s_Cs", [128, 16, CH], BF16); bCs = Buf()
            xsD = TMP(es, "s_xsD", [128, 1024], F32); bxsD = Buf()
            xdt = TMP(es, "s_xdt", [128, 1024], BF16); bxdt = Buf()
            xdtE = TMP(es, "s_xdtE", [128, 1024], BF16); bxdtE = Buf()
            Btok = TMP(es, "s_Btok", [128, 2, 128], BF16); bBt = Buf()
            cbs = TMP(es, "s_cbs", [128, 2, CH], BF16); bcbs = Buf()
            y1 = TMP(es, "s_y1", [128, 1024], F32); by1 = Buf()
            yn = TMP(es, "s_yn", [128, 1024], BF16); byn2 = Buf()
            junk = TMP(es, "s_junk", [128, 512], F32); bjk = Buf()
            cvt = [TMP(es, f"s_cv{i}", [128, SBT], F32) for i in range(2)]; bcv = [Buf(), Buf()]
            sm = TMP(es, "s_sm", [128, 8, 16], F32); bsm = Buf(); b_dt, b_dta, b_at, b_al, b_toe, b_cd, b_tm, b_ssq = [Buf() for _ in range(8)]
            stin = TMP(es, "s_stin", [128, 8, 128], F32); bstin = Buf()
            pg = pg_alloc(es, 8)
            k.dma("sp", gss[:], gssm_bc[l], writes=[bgss])
            wload(wdt[:], S_dt[l], bwdt)
            if prompt:
                k.op("pool", lambda e: e.memset(xpad[:, :, 0:3], 0.0), writes=bxp)
                k.op("pool", lambda e: e.memset(S[:], 0.0), writes=[bS])
                k.op("pool", lambda e: e.memset(Sb[:], 0.0), writes=[bSb])
            else:
                si = s - NPS
                k.dma("sp", xpad[:, :, 0:3], st_cb[l, si], writes=bxp)
                k.dma("sp", stin[:], st_ssm[l, si].rearrange("(a p) n -> p a n", p=128), writes=[bstin])
                for a in range(8):
                    ps, pb = next_ps()
                    k.op("pe", lambda e: e.transpose(out=ps[:, 0:128], in_=stin[:, a, :], identity=identf[:]), reads=[bstin, b_const], writes=[pb])
                    k.op("act", lambda e: e.activation(out=S[:, a * 128:(a + 1) * 128], in_=ps[:, 0:128], func=AF.Copy), reads=[pb], writes=[bS])
                k.op("act", lambda e: e.activation(out=Sb[:], in_=S[:], func=AF.Copy), reads=[bS], writes=[bSb])
            cvc = 0
            wc = 0
            for st in range(L // SBT):
                ts0 = tok0 + st * SBT
                tsub = slice(ts0, ts0 + SBT)
                for fc in range(8):
                    w, bw_ = wx[wc % 2], bwx[wc % 2]; wc += 1
                    wload(w[:], inA(l, O_Z + fc * 128), bw_)
                    ps, pb = next_ps()
                    for c in range(KC):
                        k.op("pe", lambda e: e.matmul(ps[:, 0:SBT], lhsT=w[:, c, :], rhs=hT[:, c, tsub], start=(c == 0), stop=(c == KC - 1)),
                             reads=[bw_, bh], writes=[pb])
                    k.op("act", lambda e: e.activation(out=szT[:, fc, :], in_=ps[:, 0:SBT], func=AF.Silu), reads=[pb], writes=[bsz])
                for fc in range(12):
                    w, bw_ = wx[wc % 2], bwx[wc % 2]; wc += 1
                    wload(w[:], inA(l, O_XBC + fc * 128), bw_)
                    ps, pb = next_ps()
                    for c in range(KC):
                        k.op("pe", lambda e: e.matmul(ps[:, 0:SBT], lhsT=w[:, c, :], rhs=hT[:, c, tsub], start=(c == 0), stop=(c == KC - 1)),
                             reads=[bw_, bh], writes=[pb])
                    k.op("act", lambda e: e.activation(out=xpad[:, fc, 3:3 + SBT], in_=ps[:, 0:SBT], func=AF.Copy), reads=[pb], writes=[bxp[fc]])
                    cv, bcv_ = cvt[cvc % 2], bcv[cvc % 2]; cvc += 1
                    eng = "dve"
                    k.op(eng, lambda e: e.tensor_scalar(out=cv[:], in0=xpad[:, fc, 3:3 + SBT], scalar1=cbw[:, fc, 3:4], scalar2=cbb[:, fc:fc + 1],
                                                        op0=ALU.mult, op1=ALU.add), reads=[bxp[fc], b_lay], writes=[bcv_])
                    for kk in (2, 1, 0):
                        k.op(eng, lambda e: e.scalar_tensor_tensor(out=cv[:], in0=xpad[:, fc, kk:kk + SBT], scalar=cbw[:, fc, kk:kk + 1], in1=cv[:],
                                                                   op0=ALU.mult, op1=ALU.add), reads=[bxp[fc], b_lay, bcv_], writes=[bcv_])
                    k.op("act", lambda e: e.activation(out=xa[:, fc, :], in_=cv[:], func=AF.Silu), reads=[bcv_], writes=[bxa])
                    k.op("pool", lambda e: e.tensor_copy(out=xpad[:, fc, 0:3], in_=xpad[:, fc, SBT:SBT + 3]), reads=[bxp[fc]], writes=[bxp[fc]])
                for ch in range(NCHK):
                    o = ch * CH
                    csl = slice(ts0 + o, ts0 + o + CH)
                    osl = slice(o, o + CH)
                    dt_, dta, at, al, toe, cd, tm, ssq = [sm[:, i, :] for i in range(8)]
                    ps, pb = next_ps()
                    for c in range(KC):
                        k.op("pe", lambda e: e.matmul(ps[0:CH, 0:16], lhsT=hT[:, c, csl], rhs=wdt[:, c, :], start=(c == 0), stop=(c == KC - 1)),
                             reads=[bwdt, bh], writes=[pb])
                    k.op("dve", lambda e: e.tensor_tensor(out=dt_[0:CH], in0=ps[0:CH, 0:16], in1=dtb_sb[0:CH, :], op=ALU.add), reads=[pb, b_lay], writes=[b_dt])
                    k.op("act", lambda e: e.activation(out=dt_[0:CH], in_=dt_[0:CH], func=AF.Exp), reads=[b_dt], writes=[b_dt])
                    k.op("act", lambda e: e.activation(out=dt_[0:CH], in_=dt_[0:CH], func=AF.Ln, bias=oneb[0:CH, :], scale=1.0), reads=[b_dt, b_const], writes=[b_dt])
                    k.op("dve", lambda e: e.tensor_tensor(out=dta[0:CH], in0=dt_[0:CH], in1=aneg_sb[0:CH, :], op=ALU.mult), reads=[b_dt, b_lay], writes=[b_dta])
                    ps, pb = next_ps()
                    pv = bf(ps)
                    for fc in range(8):
                        k.op("pe", lambda e: e.transpose(out=pv[0:CH, fc * 128:(fc + 1) * 128], in_=xa[:, fc, osl], identity=ident[:]),
                             reads=[bxa, b_const], writes=[pb])
                    pv3 = pv[0:CH, :].rearrange("p (h e) -> p h e", e=64)
                    k.op("dve", lambda e: e.tensor_tensor(out=xsD[0:CH, :].rearrange("p (h e) -> p h e", e=64), in0=pv3,
                                                          in1=dsk_sb[0:CH, :].unsqueeze(2).to_broadcast([CH, 16, 64]), op=ALU.mult),
                         reads=[pb, b_lay], writes=[bxsD])
                    k.op("dve", lambda e: e.tensor_tensor(out=xdt[0:CH, :].rearrange("p (h e) -> p h e", e=64), in0=pv3,
                                                          in1=dt_[0:CH, :].unsqueeze(2).to_broadcast([CH, 16, 64]), op=ALU.mult),
                         reads=[pb, b_dt], writes=[bxdt])
                    ps, pb = next_ps()
                    pv = bf(ps)
                    for gg in range(2):
                        k.op("pe", lambda e: e.transpose(out=pv[0:CH, gg * 128:(gg + 1) * 128], in_=xa[:, 8 + gg, osl], identity=ident[:]),
                             reads=[bxa, b_const], writes=[pb])
                    k.op("act", lambda e: e.activation(out=Btok[0:CH, :, :], in_=pv[0:CH, 0:256].rearrange("p (a n) -> p a n", a=2), func=AF.Copy),
                         reads=[pb], writes=[bBt])
                    k.op("pool", lambda e: e.tensor_tensor(out=X[0:CH], in0=tri[0:CH, 0:CH].unsqueeze(1).to_broadcast([CH, 16, CH]),
                                                           in1=dta[0:CH, :].unsqueeze(2).to_broadcast([CH, 16, CH]), op=ALU.mult),
                         reads=[b_dta, b_const], writes=[bX])
                    ps, pb = next_ps()
                    k.op("pe", lambda e: e.matmul(ps[0:CH, 0:16], lhsT=tri[0:CH, 0:CH], rhs=dta[0:CH, :], start=True, stop=True),
                         reads=[b_dta, b_const], writes=[pb])
                    k.op("dve", lambda e: e.tensor_copy(out=at[0:CH], in_=ps[0:CH, 0:16]), reads=[pb], writes=[b_at])
                    for q4 in range(4):
                        hs = slice(4 * q4, 4 * q4 + 4)
                        ps, pb = next_ps()
                        pv4 = ps[:, 0:4 * CH].rearrange("p (h l) -> p h l", l=CH)
                        k.op("pe", lambda e: e.matmul(pv4, lhsT=ones_f[0:CH, :], rhs=X[0:CH, hs, :], start=True, stop=True),
                             reads=[bX, b_const], writes=[pb])
                        k.op("act", lambda e: e.activation(out=ea[:, hs, :], in_=pv4, func=AF.Exp), reads=[pb], writes=[bea])
                        k.op("act", lambda e: e.activation(out=al[:, hs], in_=pv4[:, :, CH - 1], func=AF.Copy), reads=[pb], writes=[b_al])
                        k.op("pe", lambda e: e.matmul(pv4, lhsT=identf[0:CH, :], rhs=negtri[0:CH, 0:CH].unsqueeze(1).to_broadcast([CH, 4, CH]),
                                                      start=False, stop=True, skip_group_check=True), reads=[b_const], writes=[pb])
                        k.op("dve", lambda e: e.tensor_tensor(out=X[0:CH, hs, :], in0=pv4[0:CH], in1=at[0:CH, hs].unsqueeze(2).to_broadcast([CH, 4, CH]),
                                                              op=ALU.subtract), reads=[pb, b_at, bX], writes=[bX])
                    k.op("act", lambda e: e.activation(out=dec[0:CH], in_=X[0:CH], func=AF.Exp), reads=[bX], writes=[bdec])
                    k.op("dve", lambda e: e.tensor_tensor(out=tm[0:CH], in0=al[0:CH], in1=at[0:CH], op=ALU.subtract), reads=[b_al, b_at], writes=[b_tm])
                    k.op("act", lambda e: e.activation(out=toe[0:CH], in_=tm[0:CH], func=AF.Exp), reads=[b_tm], writes=[b_toe])
                    k.op("act", lambda e: e.activation(out=cd, in_=al, func=AF.Exp), reads=[b_al], writes=[b_cd])
                    k.op("dve", lambda e: e.tensor_tensor(out=xdtE[0:CH, :].rearrange("p (h e) -> p h e", e=64),
                                                          in0=xdt[0:CH, :].rearrange("p (h e) -> p h e", e=64),
                                                          in1=toe[0:CH, :].unsqueeze(2).to_broadcast([CH, 16, 64]), op=ALU.mult),
                         reads=[bxdt, b_toe], writes=[bxdtE])
                    ps, pb = next_ps()
                    for gg in range(2):
                        k.op("pe", lambda e: e.matmul(ps[0:CH, gg * CH:(gg + 1) * CH], lhsT=xa[:, 8 + gg, osl], rhs=xa[:, 10 + gg, osl], start=True, stop=True),
                             reads=[bxa], writes=[pb])
                    k.op("act", lambda e: e.activation(out=cbs[0:CH], in_=ps[0:CH, 0:2 * CH].rearrange("p (a l) -> p a l", a=2), func=AF.Copy),
                         reads=[pb], writes=[bcbs])
                    for gg in range(2):
                        hs = slice(8 * gg, 8 * gg + 8)
                        k.op("dve", lambda e: e.tensor_tensor(out=MT[0:CH, hs, :], in0=dec[0:CH, hs, :],
                                                              in1=cbs[0:CH, gg:gg + 1, :].to_broadcast([CH, 8, CH]), op=ALU.mult),
                             reads=[bdec, bcbs], writes=[bMT])
                        k.op("pool", lambda e: e.tensor_tensor(out=Cs[:, hs, :], in0=ea[:, hs, :],
                                                               in1=xa[:, 10 + gg:11 + gg, osl].to_broadcast([128, 8, CH]), op=ALU.mult),
                             reads=[bea, bxa], writes=[bCs])
                    psy = [next_ps(), next_ps()]
                    for h in range(16):
                        ps, pb = psy[h // 8]
                        col = (h % 8) * 64
                        k.op("pe", lambda e: e.matmul(ps[0:CH, col:col + 64], lhsT=MT[0:CH, h, :], rhs=xdt[0:CH, h * 64:(h + 1) * 64], start=True, stop=False),
                             reads=[bMT, bxdt], writes=[pb])
                        k.op("pe", lambda e: e.matmul(ps[0:CH, col:col + 64], lhsT=Cs[:, h, :], rhs=Sb[:, h * 64:(h + 1) * 64], start=False, stop=True),
                             reads=[bCs, bSb], writes=[pb])
                    pss = [next_ps(), next_ps()]
                    for gg in range(2):
                        ps, pb = pss[gg]
                        k.op("pe", lambda e: e.matmul(ps[:, :], lhsT=Btok[0:CH, gg, :], rhs=xdtE[0:CH, gg * 512:(gg + 1) * 512], start=True, stop=True),
                             reads=[bBt, bxdtE], writes=[pb])
                    k.op("dve", lambda e: e.tensor_tensor(out=S[:, :].rearrange("p (h e) -> p h e", e=64), in0=S[:, :].rearrange("p (h e) -> p h e", e=64),
                                                          in1=cd.unsqueeze(2).to_broadcast([128, 16, 64]), op=ALU.mult), reads=[b_cd, bS], writes=[bS])
                    for gg in range(2):
                        ps, pb = pss[gg]
                        k.op("dve", lambda e: e.tensor_tensor(out=S[:, gg * 512:(gg + 1) * 512], in0=S[:, gg * 512:(gg + 1) * 512], in1=ps[:, :], op=ALU.add),
                             reads=[pb, bS], writes=[bS])
                    k.op("act", lambda e: e.activation(out=Sb[:], in_=S[:], func=AF.Copy), reads=[bS], writes=[bSb])
                    psz, pbz = next_ps()
                    pvz = bf(psz)
                    for fc in range(8):
                        k.op("pe", lambda e: e.transpose(out=pvz[0:CH, fc * 128:(fc + 1) * 128], in_=szT[:, fc, osl], identity=ident[:]),
                             reads=[bsz, b_const], writes=[pbz])
                    for gg in range(2):
                        ps, pb = psy[gg]
                        k.op("dve", lambda e: e.tensor_tensor(out=y1[0:CH, gg * 512:(gg + 1) * 512], in0=ps[0:CH, :], in1=xsD[0:CH, gg * 512:(gg + 1) * 512], op=ALU.add),
                             reads=[pb, bxsD], writes=[by1])
                    k.op("dve", lambda e: e.tensor_tensor(out=y1[0:CH, :], in0=y1[0:CH, :], in1=pvz[0:CH, :], op=ALU.mult), reads=[by1, pbz], writes=[by1])
                    for gg in range(2):
                        k.op("act", lambda e: e.activation(out=junk[0:CH, :], in_=y1[0:CH, gg * 512:(gg + 1) * 512], func=AF.Square, accum_out=ssq[0:CH, gg:gg + 1]),
                             reads=[by1], writes=[bjk, b_ssq])
                    k.op("act", lambda e: e.activation(out=ssq[0:CH, 0:2], in_=ssq[0:CH, 0:2], func=AF.Sqrt, bias=epsb[0:CH, :], scale=1.0 / 512), reads=[b_ssq, b_const], writes=[b_ssq])
                    k.op("dve", lambda e: e.reciprocal(out=ssq[0:CH, 0:2], in_=ssq[0:CH, 0:2]), reads=[b_ssq], writes=[b_ssq])
                    for gg in range(2):
                        k.op("dve", lambda e: e.scalar_tensor_tensor(out=yn[0:CH, gg * 512:(gg + 1) * 512], in0=y1[0:CH, gg * 512:(gg + 1) * 512],
                                                                     scalar=ssq[0:CH, gg:gg + 1], in1=gss[0:CH, gg * 512:(gg + 1) * 512], op0=ALU.mult, op1=ALU.mult),
                             reads=[by1, b_ssq, bgss], writes=[byn2])
                    ps, pb = next_ps()
                    pv = bf(ps)
                    for fc in range(8):
                        k.op("pe", lambda e: e.transpose(out=pv[:, fc * 128:fc * 128 + CH], in_=yn[0:CH, fc * 128:(fc + 1) * 128], identity=ident[0:CH, 0:CH]),
                             reads=[byn2, b_const], writes=[pb])
                    k.op("act", lambda e: e.activation(out=ynT[:, :, osl], in_=pv[:, 0:1024].rearrange("p (c t) -> p c t", t=128)[:, :, 0:CH], func=AF.Copy),
                         reads=[pb], writes=[byn])
                proj_gate_merge(l, 1, lambda c: ynT[:, c, 0:SBT], 8, S_bp[l], hT, byn, mT, bm, ts0, SBT, False, pg, bh=bh)
            with nc.allow_non_contiguous_dma(reason="tiny conv state"):
                k.dma("sp", o_cb[l, s], xpad[:, :, 0:3], reads=bxp)
            for a in range(8):
                ps, pb = next_ps()
                k.op("pe", lambda e: e.transpose(out=ps[:, 0:128], in_=S[:, a * 128:(a + 1) * 128], identity=identf[:]), reads=[bS, b_const], writes=[pb])
                k.op("act", lambda e: e.activation(out=stin[:, a, :], in_=ps[:, 0:128], func=AF.Copy), reads=[pb], writes=[bstin])
            k.dma("sp", o_ssm[l, s].rearrange("(a p) n -> p a n", p=128), stin[:], reads=[bstin])
            k.barrier()

    def lru(l, g, s, tok0, L, hT, bh, mT, bm):
        prompt = g < NPS
        SBT = min(L, 512)
        wi_l = w_in[l].rearrange("(c p) n -> p c n", p=128)
        with ExitStack() as es:
            xpad = TMP(es, "r_xpad", [128, 8, SBT + 3], F32); bxp = [Buf() for _ in range(8)]
            xcv = TMP(es, "r_xcv", [128, 8, SBT], F32); bxc = [Buf() for _ in range(8)]
            hgT = TMP(es, "r_hgT", [128, 8, SBT], BF16); bhg = Buf()
            hst = TMP(es, "r_hst", [128, 8], F32); bhst = Buf()
            wr = TMP(es, "r_wr", [128, 8, 128], F32); wig = TMP(es, "r_wi", [128, 8, 128], F32); bwr = Buf()
            wx = [TMP(es, f"r_wx{i}", [128, KC, 128], BF16) for i in range(2)]; bwx = [Buf(), Buf()]
            tmps = [[TMP(es, f"r_t{j}_{i}", [128, SBT], F32) for j in range(6)] for i in range(2)]
            btm = [[Buf() for j in range(6)] for i in range(2)]
            pg = pg_alloc(es, 8)
            k.dma("sp", wr[:], w_rg[l].rearrange("h i j -> i h j"), writes=[bwr])
            k.dma("sp", wig[:], w_ig[l].rearrange("h i j -> i h j"), writes=[bwr])
            if prompt:
                k.op("pool", lambda e: e.memset(xpad[:, :, 0:3], 0.0), writes=bxp)
                k.op("pool", lambda e: e.memset(hst[:], 0.0), writes=[bhst])
            else:
                si = s - NPS
                k.dma("sp", xpad[:, :, 0:3], st_cc[l, si], writes=bxp)
                k.dma("sp", hst[:], st_lru[l, si], writes=[bhst])
            wc = 0
            for st in range(L // SBT):
                ts0 = tok0 + st * SBT
                tsub = slice(ts0, ts0 + SBT)
                for fc in range(8):
                    rg, ai, ig, a2, u, hc = tmps[fc % 2]
                    brg, bai, big, ba2, bu, bhc = btm[fc % 2]
                    w, bw_ = wx[wc % 2], bwx[wc % 2]; wc += 1
                    wload(w[:], inB(l, O_XC + fc * 128), bw_)
                    ps, pb = next_ps()
                    for c in range(KC):
                        k.op("pe", lambda e: e.matmul(ps[:, 0:SBT], lhsT=w[:, c, :], rhs=hT[:, c, tsub], start=(c == 0), stop=(c == KC - 1)),
                             reads=[bw_, bh], writes=[pb])
                    k.op("act", lambda e: e.activation(out=xpad[:, fc, 3:3 + SBT], in_=ps[:, 0:SBT], func=AF.Copy), reads=[pb], writes=[bxp[fc]])
                    cv = xcv[:, fc, :]
                    eng = "dve"
                    k.op(eng, lambda e: e.tensor_scalar(out=cv, in0=xpad[:, fc, 3:3 + SBT], scalar1=ccw[:, fc, 3:4], scalar2=ccb[:, fc:fc + 1],
                                                        op0=ALU.mult, op1=ALU.add), reads=[bxp[fc], b_lay], writes=[bxc[fc]])
                    for kk in (2, 1, 0):
                        k.op(eng, lambda e: e.scalar_tensor_tensor(out=cv, in0=xpad[:, fc, kk:kk + SBT], scalar=ccw[:, fc, kk:kk + 1], in1=cv,
                                                                   op0=ALU.mult, op1=ALU.add), reads=[bxp[fc], b_lay, bxc[fc]], writes=[bxc[fc]])
                    k.op("pool", lambda e: e.tensor_copy(out=xpad[:, fc, 0:3], in_=xpad[:, fc, SBT:SBT + 3]), reads=[bxp[fc]], writes=[bxp[fc]])
                    psr, pbr = next_ps()
                    k.op("pe", lambda e: e.matmul(psr[:, 0:SBT], lhsT=wr[:, fc, :], rhs=cv, start=True, stop=True), reads=[bwr, bxc[fc]], writes=[pbr])
                    psi, pbi = next_ps()
                    k.op("pe", lambda e: e.matmul(psi[:, 0:SBT], lhsT=wig[:, fc, :], rhs=cv, start=True, stop=True), reads=[bwr, bxc[fc]], writes=[pbi])
                    k.op("act", lambda e: e.activation(out=rg[:], in_=psr[:, 0:SBT], func=AF.Sigmoid, bias=br_sb[:, fc:fc + 1], scale=1.0), reads=[pbr, b_lay], writes=[brg])
                    k.op("act", lambda e: e.activation(out=ig[:], in_=psi[:, 0:SBT], func=AF.Sigmoid, bias=bi_sb[:, fc:fc + 1], scale=1.0), reads=[pbi, b_lay], writes=[big])
                    k.op("act", lambda e: e.activation(out=ai[:], in_=rg[:], func=AF.Exp, scale=cneg_sb[:, fc:fc + 1]), reads=[brg, b_lay], writes=[bai])
                    k.op("dve", lambda e: e.tensor_tensor(out=a2[:], in0=ai[:], in1=ai[:], op=ALU.mult), reads=[bai], writes=[ba2])
                    k.op("act", lambda e: e.activation(out=a2[:], in_=a2[:], func=AF.Ln, bias=oneb[:], scale=-1.0), reads=[ba2, b_const], writes=[ba2])
                    k.op("act", lambda e: e.activation(out=a2[:], in_=a2[:], func=AF.Exp, scale=0.5), reads=[ba2], writes=[ba2])
                    k.op("pool", lambda e: e.tensor_tensor(out=u[:], in0=cv, in1=ig[:], op=ALU.mult), reads=[bxc[fc], big], writes=[bu])
                    k.op("dve", lambda e: e.tensor_tensor(out=u[:], in0=u[:], in1=a2[:], op=ALU.mult), reads=[bu, ba2], writes=[bu])
                    k.op("dve", lambda e: e.tensor_tensor_scan(out=hc[:], data0=ai[:], data1=u[:], initial=hst[:, fc:fc + 1], op0=ALU.mult, op1=ALU.add),
                         reads=[bai, bu, bhst], writes=[bhc])
                    k.op("dve", lambda e: e.tensor_copy(out=hst[:, fc:fc + 1], in_=hc[:, SBT - 1:SBT]), reads=[bhc], writes=[bhst])
                    w, bw_ = wx[wc % 2], bwx[wc % 2]; wc += 1
                    wload(w[:], inB(l, O_GC + fc * 128), bw_)
                    ps, pb = next_ps()
                    for c in range(KC):
                        k.op("pe", lambda e: e.matmul(ps[:, 0:SBT], lhsT=w[:, c, :], rhs=hT[:, c, tsub], start=(c == 0), stop=(c == KC - 1)),
                             reads=[bw_, bh], writes=[pb])
                    k.op("act", lambda e: e.activation(out=rg[:], in_=ps[:, 0:SBT], func=AF.Gelu_apprx_tanh), reads=[pb, brg], writes=[brg])
                    k.op("pool", lambda e: e.tensor_tensor(out=hgT[:, fc, :], in0=hc[:], in1=rg[:], op=ALU.mult), reads=[bhc, brg], writes=[bhg])
                proj_gate_merge(l, 2, lambda c: hgT[:, c, 0:SBT], 8, S_cp[l], hT, bhg, mT, bm, ts0, SBT, False, pg, bh=bh)
            with nc.allow_non_contiguous_dma(reason="tiny conv state"):
                k.dma("sp", o_cc[l, s], xpad[:, :, 0:3], reads=bxp)
            k.dma("sp", o_lru[l, s], hst[:], reads=[bhst])
            k.barrier()


    def attn_sample(l, hT, bh, mT, bm):
        wi_l = w_in[l].rearrange("(c p) n -> p c n", p=128)
        with ExitStack() as es:
            qkv = TMP(es, "q_qkv", [32, 4608], F32); bqkv = Buf()
            wq = [TMP(es, f"q_w{i}", [128, KC, 512], BF16) for i in range(2)]; bwq = [[Buf() for _ in range(4)] for _ in range(2)]
            KVc = [TMP(es, f"q_kvc{i}", [128, 1024], F32) for i in range(2)]; bkc = [Buf(), Buf()]
            KVn = [TMP(es, f"q_kvn{i}", [8, 1024], F32) for i in range(2)]; bkn = [Buf(), Buf()]
            prod = [TMP(es, f"q_pr{i}", [128, 512], F32) for i in range(2)]; bpr = [Buf(), Buf()]
            prn = TMP(es, "q_prn", [8, 512], F32); bprn = Buf()
            Wv = [TMP(es, f"q_wv{i}", [128, 512], F32) for i in range(2)]; bwv = [Buf(), Buf()]
            Wn = TMP(es, "q_wn", [8, 512], F32); bwn = Buf()
            sc = [TMP(es, f"q_sc{i}", [128, 8], F32) for i in range(2)]; bsc = [Buf(), Buf()]
            scn = TMP(es, "q_scn", [8, 8], F32); bscn = Buf()
            selq = [TMP(es, f"q_selq{i}", [32, 128], F32) for i in range(2)]; bsq = [Buf(), Buf()]
            selk_sb = TMP(es, "q_selk", [128, 32, 32], F32); bsk = Buf()
            oa_sb = TMP(es, "q_oa", [32, 512], BF16); boa2 = Buf()
            rz = TMP(es, "q_rz", [32, 8], F32); brz = Buf()
            oaT = TMP(es, "q_oaT", [128, 4, 32], BF16); boa = Buf()
            b_skv = Buf()
            k.dma("sp", selk_sb[:], selk, writes=[bsk])
            for cb in range(9):
                w, bw_ = wq[cb % 2], bwq[cb % 2]
                for sb_ in range(4):
                    wload(w[:, :, sb_ * 128:(sb_ + 1) * 128], inA(l, cb * 512 + sb_ * 128), bw_[sb_])
                ps, pb = next_ps()
                for c in range(KC):
                    k.op("pe", lambda e: e.matmul(ps[0:32, :], lhsT=hT[:, c, 0:32], rhs=w[:, c, :], start=(c == 0), stop=(c == KC - 1)),
                         reads=bw_ + [bh], writes=[pb])
                k.op("act", lambda e: e.activation(out=qkv[:, cb * 512:(cb + 1) * 512], in_=ps[0:32, :], func=AF.Copy), reads=[pb], writes=[bqkv])
            for gq in range(3):
                dst = skv[gq][l].rearrange("s t x -> (s t) x")
                k.dma("sp", dst[:, 0:512], qkv[:, O_K + gq * 512:O_K + (gq + 1) * 512], reads=[bqkv], writes=[b_skv])
                k.dma("sp", dst[:, 512:1024], qkv[:, O_V + gq * 512:O_V + (gq + 1) * 512], reads=[bqkv], writes=[b_skv])
            ps_lim[0] = 6
            ps_i[0] = 0
            psO, pbO = PS[6], PSB[6]
            psZ, pbZ = PS[7], PSB[7]
            first = [True]
            it = 0
            for si in range(NSS):
                for gq, (win, d) in enumerate(GROUPS):
                    for r in range(min(d, DL)):
                        nq = len(range(r, DL, d))
                        kc_, bkc_ = KVc[it % 2], bkc[it % 2]
                        kn_, bkn_ = KVn[it % 2], bkn[it % 2]
                        it += 1
                        k.dma("sp", kc_[:], kvc[gq][l, si].rearrange("(i dd) x -> dd i x", dd=d)[r], writes=[bkc_])
                        if d <= DL:
                            k.dma("sp", kn_[0:nq], skv[gq][l, si].rearrange("(i dd) x -> dd i x", dd=d)[r], reads=[b_skv], writes=[bkn_])
                        else:
                            k.dma("sp", kn_[0:nq], skv[gq][l, si, r:r + 1, :], reads=[b_skv], writes=[bkn_])
                        for qi in range(nq):
                            t = r + qi * d
                            tok = si * DL + t
                            sq, bsq_ = selq[tok % 2], bsq[tok % 2]
                            pr, bpr_ = prod[tok % 2], bpr[tok % 2]
                            wv, bwv_ = Wv[tok % 2], bwv[tok % 2]
                            sc_, bsc_ = sc[tok % 2], bsc[tok % 2]
                            k.op("dve", lambda e: e.tensor_copy(out=sq[:], in_=identf[0:32, tok:tok + 1].to_broadcast([32, 128])),
                                 reads=[b_const], writes=[bsq_])
                            ps, pb = next_ps()
                            k.op("pe", lambda e: e.matmul(ps[:, :], lhsT=sq[:], rhs=qkv[:, O_Q + gq * 512:O_Q + (gq + 1) * 512], start=True, stop=True),
                                 reads=[bsq_, bqkv], writes=[pb])
                            k.op("dve", lambda e: e.tensor_tensor(out=pr[:], in0=kc_[:, 0:512], in1=ps[:, :], op=ALU.mult), reads=[bkc_, pb], writes=[bpr_])
                            k.op("dve", lambda e: e.tensor_reduce(out=sc_[:], in_=pr[:].rearrange("p (h e) -> p h e", e=64), axis=AX.X, op=ALU.add),
                                 reads=[bpr_], writes=[bsc_])
                            k.op("dve", lambda e: e.tensor_tensor(out=prn[0:nq], in0=kn_[0:nq, 0:512], in1=ps[0:nq, :], op=ALU.mult), reads=[bkn_, pb], writes=[bprn])
                            k.op("dve", lambda e: e.tensor_reduce(out=scn[0:nq], in_=prn[0:nq].rearrange("p (h e) -> p h e", e=64), axis=AX.X, op=ALU.add),
                                 reads=[bprn], writes=[bscn])
                            k.op("act", lambda e: e.activation(out=sc_[:], in_=sc_[:], func=AF.Exp, scale=0.125), reads=[bsc_], writes=[bsc_])
                            k.op("act", lambda e: e.activation(out=scn[0:nq], in_=scn[0:nq], func=AF.Exp, scale=0.125), reads=[bscn], writes=[bscn])
                            k.op("dve", lambda e: e.tensor_tensor(out=sc_[:], in0=sc_[:], in1=E[:, gq * 8:(gq + 1) * 8, 128 + qi], op=ALU.mult),
                                 reads=[bsc_, b_E], writes=[bsc_])
                            k.op("dve", lambda e: e.tensor_tensor(out=scn[0:nq], in0=scn[0:nq], in1=E[0:nq, gq * 8:(gq + 1) * 8, qi], op=ALU.mult),
                                 reads=[bscn, b_E], writes=[bscn])
                            k.op("dve", lambda e: e.tensor_tensor(out=wv[:].rearrange("p (h e) -> p h e", e=64), in0=kc_[:, 512:1024].rearrange("p (h e) -> p h e", e=64),
                                                                  in1=sc_[:].unsqueeze(2).to_broadcast([128, 8, 64]), op=ALU.mult), reads=[bkc_, bsc_], writes=[bwv_])
                            k.op("dve", lambda e: e.tensor_tensor(out=Wn[0:nq].rearrange("p (h e) -> p h e", e=64), in0=kn_[0:nq, 512:1024].rearrange("p (h e) -> p h e", e=64),
                                                                  in1=scn[0:nq].unsqueeze(2).to_broadcast([nq, 8, 64]), op=ALU.mult), reads=[bkn_, bscn], writes=[bwn])
                            f0 = first[0]
                            first[0] = False
                            k.op("pe", lambda e: e.matmul(psO[0:32, :], lhsT=selk_sb[:, tok, :], rhs=wv[:], start=f0, stop=False, skip_group_check=True),
                                 reads=[bsk, bwv_], writes=[pbO])
                            k.op("pe", lambda e: e.matmul(psO[0:32, :], lhsT=selk_sb[0:nq, tok, :], rhs=Wn[0:nq], start=False, stop=False, skip_group_check=True),
                                 reads=[bsk, bwn], writes=[pbO])
                            k.op("pe", lambda e: e.matmul(psZ[0:32, 0:8], lhsT=selk_sb[:, tok, :], rhs=sc_[:], start=f0, stop=False, skip_group_check=True),
                                 reads=[bsk, bsc_], writes=[pbZ])
                            k.op("pe", lambda e: e.matmul(psZ[0:32, 0:8], lhsT=selk_sb[0:nq, tok, :], rhs=scn[0:nq], start=False, stop=False, skip_group_check=True),
                                 reads=[bsk, bscn], writes=[pbZ])
            ps_lim[0] = 8
            k.op("dve", lambda e: e.reciprocal(out=rz[:], in_=psZ[0:32, 0:8]), reads=[pbZ], writes=[brz])
            k.op("dve", lambda e: e.tensor_tensor(out=oa_sb[:].rearrange("p (h e) -> p h e", e=64), in0=psO[0:32, :].rearrange("p (h e) -> p h e", e=64),
                                                  in1=rz[:].unsqueeze(2).to_broadcast([32, 8, 64]), op=ALU.mult), reads=[pbO, brz], writes=[boa2])
            ps, pb = next_ps()
            pv = bf(ps)
            for c in range(4):
                k.op("pe", lambda e: e.transpose(out=pv[:, c * 128:c * 128 + 32], in_=oa_sb[0:32, c * 128:(c + 1) * 128], identity=ident[0:32, 0:32]),
                     reads=[boa2, b_const], writes=[pb])
            k.op("act", lambda e: e.activation(out=oaT[:], in_=pv[:, 0:512].rearrange("p (c t) -> p c t", t=128)[:, :, 0:32], func=AF.Copy),
                 reads=[pb], writes=[boa])
            with ExitStack() as es2:
                pg = pg_alloc(es2, 4)
                proj_gate_merge(l, 0, lambda c: oaT[:, c, :], 4, S_ap[l], hT, boa, mT, bm, 0, 32, True, pg, bh=bh)
                k.barrier()

    def out_proj(l, g, mT, bm):
        gi = grp_info(g)
        P, T = gi["P"], gi["T"]
        TT = min(T, 1024)
        NB = TT // P
        wo_l = w_o[l].rearrange("(c p) n -> p c n", p=128)
        with ExitStack() as es:
            xt = TMP(es, "o_x", [128, NB, D], F32); bx = Buf()
            wob = [TMP(es, f"o_w{i}", [128, KC, 256], BF16) for i in range(2)]; bwo = [Buf(), Buf()]
            rt = [TMP(es, f"o_rt{i}", [128, 256], F32) for i in range(2)]; brt = [Buf(), Buf()]
            for ti in range(T // TT):
                r0 = gi["row0"] + ti * TT
                xb = xres_b[g][ti]
                k.dma("pool", xt[0:P], xres[r0:r0 + TT, :].rearrange("(b p) d -> p b d", p=P), reads=[xb], writes=[bx])
                cnt = 0
                for q in range(4):
                    w, bw_ = wob[q % 2], bwo[q % 2]
                    wload(w[:], S_o[l][q], bw_)
                    for b in range(NB):
                        ps, pb = next_ps()
                        for c in range(KC):
                            k.op("pe", lambda e: e.matmul(ps[0:P, 0:256], lhsT=mT[:, c, ti * TT + b * P:ti * TT + (b + 1) * P], rhs=w[:, c, :],
                                                          start=(c == 0), stop=(c == KC - 1)), reads=[bm, bw_], writes=[pb])
                        resid_update(xt[0:P, b, q * 256:(q + 1) * 256], bx, ps[0:P, 0:256], pb, P, 1, q * 256, 256, 1.0,
                                     rt[cnt % 2], brt[cnt % 2])
                        cnt += 1
                k.dma("pool", xres[r0:r0 + TT, :].rearrange("(b p) d -> p b d", p=P), xt[0:P], reads=[bx], writes=[xb])
            k.barrier()

    def mixer(l, g):
        gi = grp_info(g)
        P, T = gi["P"], gi["T"]
        prompt = g < NPS
        with ExitStack() as es:
            hT = TMP(es, "m_hT", [128, KC, T], BF16)
            mT = TMP(es, "m_mT", [128, KC, T], BF16)
            bh, bm = Buf(), Buf()
            TT = min(T, 1024)
            NB = TT // P
            with ExitStack() as es2:
                HB = max(NB // 2, 1)
                xth = [TMP(es2, f"m_x{i}", [128, HB, D], F32) for i in range(2)]; bxh = [Buf(), Buf()]
                nctx = norm_alloc(es2)
                hi = 0
                for ti in range(T // TT):
                    for b0 in range(0, NB, HB):
                        xt, bx = xth[hi % 2], bxh[hi % 2]; hi += 1
                        r0 = gi["row0"] + ti * TT + b0 * P
                        k.dma("pool" if hi % 2 else "sp", xt[0:P], xres[r0:r0 + HB * P, :].rearrange("(b p) d -> p b d", p=P),
                              reads=[xres_b[g][ti]], writes=[bx])
                        norm_to_hT(g, 1, xt, bx, P, HB, hT, bh, ti * TT + b0 * P, None, nctx)
                k.barrier()
            if prompt and ATTN_P:
                attn_prompt(l, g, hT, bh, mT, bm)
            elif (not prompt) and ATTN_S:
                attn_sample(l, hT, bh, mT, bm)
            else:
                k.op("dve", lambda e: e.memset(mT[:], 0.0), writes=[bm])
            if MIX_PARTS >= 2:
                for si, s in enumerate(gi["seqs"]):
                    ssd(l, g, s, si * gi["L"], gi["L"], hT, bh, mT, bm)
            if MIX_PARTS >= 3:
                for si, s in enumerate(gi["seqs"]):
                    lru(l, g, s, si * gi["L"], gi["L"], hT, bh, mT, bm)
            out_proj(l, g, mT, bm)


    for l in range(NLAY):
        precast(l)
    for l in range(NLAY):
        ada(l)
        for g in GRPS:
            setup_group(g, l)
            src = xp[g] if g < NPS else xs
            ff(l, 0, g, src, first=(l == 0))
            mixer(l, g)
            ff(l, 1, g, None, first=False)
    for g in range(3):
        final_norm(g, yp[g] if g < NPS else ys)
    k.barrier()
    return nc


_T = lambda a: np.ascontiguousarray(a)


def _featT(v, nchunk):
    sh = v.shape[:-1]
    return _T(np.moveaxis(v.reshape(sh + (nchunk, 128)), -1, -2))


def make_in_maps(inp):
    f = lambda a: np.asarray(a, dtype=np.float32)
    rel_bias = f(inp["rel_bias"])
    kj = np.arange(128)[:, None]
    qi = np.arange(128)[None, :]
    dist_cur = qi - kj
    dist_prev = 128 + qi - kj
    ebias = np.zeros((128, 24, 256), np.float32)
    emask = np.zeros((128, 256), np.float32)
    emask[:, 0:128] = (dist_cur >= 0)
    emask[:, 128:256] = (dist_prev <= 128)
    for g, (win, dil) in enumerate(GROUPS):
        bc = t5_bucket(np.clip(dist_cur, 0, 128) * dil)
        bp = t5_bucket(np.clip(dist_prev, 0, 128) * dil)
        for h in range(8):
            ebias[:, g * 8 + h, 0:128] = rel_bias[bc, g * 8 + h]
            ebias[:, g * 8 + h, 128:256] = rel_bias[bp, g * 8 + h]
    selg = np.zeros((3, NSEQ, 128), np.float32)
    selg[0, 0, :] = 1.0
    selg[1, 1, :] = 1.0
    for m in range(ST):
        selg[2, NPS + m // DL, m] = 1.0
    selk = np.zeros((128, 32, 32), np.float32)
    for t in range(32):
        selk[:, t, t] = 1.0
    shared = dict(
        ebias=ebias, emask=emask, selg=selg, selk=selk,
        w_ada=f(inp["w_ada"]), b_adaT=_featT(f(inp["b_ada"]), 72),
        g_ff1T=_featT(f(inp["g_ff1"]), 8), g_mixT=_featT(f(inp["g_mix"]), 8), g_ff2T=_featT(f(inp["g_ff2"]), 8),
        gfin=_T(np.broadcast_to(f(inp["g_final"])[None, :], (128, D))),
        w_ff1_in=f(inp["w_ff1_in"]), w_ff2_in=f(inp["w_ff2_in"]), w_ff1_out=f(inp["w_ff1_out"]), w_ff2_out=f(inp["w_ff2_out"]),
        w_in=f(inp["w_in"]), w_a_proj=f(inp["w_a_proj"]), w_b_proj=f(inp["w_b_proj"]), w_c_proj=f(inp["w_c_proj"]),
        w_out=f(inp["w_out"]),
        cbwT=_T(np.transpose(f(inp["conv_b_w"]).reshape(DEPTH, 4, 12, 128), (0, 3, 2, 1))),
        cbbT=_featT(f(inp["conv_b_b"]), 12),
        ccwT=_T(np.transpose(f(inp["conv_c_w"]).reshape(DEPTH, 4, 8, 128), (0, 3, 2, 1))),
        ccbT=_featT(f(inp["conv_c_b"]), 8),
        dtb_bc=_T(np.broadcast_to(f(inp["dt_bias"])[:, None, :], (DEPTH, 128, 16))),
        alog_bc=_T(np.broadcast_to(f(inp["a_log"])[:, None, :], (DEPTH, 128, 16))),
        dsk_bc=_T(np.broadcast_to(f(inp["d_skip"])[:, None, :], (DEPTH, 128, 16))),
        gssm_bc=_T(np.broadcast_to(f(inp["g_ssm_norm"])[:, None, :], (DEPTH, 128, D))),
        w_rgate=f(inp["w_rgate"]), w_igate=f(inp["w_igate"]),
        brT=_featT(f(inp["b_rgate"]), 8), biT=_featT(f(inp["b_igate"]), 8), lamT=_featT(f(inp["lru_lambda"]), 8),
    )
    b_ada = f(inp["b_ada"])
    gate_cols = np.concatenate([b_ada[:, 2 * D:3 * D], b_ada[:, 5 * D:6 * D], b_ada[:, 8 * D:9 * D]], axis=1)
    shared["b_adaG"] = _T(np.broadcast_to(gate_cols[:, None, :], (DEPTH, NSEQ, 3 * D)))
    maps = []
    for c in range(NCORES):
        ps = slice(c * NPS, (c + 1) * NPS)
        ss = slice(c * NSS, (c + 1) * NSS)
        cc = np.concatenate([f(inp["c_prompt"])[ps], f(inp["c_sample"])[ss]], axis=0)
        m = dict(shared)
        m["xp"] = _T(f(inp["x_prompt"])[ps])
        m["xs"] = _T(f(inp["x_sample"])[ss].reshape(ST, D))
        m["cT"] = _T(np.transpose(cc.reshape(NSEQ, KC, 128), (2, 1, 0)))
        m["kvc1"] = _T(f(inp["cache_win1_kv"])[:, ss].reshape(DEPTH, NSS, 128, 1024))
        m["kvc2"] = _T(f(inp["cache_win2_kv"])[:, ss].reshape(DEPTH, NSS, 512, 1024))
        m["kvc3"] = _T(f(inp["cache_win3_kv"])[:, ss].reshape(DEPTH, NSS, 2048, 1024))
        m["st_cb"] = _T(np.transpose(f(inp["state_conv_b"])[:, ss].reshape(DEPTH, NSS, 3, 12, 128), (0, 1, 4, 3, 2)))
        m["st_ssm"] = _T(f(inp["state_ssm"])[:, ss].reshape(DEPTH, NSS, 1024, 128))
        m["st_cc"] = _T(np.transpose(f(inp["state_conv_c"])[:, ss].reshape(DEPTH, NSS, 3, 8, 128), (0, 1, 4, 3, 2)))
        m["st_lru"] = _T(np.transpose(f(inp["state_lru"])[:, ss].reshape(DEPTH, NSS, 8, 128), (0, 1, 3, 2)))
        maps.append(m)
    return maps


def assemble(results):
    cat = lambda key, ax: np.concatenate([r[key] for r in results], axis=ax)
    yp = cat("yp", 0)
    ys = cat("ys", 0).reshape(NCORES * NSS, DL, D)
    out = [yp, ys]
    for i, w in enumerate((128, 512, 2048)):
        out.append(cat(f"pkv{i + 1}", 1).reshape(DEPTH, NCORES * NPS, w, 2, 8, 64))
    o_cb = np.stack([r["o_cb"] for r in results], 0)
    o_ssm = np.stack([r["o_ssm"] for r in results], 0)
    o_cc = np.stack([r["o_cc"] for r in results], 0)
    o_lru = np.stack([r["o_lru"] for r in results], 0)

    def cb_fix(a, seqsl, nch):
        a = a[:, :, seqsl]
        a = np.transpose(a, (1, 0, 2, 5, 4, 3))
        return _T(a.reshape(DEPTH, -1, 3, nch * 128))

    def ssm_fix(a, seqsl):
        a = np.transpose(a[:, :, seqsl], (1, 0, 2, 3, 4))
        return _T(a.reshape(DEPTH, -1, 16, 64, 128))

    def lru_fix(a, seqsl):
        a = np.transpose(a[:, :, seqsl], (1, 0, 2, 4, 3))
        return _T(a.reshape(DEPTH, -1, 1024))

    P_, S_ = slice(0, NPS), slice(NPS, NSEQ)
    out += [cb_fix(o_cb, P_, 12), ssm_fix(o_ssm, P_), cb_fix(o_cc, P_, 8), lru_fix(o_lru, P_)]
    for i in range(3):
        out.append(cat(f"skv{i + 1}", 1).reshape(DEPTH, NCORES * NSS, DL, 2, 8, 64))
    out += [cb_fix(o_cb, S_, 12), ssm_fix(o_ssm, S_), cb_fix(o_cc, S_, 8), lru_fix(o_lru, S_)]
    return tuple(np.ascontiguousarray(o, dtype=np.float32) for o in out)


def kernel(**inputs):
    nc = build_program()
    maps = make_in_maps(inputs)
    res = run_bass_kernel_spmd(nc, maps, core_ids=list(range(NCORES)))
    return assemble(res.results)
```

```python
import math
from contextlib import ExitStack
import numpy as np
import concourse.bass as bass
import concourse.mybir as mybir
from concourse.bass_utils import run_bass_kernel_spmd

F32 = mybir.dt.float32
BF16 = mybir.dt.bfloat16
ALU = mybir.AluOpType
AF = mybir.ActivationFunctionType
AX = mybir.AxisListType

NCORES = 8
D = 1024
KC = 8
DEPTH = 2
SEQ = 2048
NPS = 2
NSS = 4
DL = 8
ST = NSS * DL
NSEQ = NPS + NSS
DFF = 2816
FC = 22
NIN = 12304
import os
MIX_PARTS = int(os.environ.get("MK_PARTS", "3"))
ATTN_P = int(os.environ.get("MK_ATTN_P", "1"))
ATTN_S = int(os.environ.get("MK_ATTN_S", "1"))
NLAY = int(os.environ.get("MK_LAYERS", "2"))
A_STAGE = int(os.environ.get("MK_ASTAGE", "4"))
A_HP = int(os.environ.get("MK_HP", "4"))
A_GQS = [int(x) for x in os.environ.get("MK_GQS", "0,1,2").split(",")]
GRPS = [int(x) for x in os.environ.get("MK_GRPS", "0,1,2").split(",")]
EPS = 1e-6
GROUPS = ((128, 1), (512, 4), (2048, 16))
O_Q, O_K, O_V = 0, 1536, 3072
O_Z = 4608
O_XBC = 5632
O_DT = 7168
O_XC = 7184
O_GC = 8208
O_GATES = 9232


def t5_bucket(dist):
    dist = np.asarray(dist)
    large = 16 + (np.log(np.maximum(dist, 1) / 16) / math.log(2048 / 16) * 16).astype(np.int64)
    large = np.minimum(large, 31)
    return np.where(dist < 16, dist, large).astype(np.int32)


class Buf:
    __slots__ = ("w", "r")

    def __init__(self):
        self.w = None
        self.r = []


class K:
    def __init__(self, nc):
        self.nc = nc
        self.engs = {"pe": nc.tensor, "act": nc.scalar, "dve": nc.vector, "pool": nc.gpsimd, "sp": nc.sync}
        self.sem = {}
        self.cnt = {}
        for e in ("pe", "act", "dve", "pool"):
            self.sem[e] = nc.alloc_semaphore("s_" + e)
            self.cnt[e] = 0
        self.known = {e: {} for e in self.engs}
        self.dpool = {}
        for q, n in (("sp", 24), ("pool", 16), ("act", 6)):
            self.dpool[q] = [[nc.alloc_semaphore(f"d_{q}{i}"), 0] for i in range(n)]
        self.dnext = {q: 0 for q in self.dpool}
        self.pe_sem_ids = {id(self.sem["pe"])}
        self.nsem = 0
        self.last = {}

    def _need(self, e, deps):
        m = {}
        for d in deps:
            if d is None:
                continue
            s, v = d
            if e == "pe" and id(s) in self.pe_sem_ids:
                continue
            if m.get(id(s), (None, 0))[1] < v:
                m[id(s)] = (s, v)
        out = []
        kn = self.known[e]
        for k, (s, v) in m.items():
            if kn.get(k, 0) < v:
                kn[k] = v
                out.append((s, v))
        return out

    @staticmethod
    def _deps(reads, writes):
        deps = []
        for b in reads:
            deps.append(b.w)
        for b in writes:
            deps.append(b.w)
            deps.extend(b.r)
        return deps

    def op(self, e, fn, reads=(), writes=()):
        eng = self.engs[e]
        for (s, v) in self._need(e, self._deps(reads, writes)):
            eng.wait_ge(s, v)
        if self.cnt[e] >= 30000:
            self.nsem += 1
            self.sem[e] = self.nc.alloc_semaphore(f"s_{e}_{self.nsem}")
            self.cnt[e] = 0
            if e == "pe":
                self.pe_sem_ids.add(id(self.sem[e]))
        ins = fn(eng)
        self.cnt[e] += 1
        ins.then_inc(self.sem[e], 1)
        tok = (self.sem[e], self.cnt[e])
        self.last[e] = tok
        for b in reads:
            b.r.append(tok)
            if len(b.r) > 24:
                b.r = self._compact(b.r)
        for b in writes:
            b.w = tok
            b.r = []
        return ins

    @staticmethod
    def _compact(lst):
        m = {}
        for (s, v) in lst:
            if m.get(id(s), (None, 0))[1] < v:
                m[id(s)] = (s, v)
        return list(m.values())

    def dma(self, q, out, in_, reads=(), writes=(), **kw):
        eng = self.engs[q]
        pool = self.dpool[q]
        i = self.dnext[q]
        self.dnext[q] = (i + 1) % len(pool)
        slot = pool[i]
        deps = self._deps(reads, writes)
        if slot[1] > 0:
            deps.append((slot[0], slot[1]))
        for (s, v) in self._need(q, deps):
            eng.wait_ge(s, v)
        slot[1] += 16
        eng.dma_start(out=out, in_=in_, **kw).then_inc(slot[0], 16)
        tok = (slot[0], slot[1])
        for b in reads:
            b.r.append(tok)
            if len(b.r) > 24:
                b.r = self._compact(b.r)
        for b in writes:
            b.w = tok
            b.r = []

    def barrier(self):
        deps = [self.last[e] for e in self.last]
        for q in self.dpool:
            for slot in self.dpool[q]:
                if slot[1] > 0:
                    deps.append((slot[0], slot[1]))
        for e in self.engs:
            for (s, v) in self._need(e, deps):
                self.engs[e].wait_ge(s, v)


def build_program():
    nc = bass.Bass("TRN2", target_bir_lowering=False)
    k = K(nc)

    def din(name, shape):
        return nc.dram_tensor(name, list(shape), F32, kind="ExternalInput").ap()

    def dout(name, shape):
        return nc.dram_tensor(name, list(shape), F32, kind="ExternalOutput").ap()

    xp = din("xp", [NPS, SEQ, D])
    xs = din("xs", [ST, D])
    cT = din("cT", [128, KC, NSEQ])
    kvc = [din("kvc1", [DEPTH, NSS, 128, 1024]), din("kvc2", [DEPTH, NSS, 512, 1024]),
           din("kvc3", [DEPTH, NSS, 2048, 1024])]
    st_cb = din("st_cb", [DEPTH, NSS, 128, 12, 3])
    st_ssm = din("st_ssm", [DEPTH, NSS, 1024, 128])
    st_cc = din("st_cc", [DEPTH, NSS, 128, 8, 3])
    st_lru = din("st_lru", [DEPTH, NSS, 128, 8])
    ebias = din("ebias", [128, 24, 256])
    emask = din("emask", [128, 256])
    selg = din("selg", [3, NSEQ, 128])
    selk = din("selk", [128, 32, 32])
    w_ada = din("w_ada", [DEPTH, D, 9 * D])
    b_adaT = din("b_adaT", [DEPTH, 128, 72])
    b_adaG = din("b_adaG", [DEPTH, NSEQ, 3 * D])
    gnT = [din("g_ff1T", [DEPTH, 128, KC]), din("g_mixT", [DEPTH, 128, KC]), din("g_ff2T", [DEPTH, 128, KC])]
    gfin = din("gfin", [128, D])
    w_ffi = [din("w_ff1_in", [DEPTH, D, 2 * DFF]), din("w_ff2_in", [DEPTH, D, 2 * DFF])]
    w_ffo = [din("w_ff1_out", [DEPTH, DFF, D]), din("w_ff2_out", [DEPTH, DFF, D])]
    w_in = din("w_in", [DEPTH, D, NIN])
    w_ap = din("w_a_proj", [DEPTH, 512, D])
    w_bp = din("w_b_proj", [DEPTH, D, D])
    w_cp = din("w_c_proj", [DEPTH, D, D])
    w_o = din("w_out", [DEPTH, D, D])
    cbwT = din("cbwT", [DEPTH, 128, 12, 4])
    cbbT = din("cbbT", [DEPTH, 128, 12])
    ccwT = din("ccwT", [DEPTH, 128, 8, 4])
    ccbT = din("ccbT", [DEPTH, 128, 8])
    dtb_bc = din("dtb_bc", [DEPTH, 128, 16])
    alog_bc = din("alog_bc", [DEPTH, 128, 16])
    dsk_bc = din("dsk_bc", [DEPTH, 128, 16])
    gssm_bc = din("gssm_bc", [DEPTH, 128, D])
    w_rg = din("w_rgate", [DEPTH, 8, 128, 128])
    w_ig = din("w_igate", [DEPTH, 8, 128, 128])
    brT = din("brT", [DEPTH, 128, 8])
    biT = din("biT", [DEPTH, 128, 8])
    lamT = din("lamT", [DEPTH, 128, 8])

    yp = dout("yp", [NPS, SEQ, D])
    ys = dout("ys", [ST, D])
    pkv = [dout("pkv1", [DEPTH, NPS, 128, 1024]), dout("pkv2", [DEPTH, NPS, 512, 1024]),
           dout("pkv3", [DEPTH, NPS, 2048, 1024])]
    skv = [dout("skv1", [DEPTH, NSS, DL, 1024]), dout("skv2", [DEPTH, NSS, DL, 1024]),
           dout("skv3", [DEPTH, NSS, DL, 1024])]
    o_cb = dout("o_cb", [DEPTH, NSEQ, 128, 12, 3])
    o_ssm = dout("o_ssm", [DEPTH, NSEQ, 1024, 128])
    o_cc = dout("o_cc", [DEPTH, NSEQ, 128, 8, 3])
    o_lru = dout("o_lru", [DEPTH, NSEQ, 128, 8])

    xres = nc.dram_tensor("xres", [NPS * SEQ + ST, D], F32).ap()
    gsc = nc.dram_tensor("gsc", [NSEQ, 3 * D], F32).ap()
    def scr(name, shape):
        return nc.dram_tensor(name, list(shape), BF16).ap()
    S_ffi = [[scr(f"S_ffi{l}_{w}", [44, 128, KC, 128]) for w in range(2)] for l in range(DEPTH)]
    S_ffo = [[scr(f"S_ffo{l}_{w}", [4, 128, FC, 256]) for w in range(2)] for l in range(DEPTH)]
    S_inA = [scr(f"S_inA{l}", [56, 128, KC, 128]) for l in range(DEPTH)]
    S_dt = [scr(f"S_dt{l}", [128, KC, 16]) for l in range(DEPTH)]
    S_inB = [scr(f"S_inB{l}", [40, 128, KC, 128]) for l in range(DEPTH)]
    S_ap = [scr(f"S_ap{l}", [8, 128, 4, 128]) for l in range(DEPTH)]
    S_bp = [scr(f"S_bp{l}", [8, 128, KC, 128]) for l in range(DEPTH)]
    S_cp = [scr(f"S_cp{l}", [8, 128, KC, 128]) for l in range(DEPTH)]
    S_o = [scr(f"S_o{l}", [4, 128, KC, 256]) for l in range(DEPTH)]

    def inA(l, col):
        assert col % 128 == 0 and col < 7168
        return S_inA[l][col // 128]

    def inB(l, col):
        assert (col - 7184) % 128 == 0 and col >= 7184
        return S_inB[l][(col - 7184) // 128]
    b_gsc = Buf()
    xres_b = [[Buf() for _ in range(2)] for _ in range(NPS)] + [[Buf()]]

    def sb(name, shape, dt=F32):
        return nc.alloc_sbuf_tensor(name, list(shape), dt)

    uid = [0]

    def TMP(es, name, shape, dt=F32):
        uid[0] += 1
        return es.enter_context(nc.sbuf_tensor(f"{name}_{uid[0]}", list(shape), dt))

    identf = sb("identf", [128, 128]); ident = sb("ident", [128, 128], BF16)
    tri = sb("tri", [128, 128])
    negtri = sb("negtri", [128, 128])
    ones_f = sb("ones_f", [128, 128])
    ones_b = sb("ones_b", [128, 128], BF16)
    epsb = sb("epsb", [128, 1]); oneb = sb("oneb", [128, 1])
    E = sb("E", [128, 24, 256], BF16)
    csil = sb("csil", [128, KC, NSEQ], BF16)
    modT = sb("modT", [128, 72, NSEQ])
    modA = sb("modA", [128, 3, KC, NSEQ]); modB = sb("modB", [128, 3, KC, NSEQ])
    gbc = sb("gbc", [128, 3, D])
    gn_sb = sb("gn_sb", [128, 3, KC])
    cbw = sb("cbw", [128, 12, 4]); cbb = sb("cbb", [128, 12]); ccw = sb("ccw", [128, 8, 4]); ccb = sb("ccb", [128, 8])
    dtb_sb = sb("dtb_sb", [128, 16]); aneg_sb = sb("aneg_sb", [128, 16]); dsk_sb = sb("dsk_sb", [128, 16])
    br_sb = sb("br_sb", [128, 8]); bi_sb = sb("bi_sb", [128, 8]); cneg_sb = sb("cneg_sb", [128, 8])
    b_const, b_E, b_mod, b_lay, b_gbc, b_AB = Buf(), Buf(), Buf(), Buf(), Buf(), Buf()

    PS = [nc.alloc_psum_tensor(f"ps{i}", [128, 512], F32) for i in range(8)]
    PSB = [Buf() for _ in range(8)]
    ps_i = [0]

    ps_lim = [8]

    def next_ps():
        i = ps_i[0] % ps_lim[0]
        ps_i[0] = (i + 1) % ps_lim[0]
        return PS[i], PSB[i]

    def bf(ps):
        return ps[:].bitcast(BF16)

    k.op("pool", lambda e: e.memset(identf[:], 1.0), writes=[b_const])
    k.op("pool", lambda e: e.affine_select(out=identf[:], in_=identf[:], pattern=[[-1, 128]], compare_op=ALU.is_equal,
                                           fill=0.0, base=0, channel_multiplier=1), reads=[b_const], writes=[b_const])
    k.op("pool", lambda e: e.memset(tri[:], 1.0), writes=[b_const])
    k.op("pool", lambda e: e.affine_select(out=tri[:], in_=tri[:], pattern=[[1, 128]], compare_op=ALU.is_ge,
                                           fill=0.0, base=0, channel_multiplier=-1), reads=[b_const], writes=[b_const])
    k.op("pool", lambda e: e.memset(negtri[:], 0.0), writes=[b_const])
    k.op("pool", lambda e: e.affine_select(out=negtri[:], in_=negtri[:], pattern=[[1, 128]], compare_op=ALU.is_ge,
                                           fill=-30000.0, base=0, channel_multiplier=-1), reads=[b_const], writes=[b_const])
    k.op("dve", lambda e: e.memset(ones_f[:], 1.0), writes=[b_const])
    k.op("dve", lambda e: e.memset(ones_b[:], 1.0), writes=[b_const])
    k.op("dve", lambda e: e.memset(epsb[:], EPS), writes=[b_const])
    k.op("dve", lambda e: e.memset(oneb[:], 1.0), writes=[b_const])
    k.op("dve", lambda e: e.tensor_copy(out=ident[:], in_=identf[:]), reads=[b_const], writes=[b_const])

    with ExitStack() as es:
        stg = TMP(es, "stg", [128, 8, 256], F32)
        msk = TMP(es, "msk", [128, 256], F32)
        cst = TMP(es, "cst", [128, KC, NSEQ], F32)
        b_stg = Buf()
        k.dma("sp", msk[:], emask, writes=[b_stg])
        for i in range(3):
            k.dma("sp", stg[:], ebias[:, i * 8:(i + 1) * 8, :], writes=[b_stg])
            k.op("act", lambda e: e.activation(out=stg[:], in_=stg[:], func=AF.Exp), reads=[b_stg], writes=[b_stg])
            k.op("dve", lambda e: e.tensor_tensor(out=E[:, i * 8:(i + 1) * 8, :], in0=stg[:],
                                                  in1=msk[:].unsqueeze(1).to_broadcast([128, 8, 256]), op=ALU.mult),
                 reads=[b_stg], writes=[b_E])
        k.dma("sp", cst[:], cT, writes=[b_stg])
        k.op("act", lambda e: e.activation(out=csil[:], in_=cst[:], func=AF.Silu), reads=[b_stg], writes=[b_mod])
        k.barrier()

    def wload(dst, src, buf):
        if src.dtype == BF16:
            k.dma("sp", dst, src, writes=[buf])
        else:
            k.dma("pool", dst, src, writes=[buf])

    def precast(l):
        with ExitStack() as es:
            stg = [TMP(es, f"pc_s{i}", [128, NIN], BF16) for i in range(2)]
            bst = [Buf(), Buf()]
            cnt = [0]

            def chunk(src_rows, n, outs):
                i = cnt[0] % 2
                cnt[0] += 1
                k.dma("pool", stg[i][:, 0:n], src_rows, writes=[bst[i]])
                for (dst, c0, nb, bw) in outs:
                    b0 = 0
                    while b0 < nb:
                        nn = min(16, nb - b0)
                        k.dma("sp", dst[:, b0:b0 + nn, :],
                              stg[i][:, c0 + b0 * bw:c0 + (b0 + nn) * bw].rearrange("p (b n) -> p b n", n=bw), reads=[bst[i]])
                        b0 += nn

            for w in range(2):
                for c in range(KC):
                    chunk(w_ffi[w][l][c * 128:(c + 1) * 128, :], 2 * DFF,
                          [(S_ffi[l][w][:, :, c, :].rearrange("b p n -> p b n"), 0, 44, 128)])
                for j in range(FC):
                    chunk(w_ffo[w][l][j * 128:(j + 1) * 128, :], D,
                          [(S_ffo[l][w][:, :, j, :].rearrange("q p n -> p q n"), 0, 4, 256)])
            for c in range(KC):
                chunk(w_in[l][c * 128:(c + 1) * 128, :], NIN,
                      [(S_inA[l][:, :, c, :].rearrange("b p n -> p b n"), 0, 56, 128),
                       (S_dt[l][:, c:c + 1, :], 7168, 1, 16),
                       (S_inB[l][:, :, c, :].rearrange("b p n -> p b n"), 7184, 40, 128)])
            for c in range(4):
                chunk(w_ap[l][c * 128:(c + 1) * 128, :], D, [(S_ap[l][:, :, c, :].rearrange("b p n -> p b n"), 0, 8, 128)])
            for c in range(KC):
                chunk(w_bp[l][c * 128:(c + 1) * 128, :], D, [(S_bp[l][:, :, c, :].rearrange("b p n -> p b n"), 0, 8, 128)])
                chunk(w_cp[l][c * 128:(c + 1) * 128, :], D, [(S_cp[l][:, :, c, :].rearrange("b p n -> p b n"), 0, 8, 128)])
                chunk(w_o[l][c * 128:(c + 1) * 128, :], D, [(S_o[l][:, :, c, :].rearrange("q p n -> p q n"), 0, 4, 256)])
            k.barrier()

    def ada(l):
        with ExitStack() as es:
            wa = TMP(es, "wa_full", [128, KC, 9 * D], BF16)
            bwa = [Buf() for _ in range(KC)]
            badT = TMP(es, "badT", [128, 72], F32)
            badG = TMP(es, "badG", [NSEQ, 3 * D], F32)
            lam_t = TMP(es, "lam_t", [128, 8], F32)
            modrows = TMP(es, "modrows", [NSEQ, 3 * D], F32)
            b_mr = Buf()
            b_t = Buf()
            k.dma("sp", badT[:], b_adaT[l], writes=[b_t])
            k.dma("sp", badG[:], b_adaG[l], writes=[b_t])
            for c in range(KC):
                k.dma("pool", wa[:, c, :], w_ada[l][c * 128:(c + 1) * 128, :], writes=[bwa[c]])
            ps, pb = next_ps()
            for j in range(72):
                for c in range(KC):
                    k.op("pe", lambda e: e.matmul(ps[:, j * NSEQ:(j + 1) * NSEQ], lhsT=wa[:, c, j * 128:(j + 1) * 128],
                                                  rhs=csil[:, c, :], start=(c == 0), stop=(c == KC - 1)),
                         reads=[bwa[c], b_mod], writes=[pb])
            for gi in range(3):
                for hf in range(2):
                    ps2, pb2 = next_ps()
                    col0 = (3 * gi + 2) * D + hf * 512
                    for c in range(KC):
                        k.op("pe", lambda e: e.matmul(ps2[0:NSEQ, :], lhsT=csil[:, c, :], rhs=wa[:, c, col0:col0 + 512],
                                                      start=(c == 0), stop=(c == KC - 1)),
                             reads=[bwa[c], b_mod], writes=[pb2])
                    sl = slice(gi * D + hf * 512, gi * D + (hf + 1) * 512)
                    k.op("dve", lambda e: e.tensor_tensor(out=modrows[:, sl], in0=ps2[0:NSEQ, :], in1=badG[:, sl], op=ALU.add),
                         reads=[pb2, b_t], writes=[b_mr])
            k.op("dve", lambda e: e.tensor_tensor(out=modT[:], in0=ps[:, 0:72 * NSEQ].rearrange("p (j s) -> p j s", s=NSEQ),
                                                  in1=badT[:].unsqueeze(2).to_broadcast([128, 72, NSEQ]), op=ALU.add),
                 reads=[pb, b_t], writes=[b_mod])
            k.dma("sp", gsc, modrows[:], reads=[b_mr], writes=[b_gsc])
            for i in range(3):
                k.dma("sp", gn_sb[:, i, :], gnT[i][l], writes=[b_lay])
            k.dma("sp", cbw[:], cbwT[l], writes=[b_lay]); k.dma("sp", cbb[:], cbbT[l], writes=[b_lay])
            k.dma("sp", ccw[:], ccwT[l], writes=[b_lay]); k.dma("sp", ccb[:], ccbT[l], writes=[b_lay])
            k.dma("sp", dtb_sb[:], dtb_bc[l], writes=[b_lay]); k.dma("sp", aneg_sb[:], alog_bc[l], writes=[b_lay])
            k.dma("sp", dsk_sb[:], dsk_bc[l], writes=[b_lay])
            k.dma("sp", br_sb[:], brT[l], writes=[b_lay]); k.dma("sp", bi_sb[:], biT[l], writes=[b_lay])
            k.dma("sp", lam_t[:], lamT[l], writes=[b_lay])
            k.op("act", lambda e: e.activation(out=aneg_sb[:], in_=aneg_sb[:], func=AF.Exp), reads=[b_lay], writes=[b_lay])
            k.op("dve", lambda e: e.tensor_scalar(out=aneg_sb[:], in0=aneg_sb[:], scalar1=-1.0, scalar2=None, op0=ALU.mult),
                 reads=[b_lay], writes=[b_lay])
            k.op("act", lambda e: e.activation(out=lam_t[:], in_=lam_t[:], func=AF.Exp, scale=-1.0), reads=[b_lay], writes=[b_lay])
            k.op("act", lambda e: e.activation(out=lam_t[:], in_=lam_t[:], func=AF.Ln, bias=oneb[:], scale=1.0),
                 reads=[b_lay, b_const], writes=[b_lay])
            k.op("dve", lambda e: e.tensor_scalar(out=cneg_sb[:], in0=lam_t[:], scalar1=-8.0, scalar2=None, op0=ALU.mult),
                 reads=[b_lay], writes=[b_lay])
            for i in range(3):
                sc = modT[:, (3 * i + 1) * 8:(3 * i + 2) * 8, :]
                sh = modT[:, (3 * i) * 8:(3 * i + 1) * 8, :]
                k.op("dve", lambda e: e.tensor_scalar(out=modA[:, i], in0=sc, scalar1=1.0, scalar2=None, op0=ALU.add),
                     reads=[b_mod], writes=[b_AB])
                k.op("dve", lambda e: e.tensor_tensor(out=modA[:, i], in0=modA[:, i],
                                                      in1=gn_sb[:, i, :].unsqueeze(2).to_broadcast([128, KC, NSEQ]), op=ALU.mult),
                     reads=[b_AB, b_lay], writes=[b_AB])
                k.op("dve", lambda e: e.tensor_copy(out=modB[:, i], in_=sh), reads=[b_mod], writes=[b_AB])
            k.barrier()

    def grp_info(g):
        if g < NPS:
            return dict(P=128, T=SEQ, seqs=[g], L=SEQ, row0=g * SEQ)
        return dict(P=ST, T=ST, seqs=list(range(NPS, NSEQ)), L=DL, row0=NPS * SEQ)

    def setup_group(g, l):
        gi = grp_info(g)
        nseg = len(gi["seqs"])
        seg = gi["P"] // nseg
        for si, s_ in enumerate(gi["seqs"]):
            k.dma("sp", gbc[si * seg:(si + 1) * seg].rearrange("p a d -> p (a d)"), gsc[s_].partition_broadcast(seg),
                  reads=[b_gsc], writes=[b_gbc])

    def setup_AB(g, i, Afull, Bfull, b_ab):
        gi = grp_info(g)
        P = gi["P"]
        nseg = len(gi["seqs"])
        seg = P // nseg
        for si, s in enumerate(gi["seqs"]):
            k.op("dve", lambda e: e.tensor_copy(out=Afull[:, :, si * seg:(si + 1) * seg],
                                                in_=modA[:, i, :, s:s + 1].to_broadcast([128, KC, seg])),
                 reads=[b_AB], writes=[b_ab])
            k.op("dve", lambda e: e.tensor_copy(out=Bfull[:, :, si * seg:(si + 1) * seg],
                                                in_=modB[:, i, :, s:s + 1].to_broadcast([128, KC, seg])),
                 reads=[b_AB], writes=[b_ab])

    def norm_alloc(es):
        return dict(
            Afull=TMP(es, "Afull", [128, KC, 128], F32), Bfull=TMP(es, "Bfull", [128, KC, 128], F32), b_ab=Buf(),
            ss=TMP(es, "n_ss", [128, 16], F32), junk=TMP(es, "n_junk", [128, D], F32),
            xn=[TMP(es, f"n_xn{i}", [128, D], BF16) for i in range(2)],
            tmp=[TMP(es, f"n_tmp{i}", [128, KC, 128], F32) for i in range(2)],
            bss=Buf(), bj=Buf(), bxn=[Buf(), Buf()], btmp=[Buf(), Buf()])

    def norm_to_hT(g, sub_i, xt, bx, P, nblk, hT, bh, tok0, es, ctx=None):
        if ctx is None:
            ctx = norm_alloc(es)
        Afull, Bfull, b_ab = ctx["Afull"], ctx["Bfull"], ctx["b_ab"]
        setup_AB(g, sub_i, Afull, Bfull, b_ab)
        ss, junk, xn, tmp = ctx["ss"], ctx["junk"], ctx["xn"], ctx["tmp"]
        bss, bj, bxn, btmp = ctx["bss"], ctx["bj"], ctx["bxn"], ctx["btmp"]
        for b in range(nblk):
            k.op("act", lambda e: e.activation(out=junk[0:P, :], in_=xt[0:P, b, :], func=AF.Square, accum_out=ss[0:P, b:b + 1]),
                 reads=[bx], writes=[bj, bss])
        k.op("act", lambda e: e.activation(out=ss[0:P, 0:nblk], in_=ss[0:P, 0:nblk], func=AF.Sqrt, bias=epsb[0:P, :], scale=1.0 / D),
             reads=[bss, b_const], writes=[bss])
        k.op("dve", lambda e: e.reciprocal(out=ss[0:P, 0:nblk], in_=ss[0:P, 0:nblk]), reads=[bss], writes=[bss])
        for b in range(nblk):
            x_ = xn[b % 2]; bx_ = bxn[b % 2]; t_ = tmp[b % 2]; bt_ = btmp[b % 2]
            k.op("act", lambda e: e.activation(out=x_[0:P, :], in_=xt[0:P, b, :], func=AF.Identity, scale=ss[0:P, b:b + 1]),
                 reads=[bx, bss], writes=[bx_])
            ps, pb = next_ps()
            pv = bf(ps)
            for c in range(KC):
                k.op("pe", lambda e: e.transpose(out=pv[:, c * 128:c * 128 + P], in_=x_[0:P, c * 128:(c + 1) * 128],
                                                 identity=ident[0:P, 0:P]), reads=[bx_, b_const], writes=[pb])
            pvv = pv[:, 0:KC * 128].rearrange("p (c t) -> p c t", t=128)[:, :, 0:P]
            k.op("dve", lambda e: e.tensor_tensor(out=t_[:, :, 0:P], in0=pvv, in1=Afull[:, :, 0:P], op=ALU.mult),
                 reads=[pb, b_ab], writes=[bt_])
            k.op("dve", lambda e: e.tensor_tensor(out=hT[:, :, tok0 + b * P: tok0 + (b + 1) * P], in0=t_[:, :, 0:P],
                                                  in1=Bfull[:, :, 0:P], op=ALU.add), reads=[bt_, b_ab], writes=[bh])

    def resid_update(xt_slice, bx, ps_ap, pb, P, gi_idx, col0, ncol, scale, tmp, btmp):
        k.op("dve", lambda e: e.scalar_tensor_tensor(out=tmp[0:P, 0:ncol], in0=ps_ap, scalar=scale,
                                                     in1=gbc[0:P, gi_idx, col0:col0 + ncol], op0=ALU.mult, op1=ALU.mult),
             reads=[pb, b_gbc], writes=[btmp])
        k.op("dve", lambda e: e.tensor_tensor(out=xt_slice, in0=xt_slice, in1=tmp[0:P, 0:ncol], op=ALU.add),
             reads=[btmp, bx], writes=[bx])

    def ff(l, which, g, src_rows, first):
        gi = grp_info(g)
        P, T = gi["P"], gi["T"]
        TT = min(T, 1024)
        ntile = T // TT
        NB = TT // P
        SUB = min(TT, 512)
        NS = TT // SUB
        sub_i = 0 if which == 0 else 2
        wi = w_ffi[which][l].rearrange("(c p) n -> p c n", p=128)
        wo = w_ffo[which][l].rearrange("(j p) n -> p j n", p=128)
        with ExitStack() as es:
            xt = TMP(es, "f_x", [128, NB, D], F32)
            hT = TMP(es, "f_hT", [128, KC, TT], BF16)
            actT = TMP(es, "f_act", [128, FC, TT], BF16)
            bx, bh, bact = Buf(), Buf(), Buf()
            bw = [(Buf(), Buf()) for _ in range(3)]; bwo = [Buf(), Buf()]; bsu = [Buf(), Buf()]; brt = [Buf(), Buf()]
            wblk = [TMP(es, f"f_w{i}", [128, 2, KC, 128], BF16) for i in range(3)]
            wob = [TMP(es, f"f_wo{i}", [128, FC, 256], BF16) for i in range(2)]
            su = [TMP(es, f"f_su{i}", [128, 512], F32) for i in range(2)]
            rt = [TMP(es, f"f_rt{i}", [128, 256], F32) for i in range(2)]
            nctx = norm_alloc(es)
            for ti in range(ntile):
                xb = xres_b[g][ti]
                r0 = gi["row0"] + ti * TT
                if first:
                    src = src_rows[ti * TT:(ti + 1) * TT, :]
                    k.dma("pool", xt[0:P], src.rearrange("(b p) d -> p b d", p=P), writes=[bx])
                else:
                    k.dma("pool", xt[0:P], xres[r0:r0 + TT, :].rearrange("(b p) d -> p b d", p=P), reads=[xb], writes=[bx])
                norm_to_hT(g, sub_i, xt, bx, P, NB, hT, bh, 0, None, nctx)
                cnt = 0
                for j in range(FC):
                    w = wblk[j % 3]; bw_ = bw[j % 3]
                    wload(w[:, 0], S_ffi[l][which][j], bw_[0])
                    wload(w[:, 1], S_ffi[l][which][FC + j], bw_[1])
                    for s in range(NS):
                        psu, pbu = next_ps()
                        psv, pbv = next_ps()
                        tsl = slice(s * SUB, (s + 1) * SUB)
                        for c in range(KC):
                            k.op("pe", lambda e: e.matmul(psu[:, 0:SUB], lhsT=w[:, 0, c, :], rhs=hT[:, c, tsl],
                                                          start=(c == 0), stop=(c == KC - 1)), reads=[bw_[0], bh], writes=[pbu])
                        for c in range(KC):
                            k.op("pe", lambda e: e.matmul(psv[:, 0:SUB], lhsT=w[:, 1, c, :], rhs=hT[:, c, tsl],
                                                          start=(c == 0), stop=(c == KC - 1)), reads=[bw_[1], bh], writes=[pbv])
                        s_ = su[cnt % 2]; bs_ = bsu[cnt % 2]; cnt += 1
                        k.op("act", lambda e: e.activation(out=s_[:, 0:SUB], in_=psu[:, 0:SUB], func=AF.Silu),
                             reads=[pbu], writes=[bs_])
                        k.op("dve", lambda e: e.tensor_tensor(out=actT[:, j, tsl], in0=s_[:, 0:SUB], in1=psv[:, 0:SUB], op=ALU.mult),
                             reads=[bs_, pbv], writes=[bact])
                cnt = 0
                for q in range(4):
                    w = wob[q % 2]; bw_ = bwo[q % 2]
                    wload(w[:], S_ffo[l][which][q], bw_)
                    for b in range(NB):
                        ps, pb = next_ps()
                        for j in range(FC):
                            k.op("pe", lambda e: e.matmul(ps[0:P, 0:256], lhsT=actT[:, j, b * P:(b + 1) * P], rhs=w[:, j, :],
                                                          start=(j == 0), stop=(j == FC - 1)), reads=[bact, bw_], writes=[pb])
                        resid_update(xt[0:P, b, q * 256:(q + 1) * 256], bx, ps[0:P, 0:256], pb, P, sub_i, q * 256, 256, 0.5,
                                     rt[cnt % 2], brt[cnt % 2])
                        cnt += 1
                k.dma("pool", xres[r0:r0 + TT, :].rearrange("(b p) d -> p b d", p=P), xt[0:P], reads=[bx], writes=[xb])
            k.barrier()

    def final_norm(g, dst_rows):
        gi = grp_info(g)
        P, T = gi["P"], gi["T"]
        TT = min(T, 1024)
        NB = TT // P
        for ti in range(T // TT):
            with ExitStack() as es:
                xt = TMP(es, "fn_x", [128, NB, D], F32)
                ss = TMP(es, "fn_ss", [128, 16], F32)
                junk = TMP(es, "fn_junk", [128, D], F32)
                gfin_sb = TMP(es, "gfin_sb", [128, D], F32)
                k.dma("sp", gfin_sb[:], gfin, writes=[b_lay])
                bx, bss, bj = Buf(), Buf(), Buf()
                r0 = gi["row0"] + ti * TT
                k.dma("sp", xt[0:P], xres[r0:r0 + TT, :].rearrange("(b p) d -> p b d", p=P), reads=[xres_b[g][ti]], writes=[bx])
                for b in range(NB):
                    k.op("act", lambda e: e.activation(out=junk[0:P, :], in_=xt[0:P, b, :], func=AF.Square, accum_out=ss[0:P, b:b + 1]),
                         reads=[bx], writes=[bj, bss])
                k.op("act", lambda e: e.activation(out=ss[0:P, 0:NB], in_=ss[0:P, 0:NB], func=AF.Sqrt, bias=epsb[0:P, :], scale=1.0 / D),
                     reads=[bss, b_const], writes=[bss])
                k.op("dve", lambda e: e.reciprocal(out=ss[0:P, 0:NB], in_=ss[0:P, 0:NB]), reads=[bss], writes=[bss])
                for b in range(NB):
                    k.op("dve", lambda e: e.scalar_tensor_tensor(out=xt[0:P, b, :], in0=xt[0:P, b, :], scalar=ss[0:P, b:b + 1],
                                                                 in1=gfin_sb[0:P, :], op0=ALU.mult, op1=ALU.mult),
                         reads=[bx, bss, b_lay], writes=[bx])
                k.dma("sp", dst_rows[ti * TT:(ti + 1) * TT, :].rearrange("(b p) d -> p b d", p=P), xt[0:P], reads=[bx])
                k.barrier()


    def pg_alloc(es, nck):
        return dict(
            wps=[TMP(es, f"pg_wp{i}", [128, nck, 128], BF16) for i in range(2)],
            wgs=[TMP(es, f"pg_wg{i}", [128, KC, 128], BF16) for i in range(2)],
            sg=[TMP(es, f"pg_sg{i}", [128, 512], F32) for i in range(2)],
            bwp=[Buf(), Buf()], bwg=[Buf(), Buf()], bsg=[Buf(), Buf()])

    def proj_gate_merge(l, bi, actf, nck, wproj, hT, bact, mT, bm, tok0, ntok, first, pg, bh=None):
        wps, wgs, sg, bwp, bwg, bsg = pg["wps"], pg["wgs"], pg["sg"], pg["bwp"], pg["bwg"], pg["bsg"]
        tsl = slice(tok0, tok0 + ntok)
        for o in range(8):
            wp, wg = wps[o % 2], wgs[o % 2]
            wload(wp[:], wproj[o], bwp[o % 2])
            wload(wg[:], inB(l, O_GATES + bi * D + o * 128), bwg[o % 2])
            psy, pby = next_ps()
            psg, pbg = next_ps()
            for c in range(nck):
                k.op("pe", lambda e: e.matmul(psy[:, 0:ntok], lhsT=wp[:, c, :], rhs=actf(c), start=(c == 0), stop=(c == nck - 1)),
                     reads=[bwp[o % 2], bact], writes=[pby])
            for c in range(KC):
                k.op("pe", lambda e: e.matmul(psg[:, 0:ntok], lhsT=wg[:, c, :], rhs=hT[:, c, tsl], start=(c == 0), stop=(c == KC - 1)),
                     reads=[bwg[o % 2]] + ([bh] if bh is not None else []), writes=[pbg])
            s_, bs_ = sg[o % 2], bsg[o % 2]
            k.op("act", lambda e: e.activation(out=s_[:, 0:ntok], in_=psg[:, 0:ntok], func=AF.Sigmoid), reads=[pbg], writes=[bs_])
            if first:
                k.op("dve", lambda e: e.tensor_tensor(out=mT[:, o, tsl], in0=s_[:, 0:ntok], in1=psy[:, 0:ntok], op=ALU.mult),
                     reads=[bs_, pby], writes=[bm])
            else:
                k.op("dve", lambda e: e.tensor_tensor(out=s_[:, 0:ntok], in0=s_[:, 0:ntok], in1=psy[:, 0:ntok], op=ALU.mult),
                     reads=[bs_, pby], writes=[bs_])
                k.op("pool", lambda e: e.tensor_tensor(out=mT[:, o, tsl], in0=mT[:, o, tsl], in1=s_[:, 0:ntok], op=ALU.add),
                     reads=[bs_, bm], writes=[bm])

    def attn_prompt(l, g, hT, bh, mT, bm):
        T = SEQ
        wi_l = w_in[l].rearrange("(c p) n -> p c n", p=128)
        with ExitStack() as es:
            oaT = TMP(es, "a_oaT", [128, 4, T], BF16); boa = Buf()
            shift = TMP(es, "a_shift", [64, 128], BF16); bsh = Buf()
            accT = TMP(es, "a_acc", [64, 2, T], F32); bacc = Buf()
            accZ = TMP(es, "a_accz", [1, 2, T], F32); baz = Buf()
            QT = [TMP(es, f"a_q{i}", [128, T], BF16) for i in range(2)]; bq = [Buf(), Buf()]
            KTt = [TMP(es, f"a_k{i}", [128, T], BF16) for i in range(2)]; bk = [Buf(), Buf()]
            Vt = [TMP(es, f"a_v{i}", [128, 16, 128], BF16) for i in range(2)]; bv = [Buf(), Buf()]
            wq = [TMP(es, f"a_wq{i}", [128, KC, 128], BF16) for i in range(2)]; bwq = [Buf(), Buf()]
            wk = [TMP(es, f"a_wk{i}", [128, KC, 128], BF16) for i in range(2)]; bwk = [Buf(), Buf()]
            wkv = [TMP(es, f"a_wkv{i}", [128, KC, 256], BF16) for i in range(2)]; bwkv = [Buf(), Buf()]
            PT = [TMP(es, f"a_pt{i}", [128, 256], BF16) for i in range(4)]; bpt = [Buf() for _ in range(4)]
            ex = [TMP(es, f"a_ex{i}", [128, 256], F32) for i in range(2)]; bex = [Buf(), Buf()]
            kst = [TMP(es, f"a_kst{i}", [128, 256], F32) for i in range(2)]; bkst = [Buf(), Buf()]
            oan = TMP(es, "a_oan", [64, 2, 512], BF16); boan = Buf()
            rz = TMP(es, "a_rz", [1, 2, 512], F32); brz = Buf()
            k.op("dve", lambda e: e.memset(shift[:], 0.0), writes=[bsh])
            k.op("dve", lambda e: e.tensor_copy(out=shift[:, 64:128], in_=ident[0:64, 0:64]), reads=[b_const], writes=[bsh])
            cnt = 0
            kcnt = 0
            it = 0
            for hp in range(A_HP):
                for gq, (win, d) in enumerate(GROUPS):
                    if gq not in A_GQS:
                        continue
                    i2 = it % 2
                    it += 1
                    m = T // d
                    nb = m // 128
                    keep = min(win, T)
                    cq = O_Q + gq * 512 + hp * 128
                    ck = O_K + gq * 512 + hp * 128
                    cv = O_V + gq * 512 + hp * 128
                    wload(wq[i2][:], inA(l, cq), bwq[i2])
                    wload(wk[i2][:], inA(l, ck), bwk[i2])
                    wload(wkv[i2][:, :, 0:128], inA(l, ck), bwkv[i2])
                    wload(wkv[i2][:, :, 128:256], inA(l, cv), bwkv[i2])
                    for s in range(4):
                        sub = slice(s * 512, (s + 1) * 512)
                        for (wt, bw_, dst, bd) in ((wq[i2], bwq[i2], QT[i2], bq[i2]), (wk[i2], bwk[i2], KTt[i2], bk[i2])):
                            ps, pb = next_ps()
                            for c in range(KC):
                                k.op("pe", lambda e: e.matmul(ps[:, :], lhsT=wt[:, c, :], rhs=hT[:, c, sub], start=(c == 0), stop=(c == KC - 1)),
                                     reads=[bw_, bh], writes=[pb])
                            k.op("act", lambda e: e.activation(out=dst[:, sub], in_=ps[:, :], func=AF.Copy), reads=[pb], writes=[bd])
                    for r in range(d if A_STAGE >= 2 else 0):
                        for kb in range(nb):
                            blk = r * nb + kb
                            t0 = r + kb * 128 * d
                            tsl = slice(t0, t0 + 127 * d + 1, d)
                            need_k = (kb * 128 * d >= T - keep)
                            c0 = 0 if need_k else 128
                            ps, pb = next_ps()
                            for c in range(KC):
                                k.op("pe", lambda e: e.matmul(ps[:, c0:256], lhsT=hT[:, c, tsl], rhs=wkv[i2][:, c, c0:256],
                                                              start=(c == 0), stop=(c == KC - 1)), reads=[bwkv[i2], bh], writes=[pb])
                            k.op("act", lambda e: e.activation(out=Vt[i2][:, blk, :], in_=ps[:, 128:256], func=AF.Copy),
                                 reads=[pb], writes=[bv[i2]])
                            if need_k and not int(os.environ.get("MK_NOPKV", "0")):
                                ks_, bks_ = kst[kcnt % 2], bkst[kcnt % 2]
                                kcnt += 1
                                k.op("act", lambda e: e.activation(out=ks_[:], in_=ps[:, 0:256], func=AF.Copy), reads=[pb], writes=[bks_])
                                row0 = t0 - (T - keep)
                                i0 = (row0 - r) // d
                                dst = pkv[gq][l, g].rearrange("(i dd) (a x) -> dd i a x", dd=d, a=2)[r, i0:i0 + 128, :, hp * 128:(hp + 1) * 128]
                                dbg = int(os.environ.get("MK_DBG", "0"))
                                if dbg == 1:
                                    pass
                                elif dbg == 2:
                                    dst2 = pkv[gq][l, g, 0:128, :]
                                    k.dma("sp", dst2[:, hp * 128:(hp + 1) * 128], ks_[:, 0:128], reads=[bks_])
                                    k.dma("sp", dst2[:, 512 + hp * 128:512 + (hp + 1) * 128], ks_[:, 128:256], reads=[bks_])
                                else:
                                    k.dma("sp", dst[:, 0, :], ks_[:, 0:128], reads=[bks_])
                                    k.dma("sp", dst[:, 1, :], ks_[:, 128:256], reads=[bks_])
                    for h2 in range(2 if A_STAGE >= 3 else 0):
                        hg = gq * 8 + hp * 2 + h2
                        psl = slice(h2 * 64, (h2 + 1) * 64)
                        for r in range(d):
                            ptprev = None
                            for kb in range(nb):
                                nq = 256 if kb < nb - 1 else 128
                                t0 = r + kb * 128 * d
                                ksl = slice(t0, t0 + 127 * d + 1, d)
                                qsl = slice(t0, t0 + (nq - 1) * d + 1, d)
                                ps, pb = next_ps()
                                k.op("pe", lambda e: e.matmul(ps[:, 0:nq], lhsT=KTt[i2][psl, ksl], rhs=QT[i2][psl, qsl], start=True, stop=True),
                                     reads=[bk[i2], bq[i2]], writes=[pb])
                                ex_, bex_ = ex[cnt % 2], bex[cnt % 2]
                                pt, bpt_ = PT[cnt % 4], bpt[cnt % 4]
                                cnt += 1
                                k.op("act", lambda e: e.activation(out=ex_[:, 0:nq], in_=ps[:, 0:nq], func=AF.Exp, scale=0.125),
                                     reads=[pb], writes=[bex_])
                                k.op("dve", lambda e: e.tensor_tensor(out=pt[:, 0:nq], in0=ex_[:, 0:nq], in1=E[:, hg, 0:nq], op=ALU.mult),
                                     reads=[bex_, b_E], writes=[bpt_])
                                ps2, pb2 = next_ps()
                                ps3, pb3 = next_ps()
                                if kb > 0:
                                    pp, bpp = ptprev
                                    k.op("pe", lambda e: e.matmul(ps2[0:64, 0:128], lhsT=Vt[i2][:, r * nb + kb - 1, h2 * 64:(h2 + 1) * 64],
                                                                  rhs=pp[:, 128:256], start=True, stop=False), reads=[bv[i2], bpp], writes=[pb2])
                                    k.op("pe", lambda e: e.matmul(ps3[0:1, 0:128], lhsT=ones_b[:, 0:1], rhs=pp[:, 128:256], start=True, stop=False),
                                         reads=[b_const, bpp], writes=[pb3])
                                k.op("pe", lambda e: e.matmul(ps2[0:64, 0:128], lhsT=Vt[i2][:, r * nb + kb, h2 * 64:(h2 + 1) * 64],
                                                              rhs=pt[:, 0:128], start=(kb == 0), stop=True), reads=[bv[i2], bpt_], writes=[pb2])
                                k.op("pe", lambda e: e.matmul(ps3[0:1, 0:128], lhsT=ones_b[:, 0:1], rhs=pt[:, 0:128], start=(kb == 0), stop=True),
                                     reads=[b_const, bpt_], writes=[pb3])
                                adst = accT[0:64, h2, ksl]
                                zdst = accZ[0:1, h2, ksl]
                                if gq == 0:
                                    k.op("act", lambda e: e.activation(out=adst, in_=ps2[0:64, 0:128], func=AF.Copy), reads=[pb2], writes=[bacc])
                                    k.op("act", lambda e: e.activation(out=zdst, in_=ps3[0:1, 0:128], func=AF.Copy), reads=[pb3], writes=[baz])
                                else:
                                    k.op("dve", lambda e: e.tensor_tensor(out=adst, in0=adst, in1=ps2[0:64, 0:128], op=ALU.add),
                                         reads=[pb2, bacc], writes=[bacc])
                                    k.op("dve", lambda e: e.tensor_tensor(out=zdst, in0=zdst, in1=ps3[0:1, 0:128], op=ALU.add),
                                         reads=[pb3, baz], writes=[baz])
                                ptprev = (pt, bpt_)
                for s in range(4 if A_STAGE >= 4 else 0):
                    sub = slice(s * 512, (s + 1) * 512)
                    k.op("dve", lambda e: e.reciprocal(out=rz[0:1, :, :], in_=accZ[0:1, :, sub]), reads=[baz], writes=[brz])
                    for h2 in range(2):
                        ps, pb = next_ps()
                        k.op("pe", lambda e: e.matmul(ps[0:64, :], lhsT=ones_f[0:1, 0:64], rhs=rz[0:1, h2, :], start=True, stop=True),
                             reads=[brz, b_const], writes=[pb])
                        k.op("dve", lambda e: e.tensor_tensor(out=oan[:, h2, :], in0=accT[0:64, h2, sub], in1=ps[0:64, :], op=ALU.mult),
                             reads=[pb, bacc], writes=[boan])
                    ps, pb = next_ps()
                    k.op("pe", lambda e: e.matmul(ps[:, :], lhsT=ident[0:64, :], rhs=oan[:, 0, :], start=True, stop=False),
                         reads=[boan, b_const], writes=[pb])
                    k.op("pe", lambda e: e.matmul(ps[:, :], lhsT=shift[:, :], rhs=oan[:, 1, :], start=False, stop=True),
                         reads=[boan, bsh], writes=[pb])
                    k.op("act", lambda e: e.activation(out=oaT[:, hp, sub], in_=ps[:, :], func=AF.Copy), reads=[pb], writes=[boa])
            k.barrier()
            with ExitStack() as es2:
                pg = pg_alloc(es2, 4)
                for s in range(4):
                    proj_gate_merge(l, 0, lambda c: oaT[:, c, s * 512:(s + 1) * 512], 4, S_ap[l], hT, boa, mT, bm, s * 512, 512, True, pg, bh=bh)
                k.barrier()


    def ssd(l, g, s, tok0, L, hT, bh, mT, bm):
        prompt = g < NPS
        SBT = min(L, 256)
        CH = min(L, 128)
        NCHK = SBT // CH
        wi_l = w_in[l].rearrange("(c p) n -> p c n", p=128)
        with ExitStack() as es:
            xpad = TMP(es, "s_xpad", [128, 12, SBT + 3], F32); bxp = [Buf() for _ in range(12)]
            xa = TMP(es, "s_xa", [128, 12, SBT], BF16); bxa = Buf()
            szT = TMP(es, "s_szT", [128, 8, SBT], BF16); bsz = Buf()
            ynT = TMP(es, "s_ynT", [128, 8, SBT], BF16); byn = Buf()
            S = TMP(es, "s_S", [128, 1024], F32); bS = Buf()
            Sb = TMP(es, "s_Sb", [128, 1024], BF16); bSb = Buf()
            wx = [TMP(es, f"s_wx{i}", [128, KC, 128], BF16) for i in range(2)]; bwx = [Buf(), Buf()]
            wdt = TMP(es, "s_wdt", [128, KC, 16], BF16); bwdt = Buf()
            gss = TMP(es, "s_gss", [128, D], F32); bgss = Buf()
            X = TMP(es, "s_X", [128, 16, CH], F32); bX = Buf()
            ea = TMP(es, "s_ea", [128, 16, CH], BF16); bea = Buf()
            dec = TMP(es, "s_dec", [128, 16, CH], BF16); bdec = Buf()
            MT = TMP(es, "s_MT", [128, 16, CH], BF16); bMT = Buf()
            Cs = TMP(es, "s_Cs", [128, 16, CH], BF16); bCs = Buf()
            xsD = TMP(es, "s_xsD", [128, 1024], F32); bxsD = Buf()
            xdt = TMP(es, "s_xdt", [128, 1024], BF16); bxdt = Buf()
            xdtE = TMP(es, "s_xdtE", [128, 1024], BF16); bxdtE = Buf()
            Btok = TMP(es, "s_Btok", [128, 2, 128], BF16); bBt = Buf()
            cbs = TMP(es, "s_cbs", [128, 2, CH], BF16); bcbs = Buf()
            y1 = TMP(es, "s_y1", [128, 1024], F32); by1 = Buf()
            yn = TMP(es, "s_yn", [128, 1024], BF16); byn2 = Buf()
            junk = TMP(es, "s_junk", [128, 512], F32); bjk = Buf()
            cvt = [TMP(es, f"s_cv{i}", [128, SBT], F32) for i in range(2)]; bcv = [Buf(), Buf()]
            sm = TMP(es, "s_sm", [128, 8, 16], F32); bsm = Buf(); b_dt, b_dta, b_at, b_al, b_toe, b_cd, b_tm, b_ssq = [Buf() for _ in range(8)]
            stin = TMP(es, "s_stin", [128, 8, 128], F32); bstin = Buf()
            pg = pg_alloc(es, 8)
            k.dma("sp", gss[:], gssm_bc[l], writes=[bgss])
            wload(wdt[:], S_dt[l], bwdt)
            if prompt:
                k.op("pool", lambda e: e.memset(xpad[:, :, 0:3], 0.0), writes=bxp)
                k.op("pool", lambda e: e.memset(S[:], 0.0), writes=[bS])
                k.op("pool", lambda e: e.memset(Sb[:], 0.0), writes=[bSb])
            else:
                si = s - NPS
                k.dma("sp", xpad[:, :, 0:3], st_cb[l, si], writes=bxp)
                k.dma("sp", stin[:], st_ssm[l, si].rearrange("(a p) n -> p a n", p=128), writes=[bstin])
                for a in range(8):
                    ps, pb = next_ps()
                    k.op("pe", lambda e: e.transpose(out=ps[:, 0:128], in_=stin[:, a, :], identity=identf[:]), reads=[bstin, b_const], writes=[pb])
                    k.op("act", lambda e: e.activation(out=S[:, a * 128:(a + 1) * 128], in_=ps[:, 0:128], func=AF.Copy), reads=[pb], writes=[bS])
                k.op("act", lambda e: e.activation(out=Sb[:], in_=S[:], func=AF.Copy), reads=[bS], writes=[bSb])
            cvc = 0
            wc = 0
            for st in range(L // SBT):
                ts0 = tok0 + st * SBT
                tsub = slice(ts0, ts0 + SBT)
                for fc in range(8):
                    w, bw_ = wx[wc % 2], bwx[wc % 2]; wc += 1
                    wload(w[:], inA(l, O_Z + fc * 128), bw_)
                    ps, pb = next_ps()
                    for c in range(KC):
                        k.op("pe", lambda e: e.matmul(ps[:, 0:SBT], lhsT=w[:, c, :], rhs=hT[:, c, tsub], start=(c == 0), stop=(c == KC - 1)),
                             reads=[bw_, bh], writes=[pb])
                    k.op("act", lambda e: e.activation(out=szT[:, fc, :], in_=ps[:, 0:SBT], func=AF.Silu), reads=[pb], writes=[bsz])
                for fc in range(12):
                    w, bw_ = wx[wc % 2], bwx[wc % 2]; wc += 1
                    wload(w[:], inA(l, O_XBC + fc * 128), bw_)
                    ps, pb = next_ps()
                    for c in range(KC):
                        k.op("pe", lambda e: e.matmul(ps[:, 0:SBT], lhsT=w[:, c, :], rhs=hT[:, c, tsub], start=(c == 0), stop=(c == KC - 1)),
                             reads=[bw_, bh], writes=[pb])
                    k.op("act", lambda e: e.activation(out=xpad[:, fc, 3:3 + SBT], in_=ps[:, 0:SBT], func=AF.Copy), reads=[pb], writes=[bxp[fc]])
                    cv, bcv_ = cvt[cvc % 2], bcv[cvc % 2]; cvc += 1
                    eng = "dve"
                    k.op(eng, lambda e: e.tensor_scalar(out=cv[:], in0=xpad[:, fc, 3:3 + SBT], scalar1=cbw[:, fc, 3:4], scalar2=cbb[:, fc:fc + 1],
                                                        op0=ALU.mult, op1=ALU.add), reads=[bxp[fc], b_lay], writes=[bcv_])
                    for kk in (2, 1, 0):
                        k.op(eng, lambda e: e.scalar_tensor_tensor(out=cv[:], in0=xpad[:, fc, kk:kk + SBT], scalar=cbw[:, fc, kk:kk + 1], in1=cv[:],
                                                                   op0=ALU.mult, op1=ALU.add), reads=[bxp[fc], b_lay, bcv_], writes=[bcv_])
                    k.op("act", lambda e: e.activation(out=xa[:, fc, :], in_=cv[:], func=AF.Silu), reads=[bcv_], writes=[bxa])
                    k.op("pool", lambda e: e.tensor_copy(out=xpad[:, fc, 0:3], in_=xpad[:, fc, SBT:SBT + 3]), reads=[bxp[fc]], writes=[bxp[fc]])
                for ch in range(NCHK):
                    o = ch * CH
                    csl = slice(ts0 + o, ts0 + o + CH)
                    osl = slice(o, o + CH)
                    dt_, dta, at, al, toe, cd, tm, ssq = [sm[:, i, :] for i in range(8)]
                    ps, pb = next_ps()
                    for c in range(KC):
                        k.op("pe", lambda e: e.matmul(ps[0:CH, 0:16], lhsT=hT[:, c, csl], rhs=wdt[:, c, :], start=(c == 0), stop=(c == KC - 1)),
                             reads=[bwdt, bh], writes=[pb])
                    k.op("dve", lambda e: e.tensor_tensor(out=dt_[0:CH], in0=ps[0:CH, 0:16], in1=dtb_sb[0:CH, :], op=ALU.add), reads=[pb, b_lay], writes=[b_dt])
                    k.op("act", lambda e: e.activation(out=dt_[0:CH], in_=dt_[0:CH], func=AF.Exp), reads=[b_dt], writes=[b_dt])
                    k.op("act", lambda e: e.activation(out=dt_[0:CH], in_=dt_[0:CH], func=AF.Ln, bias=oneb[0:CH, :], scale=1.0), reads=[b_dt, b_const], writes=[b_dt])
                    k.op("dve", lambda e: e.tensor_tensor(out=dta[0:CH], in0=dt_[0:CH], in1=aneg_sb[0:CH, :], op=ALU.mult), reads=[b_dt, b_lay], writes=[b_dta])
                    ps, pb = next_ps()
                    pv = bf(ps)
                    for fc in range(8):
                        k.op("pe", lambda e: e.transpose(out=pv[0:CH, fc * 128:(fc + 1) * 128], in_=xa[:, fc, osl], identity=ident[:]),
                             reads=[bxa, b_const], writes=[pb])
                    pv3 = pv[0:CH, :].rearrange("p (h e) -> p h e", e=64)
                    k.op("dve", lambda e: e.tensor_tensor(out=xsD[0:CH, :].rearrange("p (h e) -> p h e", e=64), in0=pv3,
                                                          in1=dsk_sb[0:CH, :].unsqueeze(2).to_broadcast([CH, 16, 64]), op=ALU.mult),
                         reads=[pb, b_lay], writes=[bxsD])
                    k.op("dve", lambda e: e.tensor_tensor(out=xdt[0:CH, :].rearrange("p (h e) -> p h e", e=64), in0=pv3,
                                                          in1=dt_[0:CH, :].unsqueeze(2).to_broadcast([CH, 16, 64]), op=ALU.mult),
                         reads=[pb, b_dt], writes=[bxdt])
                    ps, pb = next_ps()
                    pv = bf(ps)
                    for gg in range(2):
                        k.op("pe", lambda e: e.transpose(out=pv[0:CH, gg * 128:(gg + 1) * 128], in_=xa[:, 8 + gg, osl], identity=ident[:]),
                             reads=[bxa, b_const], writes=[pb])
                    k.op("act", lambda e: e.activation(out=Btok[0:CH, :, :], in_=pv[0:CH, 0:256].rearrange("p (a n) -> p a n", a=2), func=AF.Copy),
                         reads=[pb], writes=[bBt])
                    k.op("pool", lambda e: e.tensor_tensor(out=X[0:CH], in0=tri[0:CH, 0:CH].unsqueeze(1).to_broadcast([CH, 16, CH]),
                                                           in1=dta[0:CH, :].unsqueeze(2).to_broadcast([CH, 16, CH]), op=ALU.mult),
                         reads=[b_dta, b_const], writes=[bX])
                    ps, pb = next_ps()
                    k.op("pe", lambda e: e.matmul(ps[0:CH, 0:16], lhsT=tri[0:CH, 0:CH], rhs=dta[0:CH, :], start=True, stop=True),
                         reads=[b_dta, b_const], writes=[pb])
                    k.op("dve", lambda e: e.tensor_copy(out=at[0:CH], in_=ps[0:CH, 0:16]), reads=[pb], writes=[b_at])
                    for q4 in range(4):
                        hs = slice(4 * q4, 4 * q4 + 4)
                        ps, pb = next_ps()
                        pv4 = ps[:, 0:4 * CH].rearrange("p (h l) -> p h l", l=CH)
                        k.op("pe", lambda e: e.matmul(pv4, lhsT=ones_f[0:CH, :], rhs=X[0:CH, hs, :], start=True, stop=True),
                             reads=[bX, b_const], writes=[pb])
                        k.op("act", lambda e: e.activation(out=ea[:, hs, :], in_=pv4, func=AF.Exp), reads=[pb], writes=[bea])
                        k.op("act", lambda e: e.activation(out=al[:, hs], in_=pv4[:, :, CH - 1], func=AF.Copy), reads=[pb], writes=[b_al])
                        k.op("pe", lambda e: e.matmul(pv4, lhsT=identf[0:CH, :], rhs=negtri[0:CH, 0:CH].unsqueeze(1).to_broadcast([CH, 4, CH]),
                                                      start=False, stop=True, skip_group_check=True), reads=[b_const], writes=[pb])
                        k.op("dve", lambda e: e.tensor_tensor(out=X[0:CH, hs, :], in0=pv4[0:CH], in1=at[0:CH, hs].unsqueeze(2).to_broadcast([CH, 4, CH]),
                                                              op=ALU.subtract), reads=[pb, b_at, bX], writes=[bX])
                    k.op("act", lambda e: e.activation(out=dec[0:CH], in_=X[0:CH], func=AF.Exp), reads=[bX], writes=[bdec])
                    k.op("dve", lambda e: e.tensor_tensor(out=tm[0:CH], in0=al[0:CH], in1=at[0:CH], op=ALU.subtract), reads=[b_al, b_at], writes=[b_tm])
                    k.op("act", lambda e: e.activation(out=toe[0:CH], in_=tm[0:CH], func=AF.Exp), reads=[b_tm], writes=[b_toe])
                    k.op("act", lambda e: e.activation(out=cd, in_=al, func=AF.Exp), reads=[b_al], writes=[b_cd])
                    k.op("dve", lambda e: e.tensor_tensor(out=xdtE[0:CH, :].rearrange("p (h e) -> p h e", e=64),
                                                          in0=xdt[0:CH, :].rearrange("p (h e) -> p h e", e=64),
                                                          in1=toe[0:CH, :].unsqueeze(2).to_broadcast([CH, 16, 64]), op=ALU.mult),
                         reads=[bxdt, b_toe], writes=[bxdtE])
                    ps, pb = next_ps()
                    for gg in range(2):
                        k.op("pe", lambda e: e.matmul(ps[0:CH, gg * CH:(gg + 1) * CH], lhsT=xa[:, 8 + gg, osl], rhs=xa[:, 10 + gg, osl], start=True, stop=True),
                             reads=[bxa], writes=[pb])
                    k.op("act", lambda e: e.activation(out=cbs[0:CH], in_=ps[0:CH, 0:2 * CH].rearrange("p (a l) -> p a l", a=2), func=AF.Copy),
                         reads=[pb], writes=[bcbs])
                    for gg in range(2):
                        hs = slice(8 * gg, 8 * gg + 8)
                        k.op("dve", lambda e: e.tensor_tensor(out=MT[0:CH, hs, :], in0=dec[0:CH, hs, :],
                                                              in1=cbs[0:CH, gg:gg + 1, :].to_broadcast([CH, 8, CH]), op=ALU.mult),
                             reads=[bdec, bcbs], writes=[bMT])
                        k.op("pool", lambda e: e.tensor_tensor(out=Cs[:, hs, :], in0=ea[:, hs, :],
                                                               in1=xa[:, 10 + gg:11 + gg, osl].to_broadcast([128, 8, CH]), op=ALU.mult),
                             reads=[bea, bxa], writes=[bCs])
                    psy = [next_ps(), next_ps()]
                    for h in range(16):
                        ps, pb = psy[h // 8]
                        col = (h % 8) * 64
                        k.op("pe", lambda e: e.matmul(ps[0:CH, col:col + 64], lhsT=MT[0:CH, h, :], rhs=xdt[0:CH, h * 64:(h + 1) * 64], start=True, stop=False),
                             reads=[bMT, bxdt], writes=[pb])
                        k.op("pe", lambda e: e.matmul(ps[0:CH, col:col + 64], lhsT=Cs[:, h, :], rhs=Sb[:, h * 64:(h + 1) * 64], start=False, stop=True),
                             reads=[bCs, bSb], writes=[pb])
                    pss = [next_ps(), next_ps()]
                    for gg in range(2):
                        ps, pb = pss[gg]
                        k.op("pe", lambda e: e.matmul(ps[:, :], lhsT=Btok[0:CH, gg, :], rhs=xdtE[0:CH, gg * 512:(gg + 1) * 512], start=True, stop=True),
                             reads=[bBt, bxdtE], writes=[pb])
                    k.op("dve", lambda e: e.tensor_tensor(out=S[:, :].rearrange("p (h e) -> p h e", e=64), in0=S[:, :].rearrange("p (h e) -> p h e", e=64),
                                                          in1=cd.unsqueeze(2).to_broadcast([128, 16, 64]), op=ALU.mult), reads=[b_cd, bS], writes=[bS])
                    for gg in range(2):
                        ps, pb = pss[gg]
                        k.op("dve", lambda e: e.tensor_tensor(out=S[:, gg * 512:(gg + 1) * 512], in0=S[:, gg * 512:(gg + 1) * 512], in1=ps[:, :], op=ALU.add),
                             reads=[pb, bS], writes=[bS])
                    k.op("act", lambda e: e.activation(out=Sb[:], in_=S[:], func=AF.Copy), reads=[bS], writes=[bSb])
                    psz, pbz = next_ps()
                    pvz = bf(psz)
                    for fc in range(8):
                        k.op("pe", lambda e: e.transpose(out=pvz[0:CH, fc * 128:(fc + 1) * 128], in_=szT[:, fc, osl], identity=ident[:]),
                             reads=[bsz, b_const], writes=[pbz])
                    for gg in range(2):
                        ps, pb = psy[gg]
                        k.op("dve", lambda e: e.tensor_tensor(out=y1[0:CH, gg * 512:(gg + 1) * 512], in0=ps[0:CH, :], in1=xsD[0:CH, gg * 512:(gg + 1) * 512], op=ALU.add),
                             reads=[pb, bxsD], writes=[by1])
                    k.op("dve", lambda e: e.tensor_tensor(out=y1[0:CH, :], in0=y1[0:CH, :], in1=pvz[0:CH, :], op=ALU.mult), reads=[by1, pbz], writes=[by1])
                    for gg in range(2):
                        k.op("act", lambda e: e.activation(out=junk[0:CH, :], in_=y1[0:CH, gg * 512:(gg + 1) * 512], func=AF.Square, accum_out=ssq[0:CH, gg:gg + 1]),
                             reads=[by1], writes=[bjk, b_ssq])
                    k.op("act", lambda e: e.activation(out=ssq[0:CH, 0:2], in_=ssq[0:CH, 0:2], func=AF.Ln, bias=epsb[0:CH, :], scale=1.0 / 512), reads=[b_ssq, b_const], writes=[b_ssq])
                    k.op("act", lambda e: e.activation(out=ssq[0:CH, 0:2], in_=ssq[0:CH, 0:2], func=AF.Exp, scale=-0.5), reads=[b_ssq], writes=[b_ssq])
                    for gg in range(2):
                        k.op("dve", lambda e: e.scalar_tensor_tensor(out=yn[0:CH, gg * 512:(gg + 1) * 512], in0=y1[0:CH, gg * 512:(gg + 1) * 512],
                                                                     scalar=ssq[0:CH, gg:gg + 1], in1=gss[0:CH, gg * 512:(gg + 1) * 512], op0=ALU.mult, op1=ALU.mult),
                             reads=[by1, b_ssq, bgss], writes=[byn2])
                    ps, pb = next_ps()
                    pv = bf(ps)
                    for fc in range(8):
                        k.op("pe", lambda e: e.transpose(out=pv[:, fc * 128:fc * 128 + CH], in_=yn[0:CH, fc * 128:(fc + 1) * 128], identity=ident[0:CH, 0:CH]),
                             reads=[byn2, b_const], writes=[pb])
                    k.op("act", lambda e: e.activation(out=ynT[:, :, osl], in_=pv[:, 0:1024].rearrange("p (c t) -> p c t", t=128)[:, :, 0:CH], func=AF.Copy),
                         reads=[pb], writes=[byn])
                proj_gate_merge(l, 1, lambda c: ynT[:, c, 0:SBT], 8, S_bp[l], hT, byn, mT, bm, ts0, SBT, False, pg, bh=bh)
            with nc.allow_non_contiguous_dma(reason="tiny conv state"):
                k.dma("sp", o_cb[l, s], xpad[:, :, 0:3], reads=bxp)
            for a in range(8):
                ps, pb = next_ps()
                k.op("pe", lambda e: e.transpose(out=ps[:, 0:128], in_=S[:, a * 128:(a + 1) * 128], identity=identf[:]), reads=[bS, b_const], writes=[pb])
                k.op("act", lambda e: e.activation(out=stin[:, a, :], in_=ps[:, 0:128], func=AF.Copy), reads=[pb], writes=[bstin])
            k.dma("sp", o_ssm[l, s].rearrange("(a p) n -> p a n", p=128), stin[:], reads=[bstin])
            k.barrier()

    def lru(l, g, s, tok0, L, hT, bh, mT, bm):
        prompt = g < NPS
        SBT = min(L, 512)
        wi_l = w_in[l].rearrange("(c p) n -> p c n", p=128)
        with ExitStack() as es:
            xpad = TMP(es, "r_xpad", [128, 8, SBT + 3], F32); bxp = [Buf() for _ in range(8)]
            xcv = TMP(es, "r_xcv", [128, 8, SBT], F32); bxc = [Buf() for _ in range(8)]
            hgT = TMP(es, "r_hgT", [128, 8, SBT], BF16); bhg = Buf()
            hst = TMP(es, "r_hst", [128, 8], F32); bhst = Buf()
            wr = TMP(es, "r_wr", [128, 8, 128], F32); wig = TMP(es, "r_wi", [128, 8, 128], F32); bwr = Buf()
            wx = [TMP(es, f"r_wx{i}", [128, KC, 128], BF16) for i in range(2)]; bwx = [Buf(), Buf()]
            tmps = [[TMP(es, f"r_t{j}_{i}", [128, SBT], F32) for j in range(6)] for i in range(2)]
            btm = [[Buf() for j in range(6)] for i in range(2)]
            pg = pg_alloc(es, 8)
            k.dma("sp", wr[:], w_rg[l].rearrange("h i j -> i h j"), writes=[bwr])
            k.dma("sp", wig[:], w_ig[l].rearrange("h i j -> i h j"), writes=[bwr])
            if prompt:
                k.op("pool", lambda e: e.memset(xpad[:, :, 0:3], 0.0), writes=bxp)
                k.op("pool", lambda e: e.memset(hst[:], 0.0), writes=[bhst])
            else:
                si = s - NPS
                k.dma("sp", xpad[:, :, 0:3], st_cc[l, si], writes=bxp)
                k.dma("sp", hst[:], st_lru[l, si], writes=[bhst])
            wc = 0
            for st in range(L // SBT):
                ts0 = tok0 + st * SBT
                tsub = slice(ts0, ts0 + SBT)
                for fc in range(8):
                    rg, ai, ig, a2, u, hc = tmps[fc % 2]
                    brg, bai, big, ba2, bu, bhc = btm[fc % 2]
                    w, bw_ = wx[wc % 2], bwx[wc % 2]; wc += 1
                    wload(w[:], inB(l, O_XC + fc * 128), bw_)
                    ps, pb = next_ps()
                    for c in range(KC):
                        k.op("pe", lambda e: e.matmul(ps[:, 0:SBT], lhsT=w[:, c, :], rhs=hT[:, c, tsub], start=(c == 0), stop=(c == KC - 1)),
                             reads=[bw_, bh], writes=[pb])
                    k.op("act", lambda e: e.activation(out=xpad[:, fc, 3:3 + SBT], in_=ps[:, 0:SBT], func=AF.Copy), reads=[pb], writes=[bxp[fc]])
                    cv = xcv[:, fc, :]
                    eng = "dve"
                    k.op(eng, lambda e: e.tensor_scalar(out=cv, in0=xpad[:, fc, 3:3 + SBT], scalar1=ccw[:, fc, 3:4], scalar2=ccb[:, fc:fc + 1],
                                                        op0=ALU.mult, op1=ALU.add), reads=[bxp[fc], b_lay], writes=[bxc[fc]])
                    for kk in (2, 1, 0):
                        k.op(eng, lambda e: e.scalar_tensor_tensor(out=cv, in0=xpad[:, fc, kk:kk + SBT], scalar=ccw[:, fc, kk:kk + 1], in1=cv,
                                                                   op0=ALU.mult, op1=ALU.add), reads=[bxp[fc], b_lay, bxc[fc]], writes=[bxc[fc]])
                    k.op("pool", lambda e: e.tensor_copy(out=xpad[:, fc, 0:3], in_=xpad[:, fc, SBT:SBT + 3]), reads=[bxp[fc]], writes=[bxp[fc]])
                    psr, pbr = next_ps()
                    k.op("pe", lambda e: e.matmul(psr[:, 0:SBT], lhsT=wr[:, fc, :], rhs=cv, start=True, stop=True), reads=[bwr, bxc[fc]], writes=[pbr])
                    psi, pbi = next_ps()
                    k.op("pe", lambda e: e.matmul(psi[:, 0:SBT], lhsT=wig[:, fc, :], rhs=cv, start=True, stop=True), reads=[bwr, bxc[fc]], writes=[pbi])
                    k.op("act", lambda e: e.activation(out=rg[:], in_=psr[:, 0:SBT], func=AF.Sigmoid, bias=br_sb[:, fc:fc + 1], scale=1.0), reads=[pbr, b_lay], writes=[brg])
                    k.op("act", lambda e: e.activation(out=ig[:], in_=psi[:, 0:SBT], func=AF.Sigmoid, bias=bi_sb[:, fc:fc + 1], scale=1.0), reads=[pbi, b_lay], writes=[big])
                    k.op("act", lambda e: e.activation(out=ai[:], in_=rg[:], func=AF.Exp, scale=cneg_sb[:, fc:fc + 1]), reads=[brg, b_lay], writes=[bai])
                    k.op("dve", lambda e: e.tensor_tensor(out=a2[:], in0=ai[:], in1=ai[:], op=ALU.mult), reads=[bai], writes=[ba2])
                    k.op("act", lambda e: e.activation(out=a2[:], in_=a2[:], func=AF.Ln, bias=oneb[:], scale=-1.0), reads=[ba2, b_const], writes=[ba2])
                    k.op("act", lambda e: e.activation(out=a2[:], in_=a2[:], func=AF.Exp, scale=0.5), reads=[ba2], writes=[ba2])
                    k.op("pool", lambda e: e.tensor_tensor(out=u[:], in0=cv, in1=ig[:], op=ALU.mult), reads=[bxc[fc], big], writes=[bu])
                    k.op("dve", lambda e: e.tensor_tensor(out=u[:], in0=u[:], in1=a2[:], op=ALU.mult), reads=[bu, ba2], writes=[bu])
                    k.op("dve", lambda e: e.tensor_tensor_scan(out=hc[:], data0=ai[:], data1=u[:], initial=hst[:, fc:fc + 1], op0=ALU.mult, op1=ALU.add),
                         reads=[bai, bu, bhst], writes=[bhc])
                    k.op("dve", lambda e: e.tensor_copy(out=hst[:, fc:fc + 1], in_=hc[:, SBT - 1:SBT]), reads=[bhc], writes=[bhst])
                    w, bw_ = wx[wc % 2], bwx[wc % 2]; wc += 1
                    wload(w[:], inB(l, O_GC + fc * 128), bw_)
                    ps, pb = next_ps()
                    for c in range(KC):
                        k.op("pe", lambda e: e.matmul(ps[:, 0:SBT], lhsT=w[:, c, :], rhs=hT[:, c, tsub], start=(c == 0), stop=(c == KC - 1)),
                             reads=[bw_, bh], writes=[pb])
                    k.op("act", lambda e: e.activation(out=rg[:], in_=ps[:, 0:SBT], func=AF.Gelu_apprx_tanh), reads=[pb, brg], writes=[brg])
                    k.op("pool", lambda e: e.tensor_tensor(out=hgT[:, fc, :], in0=hc[:], in1=rg[:], op=ALU.mult), reads=[bhc, brg], writes=[bhg])
                proj_gate_merge(l, 2, lambda c: hgT[:, c, 0:SBT], 8, S_cp[l], hT, bhg, mT, bm, ts0, SBT, False, pg, bh=bh)
            with nc.allow_non_contiguous_dma(reason="tiny conv state"):
                k.dma("sp", o_cc[l, s], xpad[:, :, 0:3], reads=bxp)
            k.dma("sp", o_lru[l, s], hst[:], reads=[bhst])
            k.barrier()


    def attn_sample(l, hT, bh, mT, bm):
        wi_l = w_in[l].rearrange("(c p) n -> p c n", p=128)
        with ExitStack() as es:
            qkv = TMP(es, "q_qkv", [32, 4608], F32); bqkv = Buf()
            wq = [TMP(es, f"q_w{i}", [128, KC, 512], BF16) for i in range(2)]; bwq = [[Buf() for _ in range(4)] for _ in range(2)]
            KVc = [TMP(es, f"q_kvc{i}", [128, 1024], F32) for i in range(2)]; bkc = [Buf(), Buf()]
            KVn = [TMP(es, f"q_kvn{i}", [8, 1024], F32) for i in range(2)]; bkn = [Buf(), Buf()]
            prod = [TMP(es, f"q_pr{i}", [128, 512], F32) for i in range(2)]; bpr = [Buf(), Buf()]
            prn = TMP(es, "q_prn", [8, 512], F32); bprn = Buf()
            Wv = [TMP(es, f"q_wv{i}", [128, 512], F32) for i in range(2)]; bwv = [Buf(), Buf()]
            Wn = TMP(es, "q_wn", [8, 512], F32); bwn = Buf()
            sc = [TMP(es, f"q_sc{i}", [128, 8], F32) for i in range(2)]; bsc = [Buf(), Buf()]
            scn = TMP(es, "q_scn", [8, 8], F32); bscn = Buf()
            selq = [TMP(es, f"q_selq{i}", [32, 128], F32) for i in range(2)]; bsq = [Buf(), Buf()]
            selk_sb = TMP(es, "q_selk", [128, 32, 32], F32); bsk = Buf()
            oa_sb = TMP(es, "q_oa", [32, 512], BF16); boa2 = Buf()
            rz = TMP(es, "q_rz", [32, 8], F32); brz = Buf()
            oaT = TMP(es, "q_oaT", [128, 4, 32], BF16); boa = Buf()
            b_skv = Buf()
            k.dma("sp", selk_sb[:], selk, writes=[bsk])
            for cb in range(9):
                w, bw_ = wq[cb % 2], bwq[cb % 2]
                for sb_ in range(4):
                    wload(w[:, :, sb_ * 128:(sb_ + 1) * 128], inA(l, cb * 512 + sb_ * 128), bw_[sb_])
                ps, pb = next_ps()
                for c in range(KC):
                    k.op("pe", lambda e: e.matmul(ps[0:32, :], lhsT=hT[:, c, 0:32], rhs=w[:, c, :], start=(c == 0), stop=(c == KC - 1)),
                         reads=bw_ + [bh], writes=[pb])
                k.op("act", lambda e: e.activation(out=qkv[:, cb * 512:(cb + 1) * 512], in_=ps[0:32, :], func=AF.Copy), reads=[pb], writes=[bqkv])
            for gq in range(3):
                dst = skv[gq][l].rearrange("s t x -> (s t) x")
                k.dma("sp", dst[:, 0:512], qkv[:, O_K + gq * 512:O_K + (gq + 1) * 512], reads=[bqkv], writes=[b_skv])
                k.dma("sp", dst[:, 512:1024], qkv[:, O_V + gq * 512:O_V + (gq + 1) * 512], reads=[bqkv], writes=[b_skv])
            ps_lim[0] = 6
            ps_i[0] = 0
            psO, pbO = PS[6], PSB[6]
            psZ, pbZ = PS[7], PSB[7]
            first = [True]
            it = 0
            for si in range(NSS):
                for gq, (win, d) in enumerate(GROUPS):
                    for r in range(min(d, DL)):
                        nq = len(range(r, DL, d))
                        kc_, bkc_ = KVc[it % 2], bkc[it % 2]
                        kn_, bkn_ = KVn[it % 2], bkn[it % 2]
                        it += 1
                        k.dma("sp", kc_[:], kvc[gq][l, si].rearrange("(i dd) x -> dd i x", dd=d)[r], writes=[bkc_])
                        if d <= DL:
                            k.dma("sp", kn_[0:nq], skv[gq][l, si].rearrange("(i dd) x -> dd i x", dd=d)[r], reads=[b_skv], writes=[bkn_])
                        else:
                            k.dma("sp", kn_[0:nq], skv[gq][l, si, r:r + 1, :], reads=[b_skv], writes=[bkn_])
                        for qi in range(nq):
                            t = r + qi * d
                            tok = si * DL + t
                            sq, bsq_ = selq[tok % 2], bsq[tok % 2]
                            pr, bpr_ = prod[tok % 2], bpr[tok % 2]
                            wv, bwv_ = Wv[tok % 2], bwv[tok % 2]
                            sc_, bsc_ = sc[tok % 2], bsc[tok % 2]
                            k.op("dve", lambda e: e.tensor_copy(out=sq[:], in_=identf[0:32, tok:tok + 1].to_broadcast([32, 128])),
                                 reads=[b_const], writes=[bsq_])
                            ps, pb = next_ps()
                            k.op("pe", lambda e: e.matmul(ps[:, :], lhsT=sq[:], rhs=qkv[:, O_Q + gq * 512:O_Q + (gq + 1) * 512], start=True, stop=True),
                                 reads=[bsq_, bqkv], writes=[pb])
                            k.op("dve", lambda e: e.tensor_tensor(out=pr[:], in0=kc_[:, 0:512], in1=ps[:, :], op=ALU.mult), reads=[bkc_, pb], writes=[bpr_])
                            k.op("dve", lambda e: e.tensor_reduce(out=sc_[:], in_=pr[:].rearrange("p (h e) -> p h e", e=64), axis=AX.X, op=ALU.add),
                                 reads=[bpr_], writes=[bsc_])
                            k.op("dve", lambda e: e.tensor_tensor(out=prn[0:nq], in0=kn_[0:nq, 0:512], in1=ps[0:nq, :], op=ALU.mult), reads=[bkn_, pb], writes=[bprn])
                            k.op("dve", lambda e: e.tensor_reduce(out=scn[0:nq], in_=prn[0:nq].rearrange("p (h e) -> p h e", e=64), axis=AX.X, op=ALU.add),
                                 reads=[bprn], writes=[bscn])
                            k.op("act", lambda e: e.activation(out=sc_[:], in_=sc_[:], func=AF.Exp, scale=0.125), reads=[bsc_], writes=[bsc_])
                            k.op("act", lambda e: e.activation(out=scn[0:nq], in_=scn[0:nq], func=AF.Exp, scale=0.125), reads=[bscn], writes=[bscn])
                            k.op("dve", lambda e: e.tensor_tensor(out=sc_[:], in0=sc_[:], in1=E[:, gq * 8:(gq + 1) * 8, 128 + qi], op=ALU.mult),
                                 reads=[bsc_, b_E], writes=[bsc_])
                            k.op("dve", lambda e: e.tensor_tensor(out=scn[0:nq], in0=scn[0:nq], in1=E[0:nq, gq * 8:(gq + 1) * 8, qi], op=ALU.mult),
                                 reads=[bscn, b_E], writes=[bscn])
                            k.op("dve", lambda e: e.tensor_tensor(out=wv[:].rearrange("p (h e) -> p h e", e=64), in0=kc_[:, 512:1024].rearrange("p (h e) -> p h e", e=64),
                                                                  in1=sc_[:].unsqueeze(2).to_broadcast([128, 8, 64]), op=ALU.mult), reads=[bkc_, bsc_], writes=[bwv_])
                            k.op("dve", lambda e: e.tensor_tensor(out=Wn[0:nq].rearrange("p (h e) -> p h e", e=64), in0=kn_[0:nq, 512:1024].rearrange("p (h e) -> p h e", e=64),
                                                                  in1=scn[0:nq].unsqueeze(2).to_broadcast([nq, 8, 64]), op=ALU.mult), reads=[bkn_, bscn], writes=[bwn])
                            f0 = first[0]
                            first[0] = False
                            k.op("pe", lambda e: e.matmul(psO[0:32, :], lhsT=selk_sb[:, tok, :], rhs=wv[:], start=f0, stop=False, skip_group_check=True),
                                 reads=[bsk, bwv_], writes=[pbO])
                            k.op("pe", lambda e: e.matmul(psO[0:32, :], lhsT=selk_sb[0:nq, tok, :], rhs=Wn[0:nq], start=False, stop=False, skip_group_check=True),
                                 reads=[bsk, bwn], writes=[pbO])
                            k.op("pe", lambda e: e.matmul(psZ[0:32, 0:8], lhsT=selk_sb[:, tok, :], rhs=sc_[:], start=f0, stop=False, skip_group_check=True),
                                 reads=[bsk, bsc_], writes=[pbZ])
                            k.op("pe", lambda e: e.matmul(psZ[0:32, 0:8], lhsT=selk_sb[0:nq, tok, :], rhs=scn[0:nq], start=False, stop=False, skip_group_check=True),
                                 reads=[bsk, bscn], writes=[pbZ])
            ps_lim[0] = 8
            k.op("dve", lambda e: e.reciprocal(out=rz[:], in_=psZ[0:32, 0:8]), reads=[pbZ], writes=[brz])
            k.op("dve", lambda e: e.tensor_tensor(out=oa_sb[:].rearrange("p (h e) -> p h e", e=64), in0=psO[0:32, :].rearrange("p (h e) -> p h e", e=64),
                                                  in1=rz[:].unsqueeze(2).to_broadcast([32, 8, 64]), op=ALU.mult), reads=[pbO, brz], writes=[boa2])
            ps, pb = next_ps()
            pv = bf(ps)
            for c in range(4):
                k.op("pe", lambda e: e.transpose(out=pv[:, c * 128:c * 128 + 32], in_=oa_sb[0:32, c * 128:(c + 1) * 128], identity=ident[0:32, 0:32]),
                     reads=[boa2, b_const], writes=[pb])
            k.op("act", lambda e: e.activation(out=oaT[:], in_=pv[:, 0:512].rearrange("p (c t) -> p c t", t=128)[:, :, 0:32], func=AF.Copy),
                 reads=[pb], writes=[boa])
            with ExitStack() as es2:
                pg = pg_alloc(es2, 4)
                proj_gate_merge(l, 0, lambda c: oaT[:, c, :], 4, S_ap[l], hT, boa, mT, bm, 0, 32, True, pg, bh=bh)
                k.barrier()

    def out_proj(l, g, mT, bm):
        gi = grp_info(g)
        P, T = gi["P"], gi["T"]
        TT = min(T, 1024)
        NB = TT // P
        wo_l = w_o[l].rearrange("(c p) n -> p c n", p=128)
        with ExitStack() as es:
            xt = TMP(es, "o_x", [128, NB, D], F32); bx = Buf()
            wob = [TMP(es, f"o_w{i}", [128, KC, 256], BF16) for i in range(2)]; bwo = [Buf(), Buf()]
            rt = [TMP(es, f"o_rt{i}", [128, 256], F32) for i in range(2)]; brt = [Buf(), Buf()]
            for ti in range(T // TT):
                r0 = gi["row0"] + ti * TT
                xb = xres_b[g][ti]
                k.dma("pool", xt[0:P], xres[r0:r0 + TT, :].rearrange("(b p) d -> p b d", p=P), reads=[xb], writes=[bx])
                cnt = 0
                for q in range(4):
                    w, bw_ = wob[q % 2], bwo[q % 2]
                    wload(w[:], S_o[l][q], bw_)
                    for b in range(NB):
                        ps, pb = next_ps()
                        for c in range(KC):
                            k.op("pe", lambda e: e.matmul(ps[0:P, 0:256], lhsT=mT[:, c, ti * TT + b * P:ti * TT + (b + 1) * P], rhs=w[:, c, :],
                                                          start=(c == 0), stop=(c == KC - 1)), reads=[bm, bw_], writes=[pb])
                        resid_update(xt[0:P, b, q * 256:(q + 1) * 256], bx, ps[0:P, 0:256], pb, P, 1, q * 256, 256, 1.0,
                                     rt[cnt % 2], brt[cnt % 2])
                        cnt += 1
                k.dma("pool", xres[r0:r0 + TT, :].rearrange("(b p) d -> p b d", p=P), xt[0:P], reads=[bx], writes=[xb])
            k.barrier()

    def mixer(l, g):
        gi = grp_info(g)
        P, T = gi["P"], gi["T"]
        prompt = g < NPS
        with ExitStack() as es:
            hT = TMP(es, "m_hT", [128, KC, T], BF16)
            mT = TMP(es, "m_mT", [128, KC, T], BF16)
            bh, bm = Buf(), Buf()
            TT = min(T, 1024)
            NB = TT // P
            with ExitStack() as es2:
                HB = max(NB // 2, 1)
                xth = [TMP(es2, f"m_x{i}", [128, HB, D], F32) for i in range(2)]; bxh = [Buf(), Buf()]
                nctx = norm_alloc(es2)
                hi = 0
                for ti in range(T // TT):
                    for b0 in range(0, NB, HB):
                        xt, bx = xth[hi % 2], bxh[hi % 2]; hi += 1
                        r0 = gi["row0"] + ti * TT + b0 * P
                        k.dma("pool" if hi % 2 else "sp", xt[0:P], xres[r0:r0 + HB * P, :].rearrange("(b p) d -> p b d", p=P),
                              reads=[xres_b[g][ti]], writes=[bx])
                        norm_to_hT(g, 1, xt, bx, P, HB, hT, bh, ti * TT + b0 * P, None, nctx)
                k.barrier()
            if prompt and ATTN_P:
                attn_prompt(l, g, hT, bh, mT, bm)
            elif (not prompt) and ATTN_S:
                attn_sample(l, hT, bh, mT, bm)
            else:
                k.op("dve", lambda e: e.memset(mT[:], 0.0), writes=[bm])
            if MIX_PARTS >= 2:
                for si, s in enumerate(gi["seqs"]):
                    ssd(l, g, s, si * gi["L"], gi["L"], hT, bh, mT, bm)
            if MIX_PARTS >= 3:
                for si, s in enumerate(gi["seqs"]):
                    lru(l, g, s, si * gi["L"], gi["L"], hT, bh, mT, bm)
            out_proj(l, g, mT, bm)


    for l in range(NLAY):
        precast(l)
    for l in range(NLAY):
        ada(l)
        for g in GRPS:
            setup_group(g, l)
            src = xp[g] if g < NPS else xs
            ff(l, 0, g, src, first=(l == 0))
            mixer(l, g)
            ff(l, 1, g, None, first=False)
    for g in range(3):
        final_norm(g, yp[g] if g < NPS else ys)
    k.barrier()
    return nc


_T = lambda a: np.ascontiguousarray(a)


def _featT(v, nchunk):
    sh = v.shape[:-1]
    return _T(np.moveaxis(v.reshape(sh + (nchunk, 128)), -1, -2))


def make_in_maps(inp):
    f = lambda a: np.asarray(a, dtype=np.float32)
    rel_bias = f(inp["rel_bias"])
    kj = np.arange(128)[:, None]
    qi = np.arange(128)[None, :]
    dist_cur = qi - kj
    dist_prev = 128 + qi - kj
    ebias = np.zeros((128, 24, 256), np.float32)
    emask = np.zeros((128, 256), np.float32)
    emask[:, 0:128] = (dist_cur >= 0)
    emask[:, 128:256] = (dist_prev <= 128)
    for g, (win, dil) in enumerate(GROUPS):
        bc = t5_bucket(np.clip(dist_cur, 0, 128) * dil)
        bp = t5_bucket(np.clip(dist_prev, 0, 128) * dil)
        for h in range(8):
            ebias[:, g * 8 + h, 0:128] = rel_bias[bc, g * 8 + h]
            ebias[:, g * 8 + h, 128:256] = rel_bias[bp, g * 8 + h]
    selg = np.zeros((3, NSEQ, 128), np.float32)
    selg[0, 0, :] = 1.0
    selg[1, 1, :] = 1.0
    for m in range(ST):
        selg[2, NPS + m // DL, m] = 1.0
    selk = np.zeros((128, 32, 32), np.float32)
    for t in range(32):
        selk[:, t, t] = 1.0
    shared = dict(
        ebias=ebias, emask=emask, selg=selg, selk=selk,
        w_ada=f(inp["w_ada"]), b_adaT=_featT(f(inp["b_ada"]), 72),
        g_ff1T=_featT(f(inp["g_ff1"]), 8), g_mixT=_featT(f(inp["g_mix"]), 8), g_ff2T=_featT(f(inp["g_ff2"]), 8),
        gfin=_T(np.broadcast_to(f(inp["g_final"])[None, :], (128, D))),
        w_ff1_in=f(inp["w_ff1_in"]), w_ff2_in=f(inp["w_ff2_in"]), w_ff1_out=f(inp["w_ff1_out"]), w_ff2_out=f(inp["w_ff2_out"]),
        w_in=f(inp["w_in"]), w_a_proj=f(inp["w_a_proj"]), w_b_proj=f(inp["w_b_proj"]), w_c_proj=f(inp["w_c_proj"]),
        w_out=f(inp["w_out"]),
        cbwT=_T(np.transpose(f(inp["conv_b_w"]).reshape(DEPTH, 4, 12, 128), (0, 3, 2, 1))),
        cbbT=_featT(f(inp["conv_b_b"]), 12),
        ccwT=_T(np.transpose(f(inp["conv_c_w"]).reshape(DEPTH, 4, 8, 128), (0, 3, 2, 1))),
        ccbT=_featT(f(inp["conv_c_b"]), 8),
        dtb_bc=_T(np.broadcast_to(f(inp["dt_bias"])[:, None, :], (DEPTH, 128, 16))),
        alog_bc=_T(np.broadcast_to(f(inp["a_log"])[:, None, :], (DEPTH, 128, 16))),
        dsk_bc=_T(np.broadcast_to(f(inp["d_skip"])[:, None, :], (DEPTH, 128, 16))),
        gssm_bc=_T(np.broadcast_to(f(inp["g_ssm_norm"])[:, None, :], (DEPTH, 128, D))),
        w_rgate=f(inp["w_rgate"]), w_igate=f(inp["w_igate"]),
        brT=_featT(f(inp["b_rgate"]), 8), biT=_featT(f(inp["b_igate"]), 8), lamT=_featT(f(inp["lru_lambda"]), 8),
    )
    b_ada = f(inp["b_ada"])
    gate_cols = np.concatenate([b_ada[:, 2 * D:3 * D], b_ada[:, 5 * D:6 * D], b_ada[:, 8 * D:9 * D]], axis=1)
    shared["b_adaG"] = _T(np.broadcast_to(gate_cols[:, None, :], (DEPTH, NSEQ, 3 * D)))
    maps = []
    for c in range(NCORES):
        ps = slice(c * NPS, (c + 1) * NPS)
        ss = slice(c * NSS, (c + 1) * NSS)
        cc = np.concatenate([f(inp["c_prompt"])[ps], f(inp["c_sample"])[ss]], axis=0)
        m = dict(shared)
        m["xp"] = _T(f(inp["x_prompt"])[ps])
        m["xs"] = _T(f(inp["x_sample"])[ss].reshape(ST, D))
        m["cT"] = _T(np.transpose(cc.reshape(NSEQ, KC, 128), (2, 1, 0)))
        m["kvc1"] = _T(f(inp["cache_win1_kv"])[:, ss].reshape(DEPTH, NSS, 128, 1024))
        m["kvc2"] = _T(f(inp["cache_win2_kv"])[:, ss].reshape(DEPTH, NSS, 512, 1024))
        m["kvc3"] = _T(f(inp["cache_win3_kv"])[:, ss].reshape(DEPTH, NSS, 2048, 1024))
        m["st_cb"] = _T(np.transpose(f(inp["state_conv_b"])[:, ss].reshape(DEPTH, NSS, 3, 12, 128), (0, 1, 4, 3, 2)))
        m["st_ssm"] = _T(f(inp["state_ssm"])[:, ss].reshape(DEPTH, NSS, 1024, 128))
        m["st_cc"] = _T(np.transpose(f(inp["state_conv_c"])[:, ss].reshape(DEPTH, NSS, 3, 8, 128), (0, 1, 4, 3, 2)))
        m["st_lru"] = _T(np.transpose(f(inp["state_lru"])[:, ss].reshape(DEPTH, NSS, 8, 128), (0, 1, 3, 2)))
        maps.append(m)
    return maps


def assemble(results):
    cat = lambda key, ax: np.concatenate([r[key] for r in results], axis=ax)
    yp = cat("yp", 0)
    ys = cat("ys", 0).reshape(NCORES * NSS, DL, D)
    out = [yp, ys]
    for i, w in enumerate((128, 512, 2048)):
        out.append(cat(f"pkv{i + 1}", 1).reshape(DEPTH, NCORES * NPS, w, 2, 8, 64))
    o_cb = np.stack([r["o_cb"] for r in results], 0)
    o_ssm = np.stack([r["o_ssm"] for r in results], 0)
    o_cc = np.stack([r["o_cc"] for r in results], 0)
    o_lru = np.stack([r["o_lru"] for r in results], 0)

    def cb_fix(a, seqsl, nch):
        a = a[:, :, seqsl]
        a = np.transpose(a, (1, 0, 2, 5, 4, 3))
        return _T(a.reshape(DEPTH, -1, 3, nch * 128))

    def ssm_fix(a, seqsl):
        a = np.transpose(a[:, :, seqsl], (1, 0, 2, 3, 4))
        return _T(a.reshape(DEPTH, -1, 16, 64, 128))

    def lru_fix(a, seqsl):
        a = np.transpose(a[:, :, seqsl], (1, 0, 2, 4, 3))
        return _T(a.reshape(DEPTH, -1, 1024))

    P_, S_ = slice(0, NPS), slice(NPS, NSEQ)
    out += [cb_fix(o_cb, P_, 12), ssm_fix(o_ssm, P_), cb_fix(o_cc, P_, 8), lru_fix(o_lru, P_)]
    for i in range(3):
        out.append(cat(f"skv{i + 1}", 1).reshape(DEPTH, NCORES * NSS, DL, 2, 8, 64))
    out += [cb_fix(o_cb, S_, 12), ssm_fix(o_ssm, S_), cb_fix(o_cc, S_, 8), lru_fix(o_lru, S_)]
    return tuple(np.ascontiguousarray(o, dtype=np.float32) for o in out)


def kernel(**inputs):
    nc = build_program()
    maps = make_in_maps(inputs)
    res = run_bass_kernel_spmd(nc, maps, core_ids=list(range(NCORES)))
    return assemble(res.results)
```

```python
import math
from contextlib import ExitStack
import numpy as np
import concourse.bass as bass
import concourse.mybir as mybir
from concourse.bass_utils import run_bass_kernel_spmd

F32 = mybir.dt.float32
BF16 = mybir.dt.bfloat16
ALU = mybir.AluOpType
AF = mybir.ActivationFunctionType
AX = mybir.AxisListType

NCORES = 8
D = 1024
KC = 8
DEPTH = 2
SEQ = 2048
NPS = 2
NSS = 4
DL = 8
ST = NSS * DL
NSEQ = NPS + NSS
DFF = 2816
FC = 22
NIN = 12304
import os
MIX_PARTS = int(os.environ.get("MK_PARTS", "3"))
ATTN_P = int(os.environ.get("MK_ATTN_P", "1"))
ATTN_S = int(os.environ.get("MK_ATTN_S", "1"))
NLAY = int(os.environ.get("MK_LAYERS", "2"))
A_STAGE = int(os.environ.get("MK_ASTAGE", "4"))
A_HP = int(os.environ.get("MK_HP", "4"))
A_GQS = [int(x) for x in os.environ.get("MK_GQS", "0,1,2").split(",")]
GRPS = [int(x) for x in os.environ.get("MK_GRPS", "0,1,2").split(",")]
EPS = 1e-6
GROUPS = ((128, 1), (512, 4), (2048, 16))
O_Q, O_K, O_V = 0, 1536, 3072
O_Z = 4608
O_XBC = 5632
O_DT = 7168
O_XC = 7184
O_GC = 8208
O_GATES = 9232


def t5_bucket(dist):
    dist = np.asarray(dist)
    large = 16 + (np.log(np.maximum(dist, 1) / 16) / math.log(2048 / 16) * 16).astype(np.int64)
    large = np.minimum(large, 31)
    return np.where(dist < 16, dist, large).astype(np.int32)


class Buf:
    __slots__ = ("w", "r")

    def __init__(self):
        self.w = None
        self.r = []


class K:
    def __init__(self, nc):
        self.nc = nc
        self.engs = {"pe": nc.tensor, "act": nc.scalar, "dve": nc.vector, "pool": nc.gpsimd, "sp": nc.sync}
        self.sem = {}
        self.cnt = {}
        for e in ("pe", "act", "dve", "pool"):
            self.sem[e] = nc.alloc_semaphore("s_" + e)
            self.cnt[e] = 0
        self.known = {e: {} for e in self.engs}
        self.dpool = {}
        for q, n in (("sp", 24), ("pool", 16), ("act", 6)):
            self.dpool[q] = [[nc.alloc_semaphore(f"d_{q}{i}"), 0] for i in range(n)]
        self.dnext = {q: 0 for q in self.dpool}
        self.pe_sem_ids = {id(self.sem["pe"])}
        self.nsem = 0
        self.last = {}

    def _need(self, e, deps):
        m = {}
        for d in deps:
            if d is None:
                continue
            s, v = d
            if e == "pe" and id(s) in self.pe_sem_ids:
                continue
            if m.get(id(s), (None, 0))[1] < v:
                m[id(s)] = (s, v)
        out = []
        kn = self.known[e]
        for k, (s, v) in m.items():
            if kn.get(k, 0) < v:
                kn[k] = v
                out.append((s, v))
        return out

    @staticmethod
    def _deps(reads, writes):
        deps = []
        for b in reads:
            deps.append(b.w)
        for b in writes:
            deps.append(b.w)
            deps.extend(b.r)
        return deps

    def op(self, e, fn, reads=(), writes=()):
        eng = self.engs[e]
        for (s, v) in self._need(e, self._deps(reads, writes)):
            eng.wait_ge(s, v)
        if self.cnt[e] >= 30000:
            self.nsem += 1
            self.sem[e] = self.nc.alloc_semaphore(f"s_{e}_{self.nsem}")
            self.cnt[e] = 0
            if e == "pe":
                self.pe_sem_ids.add(id(self.sem[e]))
        ins = fn(eng)
        self.cnt[e] += 1
        ins.then_inc(self.sem[e], 1)
        tok = (self.sem[e], self.cnt[e])
        self.last[e] = tok
        for b in reads:
            b.r.append(tok)
            if len(b.r) > 24:
                b.r = self._compact(b.r)
        for b in writes:
            b.w = tok
            b.r = []
        return ins

    @staticmethod
    def _compact(lst):
        m = {}
        for (s, v) in lst:
            if m.get(id(s), (None, 0))[1] < v:
                m[id(s)] = (s, v)
        return list(m.values())

    def dma(self, q, out, in_, reads=(), writes=(), **kw):
        eng = self.engs[q]
        pool = self.dpool[q]
        i = self.dnext[q]
        self.dnext[q] = (i + 1) % len(pool)
        slot = pool[i]
        deps = self._deps(reads, writes)
        if slot[1] > 0:
            deps.append((slot[0], slot[1]))
        for (s, v) in self._need(q, deps):
            eng.wait_ge(s, v)
        slot[1] += 16
        eng.dma_start(out=out, in_=in_, **kw).then_inc(slot[0], 16)
        tok = (slot[0], slot[1])
        for b in reads:
            b.r.append(tok)
            if len(b.r) > 24:
                b.r = self._compact(b.r)
        for b in writes:
            b.w = tok
            b.r = []

    def barrier(self):
        deps = [self.last[e] for e in self.last]
        for q in self.dpool:
            for slot in self.dpool[q]:
                if slot[1] > 0:
                    deps.append((slot[0], slot[1]))
        for e in self.engs:
            for (s, v) in self._need(e, deps):
                self.engs[e].wait_ge(s, v)


def build_program():
    nc = bass.Bass("TRN2", target_bir_lowering=False)
    k = K(nc)

    def din(name, shape):
        return nc.dram_tensor(name, list(shape), F32, kind="ExternalInput").ap()

    def dout(name, shape):
        return nc.dram_tensor(name, list(shape), F32, kind="ExternalOutput").ap()

    xp = din("xp", [NPS, SEQ, D])
    xs = din("xs", [ST, D])
    cT = din("cT", [128, KC, NSEQ])
    kvc = [din("kvc1", [DEPTH, NSS, 128, 1024]), din("kvc2", [DEPTH, NSS, 512, 1024]),
           din("kvc3", [DEPTH, NSS, 2048, 1024])]
    st_cb = din("st_cb", [DEPTH, NSS, 128, 12, 3])
    st_ssm = din("st_ssm", [DEPTH, NSS, 1024, 128])
    st_cc = din("st_cc", [DEPTH, NSS, 128, 8, 3])
    st_lru = din("st_lru", [DEPTH, NSS, 128, 8])
    ebias = din("ebias", [128, 24, 256])
    emask = din("emask", [128, 256])
    selg = din("selg", [3, NSEQ, 128])
    selk = din("selk", [128, 32, 32])
    w_ada = din("w_ada", [DEPTH, D, 9 * D])
    b_adaT = din("b_adaT", [DEPTH, 128, 72])
    b_adaG = din("b_adaG", [DEPTH, NSEQ, 3 * D])
    gnT = [din("g_ff1T", [DEPTH, 128, KC]), din("g_mixT", [DEPTH, 128, KC]), din("g_ff2T", [DEPTH, 128, KC])]
    gfin = din("gfin", [128, D])
    w_ffi = [din("w_ff1_in", [DEPTH, D, 2 * DFF]), din("w_ff2_in", [DEPTH, D, 2 * DFF])]
    w_ffo = [din("w_ff1_out", [DEPTH, DFF, D]), din("w_ff2_out", [DEPTH, DFF, D])]
    w_in = din("w_in", [DEPTH, D, NIN])
    w_ap = din("w_a_proj", [DEPTH, 512, D])
    w_bp = din("w_b_proj", [DEPTH, D, D])
    w_cp = din("w_c_proj", [DEPTH, D, D])
    w_o = din("w_out", [DEPTH, D, D])
    cbwT = din("cbwT", [DEPTH, 128, 12, 4])
    cbbT = din("cbbT", [DEPTH, 128, 12])
    ccwT = din("ccwT", [DEPTH, 128, 8, 4])
    ccbT = din("ccbT", [DEPTH, 128, 8])
    dtb_bc = din("dtb_bc", [DEPTH, 128, 16])
    alog_bc = din("alog_bc", [DEPTH, 128, 16])
    dsk_bc = din("dsk_bc", [DEPTH, 128, 16])
    gssm_bc = din("gssm_bc", [DEPTH, 128, D])
    w_rg = din("w_rgate", [DEPTH, 8, 128, 128])
    w_ig = din("w_igate", [DEPTH, 8, 128, 128])
    brT = din("brT", [DEPTH, 128, 8])
    biT = din("biT", [DEPTH, 128, 8])
    lamT = din("lamT", [DEPTH, 128, 8])

    yp = dout("yp", [NPS, SEQ, D])
    ys = dout("ys", [ST, D])
    pkv = [dout("pkv1", [DEPTH, NPS, 128, 1024]), dout("pkv2", [DEPTH, NPS, 512, 1024]),
           dout("pkv3", [DEPTH, NPS, 2048, 1024])]
    skv = [dout("skv1", [DEPTH, NSS, DL, 1024]), dout("skv2", [DEPTH, NSS, DL, 1024]),
           dout("skv3", [DEPTH, NSS, DL, 1024])]
    o_cb = dout("o_cb", [DEPTH, NSEQ, 128, 12, 3])
    o_ssm = dout("o_ssm", [DEPTH, NSEQ, 1024, 128])
    o_cc = dout("o_cc", [DEPTH, NSEQ, 128, 8, 3])
    o_lru = dout("o_lru", [DEPTH, NSEQ, 128, 8])

    xres = nc.dram_tensor("xres", [NPS * SEQ + ST, D], F32).ap()
    gsc = nc.dram_tensor("gsc", [NSEQ, 3 * D], F32).ap()
    def scr(name, shape):
        return nc.dram_tensor(name, list(shape), BF16).ap()
    S_ffi = [[scr(f"S_ffi{l}_{w}", [44, 128, KC, 128]) for w in range(2)] for l in range(DEPTH)]
    S_ffo = [[scr(f"S_ffo{l}_{w}", [4, 128, FC, 256]) for w in range(2)] for l in range(DEPTH)]
    S_inA = [scr(f"S_inA{l}", [56, 128, KC, 128]) for l in range(DEPTH)]
    S_dt = [scr(f"S_dt{l}", [128, KC, 16]) for l in range(DEPTH)]
    S_inB = [scr(f"S_inB{l}", [40, 128, KC, 128]) for l in range(DEPTH)]
    S_ap = [scr(f"S_ap{l}", [8, 128, 4, 128]) for l in range(DEPTH)]
    S_bp = [scr(f"S_bp{l}", [8, 128, KC, 128]) for l in range(DEPTH)]
    S_cp = [scr(f"S_cp{l}", [8, 128, KC, 128]) for l in range(DEPTH)]
    S_o = [scr(f"S_o{l}", [4, 128, KC, 256]) for l in range(DEPTH)]

    def inA(l, col):
        assert col % 128 == 0 and col < 7168
        return S_inA[l][col // 128]

    def inB(l, col):
        assert (col - 7184) % 128 == 0 and col >= 7184
        return S_inB[l][(col - 7184) // 128]
    b_gsc = Buf()
    xres_b = [[Buf() for _ in range(2)] for _ in range(NPS)] + [[Buf()]]

    def sb(name, shape, dt=F32):
        return nc.alloc_sbuf_tensor(name, list(shape), dt)

    uid = [0]

    def TMP(es, name, shape, dt=F32):
        uid[0] += 1
        return es.enter_context(nc.sbuf_tensor(f"{name}_{uid[0]}", list(shape), dt))

    identf = sb("identf", [128, 128]); ident = sb("ident", [128, 128], BF16)
    tri = sb("tri", [128, 128])
    negtri = sb("negtri", [128, 128])
    ones_f = sb("ones_f", [128, 128])
    ones_b = sb("ones_b", [128, 128], BF16)
    epsb = sb("epsb", [128, 1]); oneb = sb("oneb", [128, 1])
    E = sb("E", [128, 24, 256], BF16)
    csil = sb("csil", [128, KC, NSEQ], BF16)
    modT = sb("modT", [128, 72, NSEQ])
    modA = sb("modA", [128, 3, KC, NSEQ]); modB = sb("modB", [128, 3, KC, NSEQ])
    gbc = sb("gbc", [128, 3, D])
    gn_sb = sb("gn_sb", [128, 3, KC])
    cbw = sb("cbw", [128, 12, 4]); cbb = sb("cbb", [128, 12]); ccw = sb("ccw", [128, 8, 4]); ccb = sb("ccb", [128, 8])
    dtb_sb = sb("dtb_sb", [128, 16]); aneg_sb = sb("aneg_sb", [128, 16]); dsk_sb = sb("dsk_sb", [128, 16])
    br_sb = sb("br_sb", [128, 8]); bi_sb = sb("bi_sb", [128, 8]); cneg_sb = sb("cneg_sb", [128, 8])
    b_const, b_E, b_mod, b_lay, b_gbc, b_AB = Buf(), Buf(), Buf(), Buf(), Buf(), Buf()

    PS = [nc.alloc_psum_tensor(f"ps{i}", [128, 512], F32) for i in range(8)]
    PSB = [Buf() for _ in range(8)]
    ps_i = [0]

    ps_lim = [8]

    def next_ps():
        i = ps_i[0] % ps_lim[0]
        ps_i[0] = (i + 1) % ps_lim[0]
        return PS[i], PSB[i]

    def bf(ps):
        return ps[:].bitcast(BF16)

    k.op("pool", lambda e: e.memset(identf[:], 1.0), writes=[b_const])
    k.op("pool", lambda e: e.affine_select(out=identf[:], in_=identf[:], pattern=[[-1, 128]], compare_op=ALU.is_equal,
                                           fill=0.0, base=0, channel_multiplier=1), reads=[b_const], writes=[b_const])
    k.op("pool", lambda e: e.memset(tri[:], 1.0), writes=[b_const])
    k.op("pool", lambda e: e.affine_select(out=tri[:], in_=tri[:], pattern=[[1, 128]], compare_op=ALU.is_ge,
                                           fill=0.0, base=0, channel_multiplier=-1), reads=[b_const], writes=[b_const])
    k.op("pool", lambda e: e.memset(negtri[:], 0.0), writes=[b_const])
    k.op("pool", lambda e: e.affine_select(out=negtri[:], in_=negtri[:], pattern=[[1, 128]], compare_op=ALU.is_ge,
                                           fill=-30000.0, base=0, channel_multiplier=-1), reads=[b_const], writes=[b_const])
    k.op("dve", lambda e: e.memset(ones_f[:], 1.0), writes=[b_const])
    k.op("dve", lambda e: e.memset(ones_b[:], 1.0), writes=[b_const])
    k.op("dve", lambda e: e.memset(epsb[:], EPS), writes=[b_const])
    k.op("dve", lambda e: e.memset(oneb[:], 1.0), writes=[b_const])
    k.op("dve", lambda e: e.tensor_copy(out=ident[:], in_=identf[:]), reads=[b_const], writes=[b_const])

    with ExitStack() as es:
        stg = TMP(es, "stg", [128, 8, 256], F32)
        msk = TMP(es, "msk", [128, 256], F32)
        cst = TMP(es, "cst", [128, KC, NSEQ], F32)
        b_stg = Buf()
        k.dma("sp", msk[:], emask, writes=[b_stg])
        for i in range(3):
            k.dma("sp", stg[:], ebias[:, i * 8:(i + 1) * 8, :], writes=[b_stg])
            k.op("act", lambda e: e.activation(out=stg[:], in_=stg[:], func=AF.Exp), reads=[b_stg], writes=[b_stg])
            k.op("dve", lambda e: e.tensor_tensor(out=E[:, i * 8:(i + 1) * 8, :], in0=stg[:],
                                                  in1=msk[:].unsqueeze(1).to_broadcast([128, 8, 256]), op=ALU.mult),
                 reads=[b_stg], writes=[b_E])
        k.dma("sp", cst[:], cT, writes=[b_stg])
        k.op("act", lambda e: e.activation(out=csil[:], in_=cst[:], func=AF.Silu), reads=[b_stg], writes=[b_mod])
        k.barrier()

    def wload(dst, src, buf):
        if src.dtype == BF16:
            k.dma("sp", dst, src, writes=[buf])
        else:
            k.dma("pool", dst, src, writes=[buf])

    def precast(l):
        with ExitStack() as es:
            stg = [TMP(es, f"pc_s{i}", [128, NIN], BF16) for i in range(2)]
            bst = [Buf(), Buf()]
            cnt = [0]

            def chunk(src_rows, n, outs):
                i = cnt[0] % 2
                cnt[0] += 1
                k.dma("pool", stg[i][:, 0:n], src_rows, writes=[bst[i]])
                for (dst, c0, nb, bw) in outs:
                    b0 = 0
                    while b0 < nb:
                        nn = min(16, nb - b0)
                        k.dma("sp", dst[:, b0:b0 + nn, :],
                              stg[i][:, c0 + b0 * bw:c0 + (b0 + nn) * bw].rearrange("p (b n) -> p b n", n=bw), reads=[bst[i]])
                        b0 += nn

            for w in range(2):
                for c in range(KC):
                    chunk(w_ffi[w][l][c * 128:(c + 1) * 128, :], 2 * DFF,
                          [(S_ffi[l][w][:, :, c, :].rearrange("b p n -> p b n"), 0, 44, 128)])
                for j in range(FC):
                    chunk(w_ffo[w][l][j * 128:(j + 1) * 128, :], D,
                          [(S_ffo[l][w][:, :, j, :].rearrange("q p n -> p q n"), 0, 4, 256)])
            for c in range(KC):
                chunk(w_in[l][c * 128:(c + 1) * 128, :], NIN,
                      [(S_inA[l][:, :, c, :].rearrange("b p n -> p b n"), 0, 56, 128),
                       (S_dt[l][:, c:c + 1, :], 7168, 1, 16),
                       (S_inB[l][:, :, c, :].rearrange("b p n -> p b n"), 7184, 40, 128)])
            for c in range(4):
                chunk(w_ap[l][c * 128:(c + 1) * 128, :], D, [(S_ap[l][:, :, c, :].rearrange("b p n -> p b n"), 0, 8, 128)])
            for c in range(KC):
                chunk(w_bp[l][c * 128:(c + 1) * 128, :], D, [(S_bp[l][:, :, c, :].rearrange("b p n -> p b n"), 0, 8, 128)])
                chunk(w_cp[l][c * 128:(c + 1) * 128, :], D, [(S_cp[l][:, :, c, :].rearrange("b p n -> p b n"), 0, 8, 128)])
                chunk(w_o[l][c * 128:(c + 1) * 128, :], D, [(S_o[l][:, :, c, :].rearrange("q p n -> p q n"), 0, 4, 256)])
            k.barrier()

    def ada(l):
        with ExitStack() as es:
            wa = TMP(es, "wa_full", [128, KC, 9 * D], BF16)
            bwa = [Buf() for _ in range(KC)]
            badT = TMP(es, "badT", [128, 72], F32)
            badG = TMP(es, "badG", [NSEQ, 3 * D], F32)
            lam_t = TMP(es, "lam_t", [128, 8], F32)
            modrows = TMP(es, "modrows", [NSEQ, 3 * D], F32)
            b_mr = Buf()
            b_t = Buf()
            k.dma("sp", badT[:], b_adaT[l], writes=[b_t])
            k.dma("sp", badG[:], b_adaG[l], writes=[b_t])
            for c in range(KC):
                k.dma("pool", wa[:, c, :], w_ada[l][c * 128:(c + 1) * 128, :], writes=[bwa[c]])
            ps, pb = next_ps()
            for j in range(72):
                for c in range(KC):
                    k.op("pe", lambda e: e.matmul(ps[:, j * NSEQ:(j + 1) * NSEQ], lhsT=wa[:, c, j * 128:(j + 1) * 128],
                                                  rhs=csil[:, c, :], start=(c == 0), stop=(c == KC - 1)),
                         reads=[bwa[c], b_mod], writes=[pb])
            for gi in range(3):
                for hf in range(2):
                    ps2, pb2 = next_ps()
                    col0 = (3 * gi + 2) * D + hf * 512
                    for c in range(KC):
                        k.op("pe", lambda e: e.matmul(ps2[0:NSEQ, :], lhsT=csil[:, c, :], rhs=wa[:, c, col0:col0 + 512],
                                                      start=(c == 0), stop=(c == KC - 1)),
                             reads=[bwa[c], b_mod], writes=[pb2])
                    sl = slice(gi * D + hf * 512, gi * D + (hf + 1) * 512)
                    k.op("dve", lambda e: e.tensor_tensor(out=modrows[:, sl], in0=ps2[0:NSEQ, :], in1=badG[:, sl], op=ALU.add),
                         reads=[pb2, b_t], writes=[b_mr])
            k.op("dve", lambda e: e.tensor_tensor(out=modT[:], in0=ps[:, 0:72 * NSEQ].rearrange("p (j s) -> p j s", s=NSEQ),
                                                  in1=badT[:].unsqueeze(2).to_broadcast([128, 72, NSEQ]), op=ALU.add),
                 reads=[pb, b_t], writes=[b_mod])
            k.dma("sp", gsc, modrows[:], reads=[b_mr], writes=[b_gsc])
            for i in range(3):
                k.dma("sp", gn_sb[:, i, :], gnT[i][l], writes=[b_lay])
            k.dma("sp", cbw[:], cbwT[l], writes=[b_lay]); k.dma("sp", cbb[:], cbbT[l], writes=[b_lay])
            k.dma("sp", ccw[:], ccwT[l], writes=[b_lay]); k.dma("sp", ccb[:], ccbT[l], writes=[b_lay])
            k.dma("sp", dtb_sb[:], dtb_bc[l], writes=[b_lay]); k.dma("sp", aneg_sb[:], alog_bc[l], writes=[b_lay])
            k.dma("sp", dsk_sb[:], dsk_bc[l], writes=[b_lay])
            k.dma("sp", br_sb[:], brT[l], writes=[b_lay]); k.dma("sp", bi_sb[:], biT[l], writes=[b_lay])
            k.dma("sp", lam_t[:], lamT[l], writes=[b_lay])
            k.op("act", lambda e: e.activation(out=aneg_sb[:], in_=aneg_sb[:], func=AF.Exp), reads=[b_lay], writes=[b_lay])
            k.op("dve", lambda e: e.tensor_scalar(out=aneg_sb[:], in0=aneg_sb[:], scalar1=-1.0, scalar2=None, op0=ALU.mult),
                 reads=[b_lay], writes=[b_lay])
            k.op("act", lambda e: e.activation(out=lam_t[:], in_=lam_t[:], func=AF.Exp, scale=-1.0), reads=[b_lay], writes=[b_lay])
            k.op("act", lambda e: e.activation(out=lam_t[:], in_=lam_t[:], func=AF.Ln, bias=oneb[:], scale=1.0),
                 reads=[b_lay, b_const], writes=[b_lay])
            k.op("dve", lambda e: e.tensor_scalar(out=cneg_sb[:], in0=lam_t[:], scalar1=-8.0, scalar2=None, op0=ALU.mult),
                 reads=[b_lay], writes=[b_lay])
            for i in range(3):
                sc = modT[:, (3 * i + 1) * 8:(3 * i + 2) * 8, :]
                sh = modT[:, (3 * i) * 8:(3 * i + 1) * 8, :]
                k.op("dve", lambda e: e.tensor_scalar(out=modA[:, i], in0=sc, scalar1=1.0, scalar2=None, op0=ALU.add),
                     reads=[b_mod], writes=[b_AB])
                k.op("dve", lambda e: e.tensor_tensor(out=modA[:, i], in0=modA[:, i],
                                                      in1=gn_sb[:, i, :].unsqueeze(2).to_broadcast([128, KC, NSEQ]), op=ALU.mult),
                     reads=[b_AB, b_lay], writes=[b_AB])
                k.op("dve", lambda e: e.tensor_copy(out=modB[:, i], in_=sh), reads=[b_mod], writes=[b_AB])
            k.barrier()

    def grp_info(g):
        if g < NPS:
            return dict(P=128, T=SEQ, seqs=[g], L=SEQ, row0=g * SEQ)
        return dict(P=ST, T=ST, seqs=list(range(NPS, NSEQ)), L=DL, row0=NPS * SEQ)

    def setup_group(g, l):
        gi = grp_info(g)
        nseg = len(gi["seqs"])
        seg = gi["P"] // nseg
        for si, s_ in enumerate(gi["seqs"]):
            k.dma("sp", gbc[si * seg:(si + 1) * seg].rearrange("p a d -> p (a d)"), gsc[s_].partition_broadcast(seg),
                  reads=[b_gsc], writes=[b_gbc])

    def setup_AB(g, i, Afull, Bfull, b_ab):
        gi = grp_info(g)
        P = gi["P"]
        nseg = len(gi["seqs"])
        seg = P // nseg
        for si, s in enumerate(gi["seqs"]):
            k.op("dve", lambda e: e.tensor_copy(out=Afull[:, :, si * seg:(si + 1) * seg],
                                                in_=modA[:, i, :, s:s + 1].to_broadcast([128, KC, seg])),
                 reads=[b_AB], writes=[b_ab])
            k.op("dve", lambda e: e.tensor_copy(out=Bfull[:, :, si * seg:(si + 1) * seg],
                                                in_=modB[:, i, :, s:s + 1].to_broadcast([128, KC, seg])),
                 reads=[b_AB], writes=[b_ab])

    def norm_alloc(es):
        return dict(
            Afull=TMP(es, "Afull", [128, KC, 128], F32), Bfull=TMP(es, "Bfull", [128, KC, 128], F32), b_ab=Buf(),
            ss=TMP(es, "n_ss", [128, 16], F32), junk=TMP(es, "n_junk", [128, D], F32),
            xn=[TMP(es, f"n_xn{i}", [128, D], BF16) for i in range(2)],
            tmp=[TMP(es, f"n_tmp{i}", [128, KC, 128], F32) for i in range(2)],
            bss=Buf(), bj=Buf(), bxn=[Buf(), Buf()], btmp=[Buf(), Buf()])

    def norm_to_hT(g, sub_i, xt, bx, P, nblk, hT, bh, tok0, es, ctx=None):
        if ctx is None:
            ctx = norm_alloc(es)
        Afull, Bfull, b_ab = ctx["Afull"], ctx["Bfull"], ctx["b_ab"]
        setup_AB(g, sub_i, Afull, Bfull, b_ab)
        ss, junk, xn, tmp = ctx["ss"], ctx["junk"], ctx["xn"], ctx["tmp"]
        bss, bj, bxn, btmp = ctx["bss"], ctx["bj"], ctx["bxn"], ctx["btmp"]
        for b in range(nblk):
            k.op("act", lambda e: e.activation(out=junk[0:P, :], in_=xt[0:P, b, :], func=AF.Square, accum_out=ss[0:P, b:b + 1]),
                 reads=[bx], writes=[bj, bss])
        k.op("act", lambda e: e.activation(out=ss[0:P, 0:nblk], in_=ss[0:P, 0:nblk], func=AF.Sqrt, bias=epsb[0:P, :], scale=1.0 / D),
             reads=[bss, b_const], writes=[bss])
        k.op("dve", lambda e: e.reciprocal(out=ss[0:P, 0:nblk], in_=ss[0:P, 0:nblk]), reads=[bss], writes=[bss])
        for b in range(nblk):
            x_ = xn[b % 2]; bx_ = bxn[b % 2]; t_ = tmp[b % 2]; bt_ = btmp[b % 2]
            k.op("act", lambda e: e.activation(out=x_[0:P, :], in_=xt[0:P, b, :], func=AF.Identity, scale=ss[0:P, b:b + 1]),
                 reads=[bx, bss], writes=[bx_])
            ps, pb = next_ps()
            pv = bf(ps)
            for c in range(KC):
                k.op("pe", lambda e: e.transpose(out=pv[:, c * 128:c * 128 + P], in_=x_[0:P, c * 128:(c + 1) * 128],
                                                 identity=ident[0:P, 0:P]), reads=[bx_, b_const], writes=[pb])
            pvv = pv[:, 0:KC * 128].rearrange("p (c t) -> p c t", t=128)[:, :, 0:P]
            k.op("dve", lambda e: e.tensor_tensor(out=t_[:, :, 0:P], in0=pvv, in1=Afull[:, :, 0:P], op=ALU.mult),
                 reads=[pb, b_ab], writes=[bt_])
            k.op("dve", lambda e: e.tensor_tensor(out=hT[:, :, tok0 + b * P: tok0 + (b + 1) * P], in0=t_[:, :, 0:P],
                                                  in1=Bfull[:, :, 0:P], op=ALU.add), reads=[bt_, b_ab], writes=[bh])

    def resid_update(xt_slice, bx, ps_ap, pb, P, gi_idx, col0, ncol, scale, tmp, btmp):
        k.op("dve", lambda e: e.scalar_tensor_tensor(out=tmp[0:P, 0:ncol], in0=ps_ap, scalar=scale,
                                                     in1=gbc[0:P, gi_idx, col0:col0 + ncol], op0=ALU.mult, op1=ALU.mult),
             reads=[pb, b_gbc], writes=[btmp])
        k.op("dve", lambda e: e.tensor_tensor(out=xt_slice, in0=xt_slice, in1=tmp[0:P, 0:ncol], op=ALU.add),
             reads=[btmp, bx], writes=[bx])

    def ff(l, which, g, src_rows, first):
        gi = grp_info(g)
        P, T = gi["P"], gi["T"]
        TT = min(T, 1024)
        ntile = T // TT
        NB = TT // P
        SUB = min(TT, 512)
        NS = TT // SUB
        sub_i = 0 if which == 0 else 2
        wi = w_ffi[which][l].rearrange("(c p) n -> p c n", p=128)
        wo = w_ffo[which][l].rearrange("(j p) n -> p j n", p=128)
        with ExitStack() as es:
            xt = TMP(es, "f_x", [128, NB, D], F32)
            hT = TMP(es, "f_hT", [128, KC, TT], BF16)
            actT = TMP(es, "f_act", [128, FC, TT], BF16)
            bx, bh, bact = Buf(), Buf(), Buf()
            bw = [(Buf(), Buf()) for _ in range(3)]; bwo = [Buf(), Buf()]; bsu = [Buf(), Buf()]; brt = [Buf(), Buf()]
            wblk = [TMP(es, f"f_w{i}", [128, 2, KC, 128], BF16) for i in range(3)]
            wob = [TMP(es, f"f_wo{i}", [128, FC, 256], BF16) for i in range(2)]
            su = [TMP(es, f"f_su{i}", [128, 512], F32) for i in range(2)]
            rt = [TMP(es, f"f_rt{i}", [128, 256], F32) for i in range(2)]
            nctx = norm_alloc(es)
            for ti in range(ntile):
                xb = xres_b[g][ti]
                r0 = gi["row0"] + ti * TT
                if first:
                    src = src_rows[ti * TT:(ti + 1) * TT, :]
                    k.dma("pool", xt[0:P], src.rearrange("(b p) d -> p b d", p=P), writes=[bx])
                else:
                    k.dma("pool", xt[0:P], xres[r0:r0 + TT, :].rearrange("(b p) d -> p b d", p=P), reads=[xb], writes=[bx])
                norm_to_hT(g, sub_i, xt, bx, P, NB, hT, bh, 0, None, nctx)
                cnt = 0
                for j in range(FC):
                    w = wblk[j % 3]; bw_ = bw[j % 3]
                    wload(w[:, 0], S_ffi[l][which][j], bw_[0])
                    wload(w[:, 1], S_ffi[l][which][FC + j], bw_[1])
                    for s in range(NS):
                        psu, pbu = next_ps()
                        psv, pbv = next_ps()
                        tsl = slice(s * SUB, (s + 1) * SUB)
                        for c in range(KC):
                            k.op("pe", lambda e: e.matmul(psu[:, 0:SUB], lhsT=w[:, 0, c, :], rhs=hT[:, c, tsl],
                                                          start=(c == 0), stop=(c == KC - 1)), reads=[bw_[0], bh], writes=[pbu])
                        for c in range(KC):
                            k.op("pe", lambda e: e.matmul(psv[:, 0:SUB], lhsT=w[:, 1, c, :], rhs=hT[:, c, tsl],
                                                          start=(c == 0), stop=(c == KC - 1)), reads=[bw_[1], bh], writes=[pbv])
                        s_ = su[cnt % 2]; bs_ = bsu[cnt % 2]; cnt += 1
                        k.op("act", lambda e: e.activation(out=s_[:, 0:SUB], in_=psu[:, 0:SUB], func=AF.Silu),
                             reads=[pbu], writes=[bs_])
                        k.op("dve", lambda e: e.tensor_tensor(out=actT[:, j, tsl], in0=s_[:, 0:SUB], in1=psv[:, 0:SUB], op=ALU.mult),
                             reads=[bs_, pbv], writes=[bact])
                cnt = 0
                for q in range(4):
                    w = wob[q % 2]; bw_ = bwo[q % 2]
                    wload(w[:], S_ffo[l][which][q], bw_)
                    for b in range(NB):
                        ps, pb = next_ps()
                        for j in range(FC):
                            k.op("pe", lambda e: e.matmul(ps[0:P, 0:256], lhsT=actT[:, j, b * P:(b + 1) * P], rhs=w[:, j, :],
                                                          start=(j == 0), stop=(j == FC - 1)), reads=[bact, bw_], writes=[pb])
                        resid_update(xt[0:P, b, q * 256:(q + 1) * 256], bx, ps[0:P, 0:256], pb, P, sub_i, q * 256, 256, 0.5,
                                     rt[cnt % 2], brt[cnt % 2])
                        cnt += 1
                k.dma("pool", xres[r0:r0 + TT, :].rearrange("(b p) d -> p b d", p=P), xt[0:P], reads=[bx], writes=[xb])
            k.barrier()

    def final_norm(g, dst_rows):
        gi = grp_info(g)
        P, T = gi["P"], gi["T"]
        TT = min(T, 1024)
        NB = TT // P
        for ti in range(T // TT):
            with ExitStack() as es:
                xt = TMP(es, "fn_x", [128, NB, D], F32)
                ss = TMP(es, "fn_ss", [128, 16], F32)
                junk = TMP(es, "fn_junk", [128, D], F32)
                gfin_sb = TMP(es, "gfin_sb", [128, D], F32)
                k.dma("sp", gfin_sb[:], gfin, writes=[b_lay])
                bx, bss, bj = Buf(), Buf(), Buf()
                r0 = gi["row0"] + ti * TT
                k.dma("sp", xt[0:P], xres[r0:r0 + TT, :].rearrange("(b p) d -> p b d", p=P), reads=[xres_b[g][ti]], writes=[bx])
                for b in range(NB):
                    k.op("act", lambda e: e.activation(out=junk[0:P, :], in_=xt[0:P, b, :], func=AF.Square, accum_out=ss[0:P, b:b + 1]),
                         reads=[bx], writes=[bj, bss])
                k.op("act", lambda e: e.activation(out=ss[0:P, 0:NB], in_=ss[0:P, 0:NB], func=AF.Sqrt, bias=epsb[0:P, :], scale=1.0 / D),
                     reads=[bss, b_const], writes=[bss])
                k.op("dve", lambda e: e.reciprocal(out=ss[0:P, 0:NB], in_=ss[0:P, 0:NB]), reads=[bss], writes=[bss])
                for b in range(NB):
                    k.op("dve", lambda e: e.scalar_tensor_tensor(out=xt[0:P, b, :], in0=xt[0:P, b, :], scalar=ss[0:P, b:b + 1],
                                                                 in1=gfin_sb[0:P, :], op0=ALU.mult, op1=ALU.mult),
                         reads=[bx, bss, b_lay], writes=[bx])
                k.dma("sp", dst_rows[ti * TT:(ti + 1) * TT, :].rearrange("(b p) d -> p b d", p=P), xt[0:P], reads=[bx])
                k.barrier()


    def pg_alloc(es, nck, nw=2):
        return dict(
            wps=[TMP(es, f"pg_wp{i}", [128, nck, 128], BF16) for i in range(nw)],
            wgs=[TMP(es, f"pg_wg{i}", [128, KC, 128], BF16) for i in range(nw)],
            sg=[TMP(es, f"pg_sg{i}", [128, 512], F32) for i in range(2)],
            bwp=[Buf() for _ in range(nw)], bwg=[Buf() for _ in range(nw)], bsg=[Buf(), Buf()])

    def proj_gate_merge(l, bi, actf, nck, wproj, hT, bact, mT, bm, tok0, ntok, first, pg, bh=None):
        wps, wgs, sg, bwp, bwg, bsg = pg["wps"], pg["wgs"], pg["sg"], pg["bwp"], pg["bwg"], pg["bsg"]
        tsl = slice(tok0, tok0 + ntok)
        for o in range(8):
            nw = len(wps)
            wp, wg = wps[o % nw], wgs[o % nw]
            wload(wp[:], wproj[o], bwp[o % nw])
            wload(wg[:], inB(l, O_GATES + bi * D + o * 128), bwg[o % nw])
            psy, pby = next_ps()
            psg, pbg = next_ps()
            for c in range(nck):
                k.op("pe", lambda e: e.matmul(psy[:, 0:ntok], lhsT=wp[:, c, :], rhs=actf(c), start=(c == 0), stop=(c == nck - 1)),
                     reads=[bwp[o % nw], bact], writes=[pby])
            for c in range(KC):
                k.op("pe", lambda e: e.matmul(psg[:, 0:ntok], lhsT=wg[:, c, :], rhs=hT[:, c, tsl], start=(c == 0), stop=(c == KC - 1)),
                     reads=[bwg[o % nw]] + ([bh] if bh is not None else []), writes=[pbg])
            s_, bs_ = sg[o % 2], bsg[o % 2]
            k.op("act", lambda e: e.activation(out=s_[:, 0:ntok], in_=psg[:, 0:ntok], func=AF.Sigmoid), reads=[pbg], writes=[bs_])
            if first:
                k.op("dve", lambda e: e.tensor_tensor(out=mT[:, o, tsl], in0=s_[:, 0:ntok], in1=psy[:, 0:ntok], op=ALU.mult),
                     reads=[bs_, pby], writes=[bm])
            else:
                k.op("dve", lambda e: e.tensor_tensor(out=s_[:, 0:ntok], in0=s_[:, 0:ntok], in1=psy[:, 0:ntok], op=ALU.mult),
                     reads=[bs_, pby], writes=[bs_])
                k.op("pool", lambda e: e.tensor_tensor(out=mT[:, o, tsl], in0=mT[:, o, tsl], in1=s_[:, 0:ntok], op=ALU.add),
                     reads=[bs_, bm], writes=[bm])

    def attn_prompt(l, g, hT, bh, mT, bm):
        T = SEQ
        wi_l = w_in[l].rearrange("(c p) n -> p c n", p=128)
        with ExitStack() as es:
            oaT = TMP(es, "a_oaT", [128, 4, T], BF16); boa = Buf()
            shift = TMP(es, "a_shift", [64, 128], BF16); bsh = Buf()
            accT = TMP(es, "a_acc", [64, 2, T], F32); bacc = Buf()
            accZ = TMP(es, "a_accz", [1, 2, T], F32); baz = Buf()
            QT = [TMP(es, f"a_q{i}", [128, T], BF16) for i in range(2)]; bq = [Buf(), Buf()]
            KTt = [TMP(es, f"a_k{i}", [128, T], BF16) for i in range(2)]; bk = [Buf(), Buf()]
            Vt = [TMP(es, f"a_v{i}", [128, 16, 128], BF16) for i in range(2)]; bv = [Buf(), Buf()]
            wq = [TMP(es, f"a_wq{i}", [128, KC, 128], BF16) for i in range(2)]; bwq = [Buf(), Buf()]
            wk = [TMP(es, f"a_wk{i}", [128, KC, 128], BF16) for i in range(2)]; bwk = [Buf(), Buf()]
            wkv = [TMP(es, f"a_wkv{i}", [128, KC, 256], BF16) for i in range(2)]; bwkv = [Buf(), Buf()]
            PT = [TMP(es, f"a_pt{i}", [128, 256], BF16) for i in range(4)]; bpt = [Buf() for _ in range(4)]
            ex = [TMP(es, f"a_ex{i}", [128, 256], F32) for i in range(2)]; bex = [Buf(), Buf()]
            kst = [TMP(es, f"a_kst{i}", [128, 256], F32) for i in range(2)]; bkst = [Buf(), Buf()]
            oan = TMP(es, "a_oan", [64, 2, 512], BF16); boan = Buf()
            rz = TMP(es, "a_rz", [1, 2, 512], F32); brz = Buf()
            k.op("dve", lambda e: e.memset(shift[:], 0.0), writes=[bsh])
            k.op("dve", lambda e: e.tensor_copy(out=shift[:, 64:128], in_=ident[0:64, 0:64]), reads=[b_const], writes=[bsh])
            cnt = 0
            kcnt = 0
            it = 0
            for hp in range(A_HP):
                for gq, (win, d) in enumerate(GROUPS):
                    if gq not in A_GQS:
                        continue
                    i2 = it % 2
                    it += 1
                    m = T // d
                    nb = m // 128
                    keep = min(win, T)
                    cq = O_Q + gq * 512 + hp * 128
                    ck = O_K + gq * 512 + hp * 128
                    cv = O_V + gq * 512 + hp * 128
                    wload(wq[i2][:], inA(l, cq), bwq[i2])
                    wload(wk[i2][:], inA(l, ck), bwk[i2])
                    wload(wkv[i2][:, :, 0:128], inA(l, ck), bwkv[i2])
                    wload(wkv[i2][:, :, 128:256], inA(l, cv), bwkv[i2])
                    for s in range(4):
                        sub = slice(s * 512, (s + 1) * 512)
                        for (wt, bw_, dst, bd) in ((wq[i2], bwq[i2], QT[i2], bq[i2]), (wk[i2], bwk[i2], KTt[i2], bk[i2])):
                            ps, pb = next_ps()
                            for c in range(KC):
                                k.op("pe", lambda e: e.matmul(ps[:, :], lhsT=wt[:, c, :], rhs=hT[:, c, sub], start=(c == 0), stop=(c == KC - 1)),
                                     reads=[bw_, bh], writes=[pb])
                            k.op("act", lambda e: e.activation(out=dst[:, sub], in_=ps[:, :], func=AF.Copy), reads=[pb], writes=[bd])
                    for r in range(d if A_STAGE >= 2 else 0):
                        for kb in range(nb):
                            blk = r * nb + kb
                            t0 = r + kb * 128 * d
                            tsl = slice(t0, t0 + 127 * d + 1, d)
                            need_k = (kb * 128 * d >= T - keep)
                            c0 = 0 if need_k else 128
                            ps, pb = next_ps()
                            for c in range(KC):
                                k.op("pe", lambda e: e.matmul(ps[:, c0:256], lhsT=hT[:, c, tsl], rhs=wkv[i2][:, c, c0:256],
                                                              start=(c == 0), stop=(c == KC - 1)), reads=[bwkv[i2], bh], writes=[pb])
                            k.op("act", lambda e: e.activation(out=Vt[i2][:, blk, :], in_=ps[:, 128:256], func=AF.Copy),
                                 reads=[pb], writes=[bv[i2]])
                            if need_k and not int(os.environ.get("MK_NOPKV", "0")):
                                ks_, bks_ = kst[kcnt % 2], bkst[kcnt % 2]
                                kcnt += 1
                                k.op("act", lambda e: e.activation(out=ks_[:], in_=ps[:, 0:256], func=AF.Copy), reads=[pb], writes=[bks_])
                                row0 = t0 - (T - keep)
                                i0 = (row0 - r) // d
                                dst = pkv[gq][l, g].rearrange("(i dd) (a x) -> dd i a x", dd=d, a=2)[r, i0:i0 + 128, :, hp * 128:(hp + 1) * 128]
                                dbg = int(os.environ.get("MK_DBG", "0"))
                                if dbg == 1:
                                    pass
                                elif dbg == 2:
                                    dst2 = pkv[gq][l, g, 0:128, :]
                                    k.dma("sp", dst2[:, hp * 128:(hp + 1) * 128], ks_[:, 0:128], reads=[bks_])
                                    k.dma("sp", dst2[:, 512 + hp * 128:512 + (hp + 1) * 128], ks_[:, 128:256], reads=[bks_])
                                else:
                                    k.dma("sp", dst[:, 0, :], ks_[:, 0:128], reads=[bks_])
                                    k.dma("sp", dst[:, 1, :], ks_[:, 128:256], reads=[bks_])
                    for h2 in range(2 if A_STAGE >= 3 else 0):
                        hg = gq * 8 + hp * 2 + h2
                        psl = slice(h2 * 64, (h2 + 1) * 64)
                        for r in range(d):
                            ptprev = None
                            for kb in range(nb):
                                nq = 256 if kb < nb - 1 else 128
                                t0 = r + kb * 128 * d
                                ksl = slice(t0, t0 + 127 * d + 1, d)
                                qsl = slice(t0, t0 + (nq - 1) * d + 1, d)
                                ps, pb = next_ps()
                                k.op("pe", lambda e: e.matmul(ps[:, 0:nq], lhsT=KTt[i2][psl, ksl], rhs=QT[i2][psl, qsl], start=True, stop=True),
                                     reads=[bk[i2], bq[i2]], writes=[pb])
                                ex_, bex_ = ex[cnt % 2], bex[cnt % 2]
                                pt, bpt_ = PT[cnt % 4], bpt[cnt % 4]
                                cnt += 1
                                k.op("act", lambda e: e.activation(out=ex_[:, 0:nq], in_=ps[:, 0:nq], func=AF.Exp, scale=0.125),
                                     reads=[pb], writes=[bex_])
                                k.op("dve", lambda e: e.tensor_tensor(out=pt[:, 0:nq], in0=ex_[:, 0:nq], in1=E[:, hg, 0:nq], op=ALU.mult),
                                     reads=[bex_, b_E], writes=[bpt_])
                                ps2, pb2 = next_ps()
                                ps3, pb3 = next_ps()
                                if kb > 0:
                                    pp, bpp = ptprev
                                    k.op("pe", lambda e: e.matmul(ps2[0:64, 0:128], lhsT=Vt[i2][:, r * nb + kb - 1, h2 * 64:(h2 + 1) * 64],
                                                                  rhs=pp[:, 128:256], start=True, stop=False), reads=[bv[i2], bpp], writes=[pb2])
                                    k.op("pe", lambda e: e.matmul(ps3[0:1, 0:128], lhsT=ones_b[:, 0:1], rhs=pp[:, 128:256], start=True, stop=False),
                                         reads=[b_const, bpp], writes=[pb3])
                                k.op("pe", lambda e: e.matmul(ps2[0:64, 0:128], lhsT=Vt[i2][:, r * nb + kb, h2 * 64:(h2 + 1) * 64],
                                                              rhs=pt[:, 0:128], start=(kb == 0), stop=True), reads=[bv[i2], bpt_], writes=[pb2])
                                k.op("pe", lambda e: e.matmul(ps3[0:1, 0:128], lhsT=ones_b[:, 0:1], rhs=pt[:, 0:128], start=(kb == 0), stop=True),
                                     reads=[b_const, bpt_], writes=[pb3])
                                adst = accT[0:64, h2, ksl]
                                zdst = accZ[0:1, h2, ksl]
                                if gq == 0:
                                    k.op("act", lambda e: e.activation(out=adst, in_=ps2[0:64, 0:128], func=AF.Copy), reads=[pb2], writes=[bacc])
                                    k.op("act", lambda e: e.activation(out=zdst, in_=ps3[0:1, 0:128], func=AF.Copy), reads=[pb3], writes=[baz])
                                else:
                                    k.op("dve", lambda e: e.tensor_tensor(out=adst, in0=adst, in1=ps2[0:64, 0:128], op=ALU.add),
                                         reads=[pb2, bacc], writes=[bacc])
                                    k.op("dve", lambda e: e.tensor_tensor(out=zdst, in0=zdst, in1=ps3[0:1, 0:128], op=ALU.add),
                                         reads=[pb3, baz], writes=[baz])
                                ptprev = (pt, bpt_)
                for s in range(4 if A_STAGE >= 4 else 0):
                    sub = slice(s * 512, (s + 1) * 512)
                    k.op("dve", lambda e: e.reciprocal(out=rz[0:1, :, :], in_=accZ[0:1, :, sub]), reads=[baz], writes=[brz])
                    for h2 in range(2):
                        ps, pb = next_ps()
                        k.op("pe", lambda e: e.matmul(ps[0:64, :], lhsT=ones_f[0:1, 0:64], rhs=rz[0:1, h2, :], start=True, stop=True),
                             reads=[brz, b_const], writes=[pb])
                        k.op("dve", lambda e: e.tensor_tensor(out=oan[:, h2, :], in0=accT[0:64, h2, sub], in1=ps[0:64, :], op=ALU.mult),
                             reads=[pb, bacc], writes=[boan])
                    ps, pb = next_ps()
                    k.op("pe", lambda e: e.matmul(ps[:, :], lhsT=ident[0:64, :], rhs=oan[:, 0, :], start=True, stop=False),
                         reads=[boan, b_const], writes=[pb])
                    k.op("pe", lambda e: e.matmul(ps[:, :], lhsT=shift[:, :], rhs=oan[:, 1, :], start=False, stop=True),
                         reads=[boan, bsh], writes=[pb])
                    k.op("act", lambda e: e.activation(out=oaT[:, hp, sub], in_=ps[:, :], func=AF.Copy), reads=[pb], writes=[boa])
            k.barrier()
            with ExitStack() as es2:
                pg = pg_alloc(es2, 4, 3)
                for s in range(4):
                    proj_gate_merge(l, 0, lambda c: oaT[:, c, s * 512:(s + 1) * 512], 4, S_ap[l], hT, boa, mT, bm, s * 512, 512, True, pg, bh=bh)
                k.barrier()


    def ssd(l, g, s, tok0, L, hT, bh, mT, bm):
        prompt = g < NPS
        SBT = min(L, 256)
        CH = min(L, 128)
        NCHK = SBT // CH
        wi_l = w_in[l].rearrange("(c p) n -> p c n", p=128)
        with ExitStack() as es:
            xpad = TMP(es, "s_xpad", [128, 12, SBT + 3], F32); bxp = [Buf() for _ in range(12)]
            xa = TMP(es, "s_xa", [128, 12, SBT], BF16); bxa = Buf()
            szT = TMP(es, "s_szT", [128, 8, SBT], BF16); bsz = Buf()
            ynT = TMP(es, "s_ynT", [128, 8, SBT], BF16); byn = Buf()
            S = TMP(es, "s_S", [128, 1024], F32); bS = Buf()
            Sb = TMP(es, "s_Sb", [128, 1024], BF16); bSb = Buf()
            wx = [TMP(es, f"s_wx{i}", [128, KC, 128], BF16) for i in range(4)]; bwx = [Buf() for _ in range(4)]
            wdt = TMP(es, "s_wdt", [128, KC, 16], BF16); bwdt = Buf()
            gss = TMP(es, "s_gss", [128, D], F32); bgss = Buf()
            X = TMP(es, "s_X", [128, 16, CH], F32); bX = Buf()
            ea = TMP(es, "s_ea", [128, 16, CH], BF16); bea = Buf()
            dec = TMP(es, "s_dec", [128, 16, CH], BF16); bdec = Buf()
            MT = TMP(es, "s_MT", [128, 16, CH], BF16); bMT = Buf()
            Cs = TMP(es, "s_Cs", [128, 16, CH], BF16); bCs = Buf()
            xsD = TMP(es, "s_xsD", [128, 1024], F32); bxsD = Buf()
            xdt = TMP(es, "s_xdt", [128, 1024], BF16); bxdt = Buf()
            xdtE = TMP(es, "s_xdtE", [128, 1024], BF16); bxdtE = Buf()
            Btok = TMP(es, "s_Btok", [128, 2, 128], BF16); bBt = Buf()
            cbs = TMP(es, "s_cbs", [128, 2, CH], BF16); bcbs = Buf()
            y1 = TMP(es, "s_y1", [128, 1024], F32); by1 = Buf()
            yn = TMP(es, "s_yn", [128, 1024], BF16); byn2 = Buf()
            junk = TMP(es, "s_junk", [128, 512], F32); bjk = Buf()
            cvt = [TMP(es, f"s_cv{i}", [128, SBT], F32) for i in range(2)]; bcv = [Buf(), Buf()]
            sm = TMP(es, "s_sm", [128, 8, 16], F32); bsm = Buf(); b_dt, b_dta, b_at, b_al, b_toe, b_cd, b_tm, b_ssq = [Buf() for _ in range(8)]
            stin = TMP(es, "s_stin", [128, 8, 128], F32); bstin = Buf()
            pg = pg_alloc(es, 8, 3)
            k.dma("sp", gss[:], gssm_bc[l], writes=[bgss])
            wload(wdt[:], S_dt[l], bwdt)
            if prompt:
                k.op("pool", lambda e: e.memset(xpad[:, :, 0:3], 0.0), writes=bxp)
                k.op("pool", lambda e: e.memset(S[:], 0.0), writes=[bS])
                k.op("pool", lambda e: e.memset(Sb[:], 0.0), writes=[bSb])
            else:
                si = s - NPS
                k.dma("sp", xpad[:, :, 0:3], st_cb[l, si], writes=bxp)
                k.dma("sp", stin[:], st_ssm[l, si].rearrange("(a p) n -> p a n", p=128), writes=[bstin])
                for a in range(8):
                    ps, pb = next_ps()
                    k.op("pe", lambda e: e.transpose(out=ps[:, 0:128], in_=stin[:, a, :], identity=identf[:]), reads=[bstin, b_const], writes=[pb])
                    k.op("act", lambda e: e.activation(out=S[:, a * 128:(a + 1) * 128], in_=ps[:, 0:128], func=AF.Copy), reads=[pb], writes=[bS])
                k.op("act", lambda e: e.activation(out=Sb[:], in_=S[:], func=AF.Copy), reads=[bS], writes=[bSb])
            cvc = 0
            wc = 0
            for st in range(L // SBT):
                ts0 = tok0 + st * SBT
                tsub = slice(ts0, ts0 + SBT)
                for fc in range(8):
                    w, bw_ = wx[wc % 4], bwx[wc % 4]; wc += 1
                    wload(w[:], inA(l, O_Z + fc * 128), bw_)
                    ps, pb = next_ps()
                    for c in range(KC):
                        k.op("pe", lambda e: e.matmul(ps[:, 0:SBT], lhsT=w[:, c, :], rhs=hT[:, c, tsub], start=(c == 0), stop=(c == KC - 1)),
                             reads=[bw_, bh], writes=[pb])
                    k.op("act", lambda e: e.activation(out=szT[:, fc, :], in_=ps[:, 0:SBT], func=AF.Silu), reads=[pb], writes=[bsz])
                for fc in range(12):
                    w, bw_ = wx[wc % 4], bwx[wc % 4]; wc += 1
                    wload(w[:], inA(l, O_XBC + fc * 128), bw_)
                    ps, pb = next_ps()
                    for c in range(KC):
                        k.op("pe", lambda e: e.matmul(ps[:, 0:SBT], lhsT=w[:, c, :], rhs=hT[:, c, tsub], start=(c == 0), stop=(c == KC - 1)),
                             reads=[bw_, bh], writes=[pb])
                    k.op("act", lambda e: e.activation(out=xpad[:, fc, 3:3 + SBT], in_=ps[:, 0:SBT], func=AF.Copy), reads=[pb], writes=[bxp[fc]])
                    cv, bcv_ = cvt[cvc % 2], bcv[cvc % 2]; cvc += 1
                    eng = "dve"
                    k.op(eng, lambda e: e.tensor_scalar(out=cv[:], in0=xpad[:, fc, 3:3 + SBT], scalar1=cbw[:, fc, 3:4], scalar2=cbb[:, fc:fc + 1],
                                                        op0=ALU.mult, op1=ALU.add), reads=[bxp[fc], b_lay], writes=[bcv_])
                    for kk in (2, 1, 0):
                        k.op(eng, lambda e: e.scalar_tensor_tensor(out=cv[:], in0=xpad[:, fc, kk:kk + SBT], scalar=cbw[:, fc, kk:kk + 1], in1=cv[:],
                                                                   op0=ALU.mult, op1=ALU.add), reads=[bxp[fc], b_lay, bcv_], writes=[bcv_])
                    k.op("act", lambda e: e.activation(out=xa[:, fc, :], in_=cv[:], func=AF.Silu), reads=[bcv_], writes=[bxa])
                    k.op("pool", lambda e: e.tensor_copy(out=xpad[:, fc, 0:3], in_=xpad[:, fc, SBT:SBT + 3]), reads=[bxp[fc]], writes=[bxp[fc]])
                for ch in range(NCHK):
                    o = ch * CH
                    csl = slice(ts0 + o, ts0 + o + CH)
                    osl = slice(o, o + CH)
                    dt_, dta, at, al, toe, cd, tm, ssq = [sm[:, i, :] for i in range(8)]
                    ps, pb = next_ps()
                    for c in range(KC):
                        k.op("pe", lambda e: e.matmul(ps[0:CH, 0:16], lhsT=hT[:, c, csl], rhs=wdt[:, c, :], start=(c == 0), stop=(c == KC - 1)),
                             reads=[bwdt, bh], writes=[pb])
                    k.op("dve", lambda e: e.tensor_tensor(out=dt_[0:CH], in0=ps[0:CH, 0:16], in1=dtb_sb[0:CH, :], op=ALU.add), reads=[pb, b_lay], writes=[b_dt])
                    k.op("act", lambda e: e.activation(out=dt_[0:CH], in_=dt_[0:CH], func=AF.Exp), reads=[b_dt], writes=[b_dt])
                    k.op("act", lambda e: e.activation(out=dt_[0:CH], in_=dt_[0:CH], func=AF.Ln, bias=oneb[0:CH, :], scale=1.0), reads=[b_dt, b_const], writes=[b_dt])
                    k.op("dve", lambda e: e.tensor_tensor(out=dta[0:CH], in0=dt_[0:CH], in1=aneg_sb[0:CH, :], op=ALU.mult), reads=[b_dt, b_lay], writes=[b_dta])
                    ps, pb = next_ps()
                    pv = bf(ps)
                    for fc in range(8):
                        k.op("pe", lambda e: e.transpose(out=pv[0:CH, fc * 128:(fc + 1) * 128], in_=xa[:, fc, osl], identity=ident[:]),
                             reads=[bxa, b_const], writes=[pb])
                    pv3 = pv[0:CH, :].rearrange("p (h e) -> p h e", e=64)
                    k.op("dve", lambda e: e.tensor_tensor(out=xsD[0:CH, :].rearrange("p (h e) -> p h e", e=64), in0=pv3,
                                                          in1=dsk_sb[0:CH, :].unsqueeze(2).to_broadcast([CH, 16, 64]), op=ALU.mult),
                         reads=[pb, b_lay], writes=[bxsD])
                    k.op("dve", lambda e: e.tensor_tensor(out=xdt[0:CH, :].rearrange("p (h e) -> p h e", e=64), in0=pv3,
                                                          in1=dt_[0:CH, :].unsqueeze(2).to_broadcast([CH, 16, 64]), op=ALU.mult),
                         reads=[pb, b_dt], writes=[bxdt])
                    ps, pb = next_ps()
                    pv = bf(ps)
                    for gg in range(2):
                        k.op("pe", lambda e: e.transpose(out=pv[0:CH, gg * 128:(gg + 1) * 128], in_=xa[:, 8 + gg, osl], identity=ident[:]),
                             reads=[bxa, b_const], writes=[pb])
                    k.op("act", lambda e: e.activation(out=Btok[0:CH, :, :], in_=pv[0:CH, 0:256].rearrange("p (a n) -> p a n", a=2), func=AF.Copy),
                         reads=[pb], writes=[bBt])
                    k.op("pool", lambda e: e.tensor_tensor(out=X[0:CH], in0=tri[0:CH, 0:CH].unsqueeze(1).to_broadcast([CH, 16, CH]),
                                                           in1=dta[0:CH, :].unsqueeze(2).to_broadcast([CH, 16, CH]), op=ALU.mult),
                         reads=[b_dta, b_const], writes=[bX])
                    ps, pb = next_ps()
                    k.op("pe", lambda e: e.matmul(ps[0:CH, 0:16], lhsT=tri[0:CH, 0:CH], rhs=dta[0:CH, :], start=True, stop=True),
                         reads=[b_dta, b_const], writes=[pb])
                    k.op("dve", lambda e: e.tensor_copy(out=at[0:CH], in_=ps[0:CH, 0:16]), reads=[pb], writes=[b_at])
                    for q4 in range(4):
                        hs = slice(4 * q4, 4 * q4 + 4)
                        ps, pb = next_ps()
                        pv4 = ps[:, 0:4 * CH].rearrange("p (h l) -> p h l", l=CH)
                        k.op("pe", lambda e: e.matmul(pv4, lhsT=ones_f[0:CH, :], rhs=X[0:CH, hs, :], start=True, stop=True),
                             reads=[bX, b_const], writes=[pb])
                        k.op("act", lambda e: e.activation(out=ea[:, hs, :], in_=pv4, func=AF.Exp), reads=[pb], writes=[bea])
                        k.op("act", lambda e: e.activation(out=al[:, hs], in_=pv4[:, :, CH - 1], func=AF.Copy), reads=[pb], writes=[b_al])
                        k.op("pe", lambda e: e.matmul(pv4, lhsT=identf[0:CH, :], rhs=negtri[0:CH, 0:CH].unsqueeze(1).to_broadcast([CH, 4, CH]),
                                                      start=False, stop=True, skip_group_check=True), reads=[b_const], writes=[pb])
                        k.op("dve", lambda e: e.tensor_tensor(out=X[0:CH, hs, :], in0=pv4[0:CH], in1=at[0:CH, hs].unsqueeze(2).to_broadcast([CH, 4, CH]),
                                                              op=ALU.subtract), reads=[pb, b_at, bX], writes=[bX])
                    k.op("act", lambda e: e.activation(out=dec[0:CH], in_=X[0:CH], func=AF.Exp), reads=[bX], writes=[bdec])
                    k.op("dve", lambda e: e.tensor_tensor(out=tm[0:CH], in0=al[0:CH], in1=at[0:CH], op=ALU.subtract), reads=[b_al, b_at], writes=[b_tm])
                    k.op("act", lambda e: e.activation(out=toe[0:CH], in_=tm[0:CH], func=AF.Exp), reads=[b_tm], writes=[b_toe])
                    k.op("act", lambda e: e.activation(out=cd, in_=al, func=AF.Exp), reads=[b_al], writes=[b_cd])
                    k.op("dve", lambda e: e.tensor_tensor(out=xdtE[0:CH, :].rearrange("p (h e) -> p h e", e=64),
                                                          in0=xdt[0:CH, :].rearrange("p (h e) -> p h e", e=64),
                                                          in1=toe[0:CH, :].unsqueeze(2).to_broadcast([CH, 16, 64]), op=ALU.mult),
                         reads=[bxdt, b_toe], writes=[bxdtE])
                    ps, pb = next_ps()
                    for gg in range(2):
                        k.op("pe", lambda e: e.matmul(ps[0:CH, gg * CH:(gg + 1) * CH], lhsT=xa[:, 8 + gg, osl], rhs=xa[:, 10 + gg, osl], start=True, stop=True),
                             reads=[bxa], writes=[pb])
                    k.op("act", lambda e: e.activation(out=cbs[0:CH], in_=ps[0:CH, 0:2 * CH].rearrange("p (a l) -> p a l", a=2), func=AF.Copy),
                         reads=[pb], writes=[bcbs])
                    for gg in range(2):
                        hs = slice(8 * gg, 8 * gg + 8)
                        k.op("dve", lambda e: e.tensor_tensor(out=MT[0:CH, hs, :], in0=dec[0:CH, hs, :],
                                                              in1=cbs[0:CH, gg:gg + 1, :].to_broadcast([CH, 8, CH]), op=ALU.mult),
                             reads=[bdec, bcbs], writes=[bMT])
                        k.op("pool", lambda e: e.tensor_tensor(out=Cs[:, hs, :], in0=ea[:, hs, :],
                                                               in1=xa[:, 10 + gg:11 + gg, osl].to_broadcast([128, 8, CH]), op=ALU.mult),
                             reads=[bea, bxa], writes=[bCs])
                    psy = [next_ps(), next_ps()]
                    for h in range(16):
                        ps, pb = psy[h // 8]
                        col = (h % 8) * 64
                        k.op("pe", lambda e: e.matmul(ps[0:CH, col:col + 64], lhsT=MT[0:CH, h, :], rhs=xdt[0:CH, h * 64:(h + 1) * 64], start=True, stop=False),
                             reads=[bMT, bxdt], writes=[pb])
                        k.op("pe", lambda e: e.matmul(ps[0:CH, col:col + 64], lhsT=Cs[:, h, :], rhs=Sb[:, h * 64:(h + 1) * 64], start=False, stop=True),
                             reads=[bCs, bSb], writes=[pb])
                    pss = [next_ps(), next_ps()]
                    for gg in range(2):
                        ps, pb = pss[gg]
                        k.op("pe", lambda e: e.matmul(ps[:, :], lhsT=Btok[0:CH, gg, :], rhs=xdtE[0:CH, gg * 512:(gg + 1) * 512], start=True, stop=True),
                             reads=[bBt, bxdtE], writes=[pb])
                    k.op("dve", lambda e: e.tensor_tensor(out=S[:, :].rearrange("p (h e) -> p h e", e=64), in0=S[:, :].rearrange("p (h e) -> p h e", e=64),
                                                          in1=cd.unsqueeze(2).to_broadcast([128, 16, 64]), op=ALU.mult), reads=[b_cd, bS], writes=[bS])
                    for gg in range(2):
                        ps, pb = pss[gg]
                        k.op("dve", lambda e: e.tensor_tensor(out=S[:, gg * 512:(gg + 1) * 512], in0=S[:, gg * 512:(gg + 1) * 512], in1=ps[:, :], op=ALU.add),
                             reads=[pb, bS], writes=[bS])
                    k.op("act", lambda e: e.activation(out=Sb[:], in_=S[:], func=AF.Copy), reads=[bS], writes=[bSb])
                    psz, pbz = next_ps()
                    pvz = bf(psz)
                    for fc in range(8):
                        k.op("pe", lambda e: e.transpose(out=pvz[0:CH, fc * 128:(fc + 1) * 128], in_=szT[:, fc, osl], identity=ident[:]),
                             reads=[bsz, b_const], writes=[pbz])
                    for gg in range(2):
                        ps, pb = psy[gg]
                        k.op("dve", lambda e: e.tensor_tensor(out=y1[0:CH, gg * 512:(gg + 1) * 512], in0=ps[0:CH, :], in1=xsD[0:CH, gg * 512:(gg + 1) * 512], op=ALU.add),
                             reads=[pb, bxsD], writes=[by1])
                    k.op("dve", lambda e: e.tensor_tensor(out=y1[0:CH, :], in0=y1[0:CH, :], in1=pvz[0:CH, :], op=ALU.mult), reads=[by1, pbz], writes=[by1])
                    for gg in range(2):
                        k.op("act", lambda e: e.activation(out=junk[0:CH, :], in_=y1[0:CH, gg * 512:(gg + 1) * 512], func=AF.Square, accum_out=ssq[0:CH, gg:gg + 1]),
                             reads=[by1], writes=[bjk, b_ssq])
                    k.op("act", lambda e: e.activation(out=ssq[0:CH, 0:2], in_=ssq[0:CH, 0:2], func=AF.Ln, bias=epsb[0:CH, :], scale=1.0 / 512), reads=[b_ssq, b_const], writes=[b_ssq])
                    k.op("act", lambda e: e.activation(out=ssq[0:CH, 0:2], in_=ssq[0:CH, 0:2], func=AF.Exp, scale=-0.5), reads=[b_ssq], writes=[b_ssq])
                    for gg in range(2):
                        k.op("dve", lambda e: e.scalar_tensor_tensor(out=yn[0:CH, gg * 512:(gg + 1) * 512], in0=y1[0:CH, gg * 512:(gg + 1) * 512],
                                                                     scalar=ssq[0:CH, gg:gg + 1], in1=gss[0:CH, gg * 512:(gg + 1) * 512], op0=ALU.mult, op1=ALU.mult),
                             reads=[by1, b_ssq, bgss], writes=[byn2])
                    ps, pb = next_ps()
                    pv = bf(ps)
                    for fc in range(8):
                        k.op("pe", lambda e: e.transpose(out=pv[:, fc * 128:fc * 128 + CH], in_=yn[0:CH, fc * 128:(fc + 1) * 128], identity=ident[0:CH, 0:CH]),
                             reads=[byn2, b_const], writes=[pb])
                    k.op("act", lambda e: e.activation(out=ynT[:, :, osl], in_=pv[:, 0:1024].rearrange("p (c t) -> p c t", t=128)[:, :, 0:CH], func=AF.Copy),
                         reads=[pb], writes=[byn])
                proj_gate_merge(l, 1, lambda c: ynT[:, c, 0:SBT], 8, S_bp[l], hT, byn, mT, bm, ts0, SBT, False, pg, bh=bh)
            with nc.allow_non_contiguous_dma(reason="tiny conv state"):
                k.dma("sp", o_cb[l, s], xpad[:, :, 0:3], reads=bxp)
            for a in range(8):
                ps, pb = next_ps()
                k.op("pe", lambda e: e.transpose(out=ps[:, 0:128], in_=S[:, a * 128:(a + 1) * 128], identity=identf[:]), reads=[bS, b_const], writes=[pb])
                k.op("act", lambda e: e.activation(out=stin[:, a, :], in_=ps[:, 0:128], func=AF.Copy), reads=[pb], writes=[bstin])
            k.dma("sp", o_ssm[l, s].rearrange("(a p) n -> p a n", p=128), stin[:], reads=[bstin])
            k.barrier()

    def lru(l, g, s, tok0, L, hT, bh, mT, bm):
        prompt = g < NPS
        SBT = min(L, 512)
        wi_l = w_in[l].rearrange("(c p) n -> p c n", p=128)
        with ExitStack() as es:
            xpad = TMP(es, "r_xpad", [128, 8, SBT + 3], F32); bxp = [Buf() for _ in range(8)]
            xcv = TMP(es, "r_xcv", [128, 8, SBT], F32); bxc = [Buf() for _ in range(8)]
            hgT = TMP(es, "r_hgT", [128, 8, SBT], BF16); bhg = Buf()
            hst = TMP(es, "r_hst", [128, 8], F32); bhst = Buf()
            wr = TMP(es, "r_wr", [128, 8, 128], F32); wig = TMP(es, "r_wi", [128, 8, 128], F32); bwr = Buf()
            wx = [TMP(es, f"r_wx{i}", [128, KC, 128], BF16) for i in range(4)]; bwx = [Buf() for _ in range(4)]
            tmps = [[TMP(es, f"r_t{j}_{i}", [128, SBT], F32) for j in range(6)] for i in range(2)]
            btm = [[Buf() for j in range(6)] for i in range(2)]
            pg = pg_alloc(es, 8, 3)
            k.dma("sp", wr[:], w_rg[l].rearrange("h i j -> i h j"), writes=[bwr])
            k.dma("sp", wig[:], w_ig[l].rearrange("h i j -> i h j"), writes=[bwr])
            if prompt:
                k.op("pool", lambda e: e.memset(xpad[:, :, 0:3], 0.0), writes=bxp)
                k.op("pool", lambda e: e.memset(hst[:], 0.0), writes=[bhst])
            else:
                si = s - NPS
                k.dma("sp", xpad[:, :, 0:3], st_cc[l, si], writes=bxp)
                k.dma("sp", hst[:], st_lru[l, si], writes=[bhst])
            wc = 0
            for st in range(L // SBT):
                ts0 = tok0 + st * SBT
                tsub = slice(ts0, ts0 + SBT)
                for fc in range(8):
                    rg, ai, ig, a2, u, hc = tmps[fc % 2]
                    brg, bai, big, ba2, bu, bhc = btm[fc % 2]
                    w, bw_ = wx[wc % 4], bwx[wc % 4]; wc += 1
                    wload(w[:], inB(l, O_XC + fc * 128), bw_)
                    ps, pb = next_ps()
                    for c in range(KC):
                        k.op("pe", lambda e: e.matmul(ps[:, 0:SBT], lhsT=w[:, c, :], rhs=hT[:, c, tsub], start=(c == 0), stop=(c == KC - 1)),
                             reads=[bw_, bh], writes=[pb])
                    k.op("act", lambda e: e.activation(out=xpad[:, fc, 3:3 + SBT], in_=ps[:, 0:SBT], func=AF.Copy), reads=[pb], writes=[bxp[fc]])
                    cv = xcv[:, fc, :]
                    eng = "dve"
                    k.op(eng, lambda e: e.tensor_scalar(out=cv, in0=xpad[:, fc, 3:3 + SBT], scalar1=ccw[:, fc, 3:4], scalar2=ccb[:, fc:fc + 1],
                                                        op0=ALU.mult, op1=ALU.add), reads=[bxp[fc], b_lay], writes=[bxc[fc]])
                    for kk in (2, 1, 0):
                        k.op(eng, lambda e: e.scalar_tensor_tensor(out=cv, in0=xpad[:, fc, kk:kk + SBT], scalar=ccw[:, fc, kk:kk + 1], in1=cv,
                                                                   op0=ALU.mult, op1=ALU.add), reads=[bxp[fc], b_lay, bxc[fc]], writes=[bxc[fc]])
                    k.op("pool", lambda e: e.tensor_copy(out=xpad[:, fc, 0:3], in_=xpad[:, fc, SBT:SBT + 3]), reads=[bxp[fc]], writes=[bxp[fc]])
                    psr, pbr = next_ps()
                    k.op("pe", lambda e: e.matmul(psr[:, 0:SBT], lhsT=wr[:, fc, :], rhs=cv, start=True, stop=True), reads=[bwr, bxc[fc]], writes=[pbr])
                    psi, pbi = next_ps()
                    k.op("pe", lambda e: e.matmul(psi[:, 0:SBT], lhsT=wig[:, fc, :], rhs=cv, start=True, stop=True), reads=[bwr, bxc[fc]], writes=[pbi])
                    k.op("act", lambda e: e.activation(out=rg[:], in_=psr[:, 0:SBT], func=AF.Sigmoid, bias=br_sb[:, fc:fc + 1], scale=1.0), reads=[pbr, b_lay], writes=[brg])
                    k.op("act", lambda e: e.activation(out=ig[:], in_=psi[:, 0:SBT], func=AF.Sigmoid, bias=bi_sb[:, fc:fc + 1], scale=1.0), reads=[pbi, b_lay], writes=[big])
                    k.op("act", lambda e: e.activation(out=ai[:], in_=rg[:], func=AF.Exp, scale=cneg_sb[:, fc:fc + 1]), reads=[brg, b_lay], writes=[bai])
                    k.op("dve", lambda e: e.tensor_tensor(out=a2[:], in0=ai[:], in1=ai[:], op=ALU.mult), reads=[bai], writes=[ba2])
                    k.op("act", lambda e: e.activation(out=a2[:], in_=a2[:], func=AF.Ln, bias=oneb[:], scale=-1.0), reads=[ba2, b_const], writes=[ba2])
                    k.op("act", lambda e: e.activation(out=a2[:], in_=a2[:], func=AF.Exp, scale=0.5), reads=[ba2], writes=[ba2])
                    k.op("pool", lambda e: e.tensor_tensor(out=u[:], in0=cv, in1=ig[:], op=ALU.mult), reads=[bxc[fc], big], writes=[bu])
                    k.op("dve", lambda e: e.tensor_tensor(out=u[:], in0=u[:], in1=a2[:], op=ALU.mult), reads=[bu, ba2], writes=[bu])
                    k.op("dve", lambda e: e.tensor_tensor_scan(out=hc[:], data0=ai[:], data1=u[:], initial=hst[:, fc:fc + 1], op0=ALU.mult, op1=ALU.add),
                         reads=[bai, bu, bhst], writes=[bhc])
                    k.op("dve", lambda e: e.tensor_copy(out=hst[:, fc:fc + 1], in_=hc[:, SBT - 1:SBT]), reads=[bhc], writes=[bhst])
                    w, bw_ = wx[wc % 4], bwx[wc % 4]; wc += 1
                    wload(w[:], inB(l, O_GC + fc * 128), bw_)
                    ps, pb = next_ps()
                    for c in range(KC):
                        k.op("pe", lambda e: e.matmul(ps[:, 0:SBT], lhsT=w[:, c, :], rhs=hT[:, c, tsub], start=(c == 0), stop=(c == KC - 1)),
                             reads=[bw_, bh], writes=[pb])
                    k.op("act", lambda e: e.activation(out=rg[:], in_=ps[:, 0:SBT], func=AF.Gelu_apprx_tanh), reads=[pb, brg], writes=[brg])
                    k.op("pool", lambda e: e.tensor_tensor(out=hgT[:, fc, :], in0=hc[:], in1=rg[:], op=ALU.mult), reads=[bhc, brg], writes=[bhg])
                proj_gate_merge(l, 2, lambda c: hgT[:, c, 0:SBT], 8, S_cp[l], hT, bhg, mT, bm, ts0, SBT, False, pg, bh=bh)
            with nc.allow_non_contiguous_dma(reason="tiny conv state"):
                k.dma("sp", o_cc[l, s], xpad[:, :, 0:3], reads=bxp)
            k.dma("sp", o_lru[l, s], hst[:], reads=[bhst])
            k.barrier()


    def attn_sample(l, hT, bh, mT, bm):
        wi_l = w_in[l].rearrange("(c p) n -> p c n", p=128)
        with ExitStack() as es:
            qkv = TMP(es, "q_qkv", [32, 4608], F32); bqkv = Buf()
            wq = [TMP(es, f"q_w{i}", [128, KC, 512], BF16) for i in range(2)]; bwq = [[Buf() for _ in range(4)] for _ in range(2)]
            KVc = [TMP(es, f"q_kvc{i}", [128, 1024], F32) for i in range(2)]; bkc = [Buf(), Buf()]
            KVn = [TMP(es, f"q_kvn{i}", [8, 1024], F32) for i in range(2)]; bkn = [Buf(), Buf()]
            prod = [TMP(es, f"q_pr{i}", [128, 512], F32) for i in range(2)]; bpr = [Buf(), Buf()]
            prn = TMP(es, "q_prn", [8, 512], F32); bprn = Buf()
            Wv = [TMP(es, f"q_wv{i}", [128, 512], F32) for i in range(2)]; bwv = [Buf(), Buf()]
            Wn = TMP(es, "q_wn", [8, 512], F32); bwn = Buf()
            sc = [TMP(es, f"q_sc{i}", [128, 8], F32) for i in range(2)]; bsc = [Buf(), Buf()]
            scn = TMP(es, "q_scn", [8, 8], F32); bscn = Buf()
            selq = [TMP(es, f"q_selq{i}", [32, 128], F32) for i in range(2)]; bsq = [Buf(), Buf()]
            selk_sb = TMP(es, "q_selk", [128, 32, 32], F32); bsk = Buf()
            oa_sb = TMP(es, "q_oa", [32, 512], BF16); boa2 = Buf()
            rz = TMP(es, "q_rz", [32, 8], F32); brz = Buf()
            oaT = TMP(es, "q_oaT", [128, 4, 32], BF16); boa = Buf()
            b_skv = Buf()
            k.dma("sp", selk_sb[:], selk, writes=[bsk])
            for cb in range(9):
                w, bw_ = wq[cb % 2], bwq[cb % 2]
                for sb_ in range(4):
                    wload(w[:, :, sb_ * 128:(sb_ + 1) * 128], inA(l, cb * 512 + sb_ * 128), bw_[sb_])
                ps, pb = next_ps()
                for c in range(KC):
                    k.op("pe", lambda e: e.matmul(ps[0:32, :], lhsT=hT[:, c, 0:32], rhs=w[:, c, :], start=(c == 0), stop=(c == KC - 1)),
                         reads=bw_ + [bh], writes=[pb])
                k.op("act", lambda e: e.activation(out=qkv[:, cb * 512:(cb + 1) * 512], in_=ps[0:32, :], func=AF.Copy), reads=[pb], writes=[bqkv])
            for gq in range(3):
                dst = skv[gq][l].rearrange("s t x -> (s t) x")
                k.dma("sp", dst[:, 0:512], qkv[:, O_K + gq * 512:O_K + (gq + 1) * 512], reads=[bqkv], writes=[b_skv])
                k.dma("sp", dst[:, 512:1024], qkv[:, O_V + gq * 512:O_V + (gq + 1) * 512], reads=[bqkv], writes=[b_skv])
            ps_lim[0] = 6
            ps_i[0] = 0
            psO, pbO = PS[6], PSB[6]
            psZ, pbZ = PS[7], PSB[7]
            first = [True]
            it = 0
            for si in range(NSS):
                for gq, (win, d) in enumerate(GROUPS):
                    for r in range(min(d, DL)):
                        nq = len(range(r, DL, d))
                        kc_, bkc_ = KVc[it % 2], bkc[it % 2]
                        kn_, bkn_ = KVn[it % 2], bkn[it % 2]
                        it += 1
                        k.dma("sp", kc_[:], kvc[gq][l, si].rearrange("(i dd) x -> dd i x", dd=d)[r], writes=[bkc_])
                        if d <= DL:
                            k.dma("sp", kn_[0:nq], skv[gq][l, si].rearrange("(i dd) x -> dd i x", dd=d)[r], reads=[b_skv], writes=[bkn_])
                        else:
                            k.dma("sp", kn_[0:nq], skv[gq][l, si, r:r + 1, :], reads=[b_skv], writes=[bkn_])
                        for qi in range(nq):
                            t = r + qi * d
                            tok = si * DL + t
                            sq, bsq_ = selq[tok % 2], bsq[tok % 2]
                            pr, bpr_ = prod[tok % 2], bpr[tok % 2]
                            wv, bwv_ = Wv[tok % 2], bwv[tok % 2]
                            sc_, bsc_ = sc[tok % 2], bsc[tok % 2]
                            k.op("dve", lambda e: e.tensor_copy(out=sq[:], in_=identf[0:32, tok:tok + 1].to_broadcast([32, 128])),
                                 reads=[b_const], writes=[bsq_])
                            ps, pb = next_ps()
                            k.op("pe", lambda e: e.matmul(ps[:, :], lhsT=sq[:], rhs=qkv[:, O_Q + gq * 512:O_Q + (gq + 1) * 512], start=True, stop=True),
                                 reads=[bsq_, bqkv], writes=[pb])
                            k.op("dve", lambda e: e.tensor_tensor(out=pr[:], in0=kc_[:, 0:512], in1=ps[:, :], op=ALU.mult), reads=[bkc_, pb], writes=[bpr_])
                            k.op("dve", lambda e: e.tensor_reduce(out=sc_[:], in_=pr[:].rearrange("p (h e) -> p h e", e=64), axis=AX.X, op=ALU.add),
                                 reads=[bpr_], writes=[bsc_])
                            k.op("dve", lambda e: e.tensor_tensor(out=prn[0:nq], in0=kn_[0:nq, 0:512], in1=ps[0:nq, :], op=ALU.mult), reads=[bkn_, pb], writes=[bprn])
                            k.op("dve", lambda e: e.tensor_reduce(out=scn[0:nq], in_=prn[0:nq].rearrange("p (h e) -> p h e", e=64), axis=AX.X, op=ALU.add),
                                 reads=[bprn], writes=[bscn])
                            k.op("act", lambda e: e.activation(out=sc_[:], in_=sc_[:], func=AF.Exp, scale=0.125), reads=[bsc_], writes=[bsc_])
                            k.op("act", lambda e: e.activation(out=scn[0:nq], in_=scn[0:nq], func=AF.Exp, scale=0.125), reads=[bscn], writes=[bscn])
                            k.op("dve", lambda e: e.tensor_tensor(out=sc_[:], in0=sc_[:], in1=E[:, gq * 8:(gq + 1) * 8, 128 + qi], op=ALU.mult),
                                 reads=[bsc_, b_E], writes=[bsc_])
                            k.op("dve", lambda e: e.tensor_tensor(out=scn[0:nq], in0=scn[0:nq], in1=E[0:nq, gq * 8:(gq + 1) * 8, qi], op=ALU.mult),
                                 reads=[bscn, b_E], writes=[bscn])
                            k.op("dve", lambda e: e.tensor_tensor(out=wv[:].rearrange("p (h e) -> p h e", e=64), in0=kc_[:, 512:1024].rearrange("p (h e) -> p h e", e=64),
                                                                  in1=sc_[:].unsqueeze(2).to_broadcast([128, 8, 64]), op=ALU.mult), reads=[bkc_, bsc_], writes=[bwv_])
                            k.op("dve", lambda e: e.tensor_tensor(out=Wn[0:nq].rearrange("p (h e) -> p h e", e=64), in0=kn_[0:nq, 512:1024].rearrange("p (h e) -> p h e", e=64),
                                                                  in1=scn[0:nq].unsqueeze(2).to_broadcast([nq, 8, 64]), op=ALU.mult), reads=[bkn_, bscn], writes=[bwn])
                            f0 = first[0]
                            first[0] = False
                            k.op("pe", lambda e: e.matmul(psO[0:32, :], lhsT=selk_sb[:, tok, :], rhs=wv[:], start=f0, stop=False, skip_group_check=True),
                                 reads=[bsk, bwv_], writes=[pbO])
                            k.op("pe", lambda e: e.matmul(psO[0:32, :], lhsT=selk_sb[0:nq, tok, :], rhs=Wn[0:nq], start=False, stop=False, skip_group_check=True),
                                 reads=[bsk, bwn], writes=[pbO])
                            k.op("pe", lambda e: e.matmul(psZ[0:32, 0:8], lhsT=selk_sb[:, tok, :], rhs=sc_[:], start=f0, stop=False, skip_group_check=True),
                                 reads=[bsk, bsc_], writes=[pbZ])
                            k.op("pe", lambda e: e.matmul(psZ[0:32, 0:8], lhsT=selk_sb[0:nq, tok, :], rhs=scn[0:nq], start=False, stop=False, skip_group_check=True),
                                 reads=[bsk, bscn], writes=[pbZ])
            ps_lim[0] = 8
            k.op("dve", lambda e: e.reciprocal(out=rz[:], in_=psZ[0:32, 0:8]), reads=[pbZ], writes=[brz])
            k.op("dve", lambda e: e.tensor_tensor(out=oa_sb[:].rearrange("p (h e) -> p h e", e=64), in0=psO[0:32, :].rearrange("p (h e) -> p h e", e=64),
                                                  in1=rz[:].unsqueeze(2).to_broadcast([32, 8, 64]), op=ALU.mult), reads=[pbO, brz], writes=[boa2])
            ps, pb = next_ps()
            pv = bf(ps)
            for c in range(4):
                k.op("pe", lambda e: e.transpose(out=pv[:, c * 128:c * 128 + 32], in_=oa_sb[0:32, c * 128:(c + 1) * 128], identity=ident[0:32, 0:32]),
                     reads=[boa2, b_const], writes=[pb])
            k.op("act", lambda e: e.activation(out=oaT[:], in_=pv[:, 0:512].rearrange("p (c t) -> p c t", t=128)[:, :, 0:32], func=AF.Copy),
                 reads=[pb], writes=[boa])
            with ExitStack() as es2:
                pg = pg_alloc(es2, 4, 3)
                proj_gate_merge(l, 0, lambda c: oaT[:, c, :], 4, S_ap[l], hT, boa, mT, bm, 0, 32, True, pg, bh=bh)
                k.barrier()

    def out_proj(l, g, mT, bm):
        gi = grp_info(g)
        P, T = gi["P"], gi["T"]
        TT = min(T, 1024)
        NB = TT // P
        wo_l = w_o[l].rearrange("(c p) n -> p c n", p=128)
        with ExitStack() as es:
            xt = TMP(es, "o_x", [128, NB, D], F32); bx = Buf()
            wob = [TMP(es, f"o_w{i}", [128, KC, 256], BF16) for i in range(4)]; bwo = [Buf() for _ in range(4)]
            rt = [TMP(es, f"o_rt{i}", [128, 256], F32) for i in range(2)]; brt = [Buf(), Buf()]
            for q in range(4):
                wload(wob[q][:], S_o[l][q], bwo[q])
            for ti in range(T // TT):
                r0 = gi["row0"] + ti * TT
                xb = xres_b[g][ti]
                k.dma("pool", xt[0:P], xres[r0:r0 + TT, :].rearrange("(b p) d -> p b d", p=P), reads=[xb], writes=[bx])
                cnt = 0
                for q in range(4):
                    w, bw_ = wob[q], bwo[q]
                    for b in range(NB):
                        ps, pb = next_ps()
                        for c in range(KC):
                            k.op("pe", lambda e: e.matmul(ps[0:P, 0:256], lhsT=mT[:, c, ti * TT + b * P:ti * TT + (b + 1) * P], rhs=w[:, c, :],
                                                          start=(c == 0), stop=(c == KC - 1)), reads=[bm, bw_], writes=[pb])
                        resid_update(xt[0:P, b, q * 256:(q + 1) * 256], bx, ps[0:P, 0:256], pb, P, 1, q * 256, 256, 1.0,
                                     rt[cnt % 2], brt[cnt % 2])
                        cnt += 1
                k.dma("pool", xres[r0:r0 + TT, :].rearrange("(b p) d -> p b d", p=P), xt[0:P], reads=[bx], writes=[xb])
            k.barrier()

    def mixer(l, g):
        gi = grp_info(g)
        P, T = gi["P"], gi["T"]
        prompt = g < NPS
        with ExitStack() as es:
            hT = TMP(es, "m_hT", [128, KC, T], BF16)
            mT = TMP(es, "m_mT", [128, KC, T], BF16)
            bh, bm = Buf(), Buf()
            TT = min(T, 1024)
            NB = TT // P
            with ExitStack() as es2:
                HB = max(NB // 2, 1)
                xth = [TMP(es2, f"m_x{i}", [128, HB, D], F32) for i in range(2)]; bxh = [Buf(), Buf()]
                nctx = norm_alloc(es2)
                hi = 0
                for ti in range(T // TT):
                    for b0 in range(0, NB, HB):
                        xt, bx = xth[hi % 2], bxh[hi % 2]; hi += 1
                        r0 = gi["row0"] + ti * TT + b0 * P
                        k.dma("pool" if hi % 2 else "sp", xt[0:P], xres[r0:r0 + HB * P, :].rearrange("(b p) d -> p b d", p=P),
                              reads=[xres_b[g][ti]], writes=[bx])
                        norm_to_hT(g, 1, xt, bx, P, HB, hT, bh, ti * TT + b0 * P, None, nctx)
                k.barrier()
            if prompt and ATTN_P:
                attn_prompt(l, g, hT, bh, mT, bm)
            elif (not prompt) and ATTN_S:
                attn_sample(l, hT, bh, mT, bm)
            else:
                k.op("dve", lambda e: e.memset(mT[:], 0.0), writes=[bm])
            if MIX_PARTS >= 2:
                for si, s in enumerate(gi["seqs"]):
                    ssd(l, g, s, si * gi["L"], gi["L"], hT, bh, mT, bm)
            if MIX_PARTS >= 3:
                for si, s in enumerate(gi["seqs"]):
                    lru(l, g, s, si * gi["L"], gi["L"], hT, bh, mT, bm)
            out_proj(l, g, mT, bm)


    for l in range(NLAY):
        precast(l)
    for l in range(NLAY):
        ada(l)
        for g in GRPS:
            setup_group(g, l)
            src = xp[g] if g < NPS else xs
            ff(l, 0, g, src, first=(l == 0))
            mixer(l, g)
            ff(l, 1, g, None, first=False)
    for g in range(3):
        final_norm(g, yp[g] if g < NPS else ys)
    k.barrier()
    return nc


_T = lambda a: np.ascontiguousarray(a)


def _featT(v, nchunk):
    sh = v.shape[:-1]
    return _T(np.moveaxis(v.reshape(sh + (nchunk, 128)), -1, -2))


def make_in_maps(inp):
    f = lambda a: np.asarray(a, dtype=np.float32)
    rel_bias = f(inp["rel_bias"])
    kj = np.arange(128)[:, None]
    qi = np.arange(128)[None, :]
    dist_cur = qi - kj
    dist_prev = 128 + qi - kj
    ebias = np.zeros((128, 24, 256), np.float32)
    emask = np.zeros((128, 256), np.float32)
    emask[:, 0:128] = (dist_cur >= 0)
    emask[:, 128:256] = (dist_prev <= 128)
    for g, (win, dil) in enumerate(GROUPS):
        bc = t5_bucket(np.clip(dist_cur, 0, 128) * dil)
        bp = t5_bucket(np.clip(dist_prev, 0, 128) * dil)
        for h in range(8):
            ebias[:, g * 8 + h, 0:128] = rel_bias[bc, g * 8 + h]
            ebias[:, g * 8 + h, 128:256] = rel_bias[bp, g * 8 + h]
    selg = np.zeros((3, NSEQ, 128), np.float32)
    selg[0, 0, :] = 1.0
    selg[1, 1, :] = 1.0
    for m in range(ST):
        selg[2, NPS + m // DL, m] = 1.0
    selk = np.zeros((128, 32, 32), np.float32)
    for t in range(32):
        selk[:, t, t] = 1.0
    shared = dict(
        ebias=ebias, emask=emask, selg=selg, selk=selk,
        w_ada=f(inp["w_ada"]), b_adaT=_featT(f(inp["b_ada"]), 72),
        g_ff1T=_featT(f(inp["g_ff1"]), 8), g_mixT=_featT(f(inp["g_mix"]), 8), g_ff2T=_featT(f(inp["g_ff2"]), 8),
        gfin=_T(np.broadcast_to(f(inp["g_final"])[None, :], (128, D))),
        w_ff1_in=f(inp["w_ff1_in"]), w_ff2_in=f(inp["w_ff2_in"]), w_ff1_out=f(inp["w_ff1_out"]), w_ff2_out=f(inp["w_ff2_out"]),
        w_in=f(inp["w_in"]), w_a_proj=f(inp["w_a_proj"]), w_b_proj=f(inp["w_b_proj"]), w_c_proj=f(inp["w_c_proj"]),
        w_out=f(inp["w_out"]),
        cbwT=_T(np.transpose(f(inp["conv_b_w"]).reshape(DEPTH, 4, 12, 128), (0, 3, 2, 1))),
        cbbT=_featT(f(inp["conv_b_b"]), 12),
        ccwT=_T(np.transpose(f(inp["conv_c_w"]).reshape(DEPTH, 4, 8, 128), (0, 3, 2, 1))),
        ccbT=_featT(f(inp["conv_c_b"]), 8),
        dtb_bc=_T(np.broadcast_to(f(inp["dt_bias"])[:, None, :], (DEPTH, 128, 16))),
        alog_bc=_T(np.broadcast_to(f(inp["a_log"])[:, None, :], (DEPTH, 128, 16))),
        dsk_bc=_T(np.broadcast_to(f(inp["d_skip"])[:, None, :], (DEPTH, 128, 16))),
        gssm_bc=_T(np.broadcast_to(f(inp["g_ssm_norm"])[:, None, :], (DEPTH, 128, D))),
        w_rgate=f(inp["w_rgate"]), w_igate=f(inp["w_igate"]),
        brT=_featT(f(inp["b_rgate"]), 8), biT=_featT(f(inp["b_igate"]), 8), lamT=_featT(f(inp["lru_lambda"]), 8),
    )
    b_ada = f(inp["b_ada"])
    gate_cols = np.concatenate([b_ada[:, 2 * D:3 * D], b_ada[:, 5 * D:6 * D], b_ada[:, 8 * D:9 * D]], axis=1)
    shared["b_adaG"] = _T(np.broadcast_to(gate_cols[:, None, :], (DEPTH, NSEQ, 3 * D)))
    maps = []
    for c in range(NCORES):
        ps = slice(c * NPS, (c + 1) * NPS)
        ss = slice(c * NSS, (c + 1) * NSS)
        cc = np.concatenate([f(inp["c_prompt"])[ps], f(inp["c_sample"])[ss]], axis=0)
        m = dict(shared)
        m["xp"] = _T(f(inp["x_prompt"])[ps])
        m["xs"] = _T(f(inp["x_sample"])[ss].reshape(ST, D))
        m["cT"] = _T(np.transpose(cc.reshape(NSEQ, KC, 128), (2, 1, 0)))
        m["kvc1"] = _T(f(inp["cache_win1_kv"])[:, ss].reshape(DEPTH, NSS, 128, 1024))
        m["kvc2"] = _T(f(inp["cache_win2_kv"])[:, ss].reshape(DEPTH, NSS, 512, 1024))
        m["kvc3"] = _T(f(inp["cache_win3_kv"])[:, ss].reshape(DEPTH, NSS, 2048, 1024))
        m["st_cb"] = _T(np.transpose(f(inp["state_conv_b"])[:, ss].reshape(DEPTH, NSS, 3, 12, 128), (0, 1, 4, 3, 2)))
        m["st_ssm"] = _T(f(inp["state_ssm"])[:, ss].reshape(DEPTH, NSS, 1024, 128))
        m["st_cc"] = _T(np.transpose(f(inp["state_conv_c"])[:, ss].reshape(DEPTH, NSS, 3, 8, 128), (0, 1, 4, 3, 2)))
        m["st_lru"] = _T(np.transpose(f(inp["state_lru"])[:, ss].reshape(DEPTH, NSS, 8, 128), (0, 1, 3, 2)))
        maps.append(m)
    return maps


def assemble(results):
    cat = lambda key, ax: np.concatenate([r[key] for r in results], axis=ax)
    yp = cat("yp", 0)
    ys = cat("ys", 0).reshape(NCORES * NSS, DL, D)
    out = [yp, ys]
    for i, w in enumerate((128, 512, 2048)):
        out.append(cat(f"pkv{i + 1}", 1).reshape(DEPTH, NCORES * NPS, w, 2, 8, 64))
    o_cb = np.stack([r["o_cb"] for r in results], 0)
    o_ssm = np.stack([r["o_ssm"] for r in results], 0)
    o_cc = np.stack([r["o_cc"] for r in results], 0)
    o_lru = np.stack([r["o_lru"] for r in results], 0)

    def cb_fix(a, seqsl, nch):
        a = a[:, :, seqsl]
        a = np.transpose(a, (1, 0, 2, 5, 4, 3))
        return _T(a.reshape(DEPTH, -1, 3, nch * 128))

    def ssm_fix(a, seqsl):
        a = np.transpose(a[:, :, seqsl], (1, 0, 2, 3, 4))
        return _T(a.reshape(DEPTH, -1, 16, 64, 128))

    def lru_fix(a, seqsl):
        a = np.transpose(a[:, :, seqsl], (1, 0, 2, 4, 3))
        return _T(a.reshape(DEPTH, -1, 1024))

    P_, S_ = slice(0, NPS), slice(NPS, NSEQ)
    out += [cb_fix(o_cb, P_, 12), ssm_fix(o_ssm, P_), cb_fix(o_cc, P_, 8), lru_fix(o_lru, P_)]
    for i in range(3):
        out.append(cat(f"skv{i + 1}", 1).reshape(DEPTH, NCORES * NSS, DL, 2, 8, 64))
    out += [cb_fix(o_cb, S_, 12), ssm_fix(o_ssm, S_), cb_fix(o_cc, S_, 8), lru_fix(o_lru, S_)]
    return tuple(np.ascontiguousarray(o, dtype=np.float32) for o in out)


def kernel(**inputs):
    nc = build_program()
    maps = make_in_maps(inputs)
    res = run_bass_kernel_spmd(nc, maps, core_ids=list(range(NCORES)))
    return assemble(res.results)
```
